# Optimizing a Trainium2 kernel written in Bass

```python
import math
import jax, jax.numpy as jnp
from jax import lax
import numpy as np

D_MODEL = 1024
BATCH = 8
SEQ = 4096
DEPTH = 1

CHUNK = 64
N_META = 16
DC = D_MODEL
CONF_K = 31
DN_HEADS = 8
DN_DK = 128
DN_DV = 128
DN_CONV_K = 4
D_FF = 2816
FFN_CONV_K = 3
N_BRANCH = 2
EPS = 1e-6
HK = DN_HEADS * DN_DK
HV = DN_HEADS * DN_DV
D_IN = 2 * DC + 2 * HK + 2 * HV + 2 * DN_HEADS + N_BRANCH * D_MODEL

kernel_name = 'hybrid_conformer_gdn_convffn_block'


def rmsnorm(x, w):
    xf = x.astype(jnp.float32)
    y = xf * lax.rsqrt(jnp.mean(xf * xf, axis=-1, keepdims=True) + EPS)
    return (y * w.astype(jnp.float32)).astype(x.dtype)


def layernorm(x, w, b):
    xf = x.astype(jnp.float32)
    mu = jnp.mean(xf, axis=-1, keepdims=True)
    xc = xf - mu
    y = xc * lax.rsqrt(jnp.mean(xc * xc, axis=-1, keepdims=True) + EPS)
    return (y * w.astype(jnp.float32) + b.astype(jnp.float32)).astype(x.dtype)


def l2norm(x):
    xf = x.astype(jnp.float32)
    return xf * lax.rsqrt(jnp.sum(xf * xf, axis=-1, keepdims=True) + EPS)


def causal_dwconv(x, w, b=None):
    k = w.shape[0]
    y = lax.conv_general_dilated(
        x, w[:, None, :].astype(x.dtype), window_strides=(1,), padding=[(k - 1, 0)],
        dimension_numbers=('NWC', 'WIO', 'NWC'), feature_group_count=x.shape[-1])
    return y if b is None else y + b.astype(x.dtype)


def chunk_gated_delta_rule(q, k, v, g, beta):
    b, lp, h, dk = q.shape
    dv = v.shape[-1]
    n = lp // CHUNK

    def to_chunks(t):
        t = t.astype(jnp.float32).reshape((b, n, CHUNK, h) + t.shape[3:])
        return jnp.moveaxis(t, (1, 3), (0, 2))

    qc = to_chunks(q) * (dk ** -0.5)
    kc = to_chunks(k)
    vc = to_chunks(v)
    bc = to_chunks(beta)
    gc = jnp.cumsum(to_chunks(g), axis=-1)
    idx = jnp.arange(CHUNK)
    incl = idx[:, None] >= idx[None, :]
    strict = idx[:, None] > idx[None, :]
    decay = jnp.exp(jnp.where(incl, gc[..., :, None] - gc[..., None, :], -jnp.inf))
    kb = kc * bc[..., None]
    a_kk = jnp.where(strict, jnp.einsum('nbhid,nbhjd->nbhij', kb, kc) * decay, 0.0)
    rhs = jnp.concatenate([vc * bc[..., None], kb * jnp.exp(gc)[..., None]], axis=-1)
    sol = lax.linalg.triangular_solve(a_kk, rhs, left_side=True, lower=True, unit_diagonal=True)
    u, w = sol[..., :dv], sol[..., dv:]
    a_qk = jnp.where(incl, jnp.einsum('nbhid,nbhjd->nbhij', qc, kc) * decay, 0.0)
    q_dec = qc * jnp.exp(gc)[..., None]
    g_last = gc[..., -1:]
    k_dec = kc * jnp.exp(g_last - gc)[..., None]
    chunk_decay = jnp.exp(g_last)[..., None]

    def step(state, xs):
        u_c, w_c, qd_c, aqk_c, kd_c, cd_c = xs
        v_new = u_c - jnp.einsum('bhck,bhkv->bhcv', w_c, state)
        o_c = (jnp.einsum('bhck,bhkv->bhcv', qd_c, state)
               + jnp.einsum('bhij,bhjv->bhiv', aqk_c, v_new))
        state = state * cd_c + jnp.einsum('bhck,bhcv->bhkv', kd_c, v_new)
        return state, o_c

    s0 = jnp.zeros((b, h, dk, dv), jnp.float32)
    _, o = lax.scan(step, s0, (u, w, q_dec, a_qk, k_dec, chunk_decay))
    return jnp.moveaxis(o, (0, 2), (1, 3)).reshape(b, lp, h, dv)


def hybrid_layer(h, norm_mix_w, w_in, b_gate, conf_dw_w, conf_dw_b, conf_ln_w, conf_ln_b,
                 w_conf_out, dn_conv_w, dn_A_log, dn_dt_bias, dn_norm_w, w_dn_out, w_out,
                 norm_ffn_w, w_up, ffn_dw_w, ffn_dw_b, w_down):
    b, l, _ = h.shape
    u = rmsnorm(h, norm_mix_w)
    proj = u @ w_in
    cuts = [2 * DC, 2 * DC + HK, 2 * DC + 2 * HK, 2 * DC + 2 * HK + HV,
            2 * DC + 2 * HK + 2 * HV, 2 * DC + 2 * HK + 2 * HV + DN_HEADS,
            2 * DC + 2 * HK + 2 * HV + 2 * DN_HEADS]
    c_in, q, k, v, z, a_dt, b_beta, gate_logits = jnp.split(proj, cuts, axis=-1)

    c_val, c_gate = jnp.split(c_in, 2, axis=-1)
    c = c_val * jax.nn.sigmoid(c_gate)
    c = causal_dwconv(c, conf_dw_w, conf_dw_b)
    c = jax.nn.silu(layernorm(c, conf_ln_w, conf_ln_b))
    y_conf = c @ w_conf_out

    qkv = jax.nn.silu(causal_dwconv(jnp.concatenate([q, k, v], axis=-1), dn_conv_w))
    q, k, v = jnp.split(qkv, [HK, 2 * HK], axis=-1)
    q = l2norm(q.reshape(b, l, DN_HEADS, DN_DK))
    k = l2norm(k.reshape(b, l, DN_HEADS, DN_DK))
    v = v.reshape(b, l, DN_HEADS, DN_DV)
    g = -jnp.exp(dn_A_log.astype(jnp.float32)) * jax.nn.softplus(
        a_dt.astype(jnp.float32) + dn_dt_bias.astype(jnp.float32))
    beta = jax.nn.sigmoid(b_beta.astype(jnp.float32))
    pad = CHUNK - N_META

    def lpad(t):
        return jnp.pad(t, ((0, 0), (pad, 0)) + ((0, 0),) * (t.ndim - 2))

    o = chunk_gated_delta_rule(lpad(q), lpad(k), lpad(v), lpad(g), lpad(beta))[:, pad:]
    o = rmsnorm(o, dn_norm_w).astype(h.dtype)
    o = o * jax.nn.silu(z.reshape(b, l, DN_HEADS, DN_DV))
    y_dn = o.reshape(b, l, HV) @ w_dn_out

    gates = jax.nn.sigmoid(gate_logits + b_gate)
    g_conf, g_dn = jnp.split(gates, N_BRANCH, axis=-1)
    h = h + (g_conf * y_conf + g_dn * y_dn) @ w_out

    u2 = rmsnorm(h, norm_ffn_w)
    up = causal_dwconv(u2 @ w_up, ffn_dw_w, ffn_dw_b)
    f_gate, f_val = jnp.split(up, 2, axis=-1)
    return h + (jax.nn.silu(f_gate) * f_val) @ w_down


def setup_inputs(seed: int = 0) -> dict:
    key = jax.random.key(seed)
    ks = jax.random.split(key, 24)
    f32 = jnp.float32

    def nrm(k, shape, scale):
        return scale * jax.random.normal(k, shape, f32)

    dt = jnp.exp(jax.random.uniform(ks[11], (DEPTH, DN_HEADS), f32, math.log(1e-3), math.log(1e-1)))
    return {
        'x': nrm(ks[0], (BATCH, SEQ, D_MODEL), 1.0),
        'meta_tokens': nrm(ks[1], (N_META, D_MODEL), 1.0),
        'norm_mix_w': 1.0 + nrm(ks[2], (DEPTH, D_MODEL), 0.02),
        'w_in': nrm(ks[3], (DEPTH, D_MODEL, D_IN), D_MODEL ** -0.5),
        'b_gate': nrm(ks[4], (DEPTH, N_BRANCH * D_MODEL), 0.1),
        'conf_dw_w': nrm(ks[5], (DEPTH, CONF_K, DC), CONF_K ** -0.5),
        'conf_dw_b': nrm(ks[6], (DEPTH, DC), 0.02),
        'conf_ln_w': 1.0 + nrm(ks[7], (DEPTH, DC), 0.02),
        'conf_ln_b': nrm(ks[8], (DEPTH, DC), 0.02),
        'w_conf_out': nrm(ks[9], (DEPTH, DC, D_MODEL), DC ** -0.5),
        'dn_conv_w': nrm(ks[10], (DEPTH, DN_CONV_K, 2 * HK + HV), DN_CONV_K ** -0.5),
        'dn_A_log': jnp.log(jax.random.uniform(ks[12], (DEPTH, DN_HEADS), f32, 1.0, 16.0)),
        'dn_dt_bias': dt + jnp.log(-jnp.expm1(-dt)),
        'dn_norm_w': 1.0 + nrm(ks[13], (DEPTH, DN_DV), 0.02),
        'w_dn_out': nrm(ks[14], (DEPTH, HV, D_MODEL), HV ** -0.5),
        'w_out': nrm(ks[15], (DEPTH, D_MODEL, D_MODEL), D_MODEL ** -0.5),
        'norm_ffn_w': 1.0 + nrm(ks[16], (DEPTH, D_MODEL), 0.02),
        'w_up': nrm(ks[17], (DEPTH, D_MODEL, 2 * D_FF), D_MODEL ** -0.5),
        'ffn_dw_w': nrm(ks[18], (DEPTH, FFN_CONV_K, 2 * D_FF), FFN_CONV_K ** -0.5),
        'ffn_dw_b': nrm(ks[19], (DEPTH, 2 * D_FF), 0.02),
        'w_down': nrm(ks[20], (DEPTH, D_FF, D_MODEL), D_FF ** -0.5),
        'norm_final_w': 1.0 + nrm(ks[21], (D_MODEL,), 0.02),
    }


def reference(x, meta_tokens, norm_mix_w, w_in, b_gate, conf_dw_w, conf_dw_b, conf_ln_w,
              conf_ln_b, w_conf_out, dn_conv_w, dn_A_log, dn_dt_bias, dn_norm_w, w_dn_out,
              w_out, norm_ffn_w, w_up, ffn_dw_w, ffn_dw_b, w_down, norm_final_w):
    b = x.shape[0]
    meta = jnp.broadcast_to(meta_tokens[None].astype(x.dtype), (b, N_META, D_MODEL))
    h = jnp.concatenate([meta, x], axis=1)
    for i in range(DEPTH):
        h = hybrid_layer(h, norm_mix_w[i], w_in[i], b_gate[i], conf_dw_w[i], conf_dw_b[i],
                         conf_ln_w[i], conf_ln_b[i], w_conf_out[i], dn_conv_w[i], dn_A_log[i],
                         dn_dt_bias[i], dn_norm_w[i], w_dn_out[i], w_out[i], norm_ffn_w[i],
                         w_up[i], ffn_dw_w[i], ffn_dw_b[i], w_down[i])
    return rmsnorm(h, norm_final_w)[:, N_META:]
```

```python
from contextlib import ExitStack

import numpy as np
import concourse.bass as bass
import concourse.mybir as mybir
from concourse.bass_utils import run_bass_kernel_spmd

F32 = mybir.dt.float32
BF16 = mybir.dt.bfloat16
ALU = mybir.AluOpType
AF = mybir.ActivationFunctionType

D = 1024
NMETA = 16
CK = 31
DFF = 2816
NFF = DFF // 128
EPS = 1e-6
T = 256
NBLK = 39
NEG = -30000.0

P_MIXW = 0
P_BGATE = P_MIXW + 8
P_CDW = P_BGATE + 16
P_CDB = P_CDW + 8 * CK
P_LNW = P_CDB + 8
P_LNB = P_LNW + 8
P_DNC = P_LNB + 8
P_DNW = P_DNC + 24 * 4
P_FFW = P_DNW + 1
P_FDW = P_FFW + 8
P_FDB = P_FDW + 44 * 3
NPAR = P_FDB + 44
TP_DTB = 0
TP_ALOG = 8
TP_NFW = 16
NTOKP = 16 + D


class Eng:
    def __init__(self, name, h, sem, inc):
        self.name = name
        self.h = h
        self.sem = sem
        self.inc = inc
        self.count = 0
        self.known = {}


class Buf:
    __slots__ = ("name", "w", "r")

    def __init__(self, name):
        self.name = name
        self.w = None
        self.r = {}


class Ctx:
    def __init__(self, nc, es, ndma=24, nsw=40):
        self.nc = nc
        self.es = es
        mk = lambda n: es.enter_context(nc.semaphore(n))
        self.PE = Eng("PE", nc.tensor, mk("s_pe"), 1)
        self.ACT = Eng("ACT", nc.scalar, mk("s_act"), 1)
        self.DVE = Eng("DVE", nc.vector, mk("s_dve"), 1)
        self.POOL = Eng("POOL", nc.gpsimd, mk("s_pool"), 1)
        self.SP = Eng("SP", nc.sync, None, 0)
        self.dsems = [Eng("D%d" % i, None, mk("s_d%d" % i), 16) for i in range(ndma)]
        self.dsems_sw = [Eng("W%d" % i, None, mk("s_w%d" % i), 16) for i in range(nsw)]
        self.dnext = 0
        self.dnext_sw = 0
        self.ninstr = 0

    def _deps(self, R, W):
        d = {}
        for b in R:
            if b.w is not None:
                e, i = b.w
                if d.get(e, 0) < i:
                    d[e] = i
        for b in W:
            if b.w is not None:
                e, i = b.w
                if d.get(e, 0) < i:
                    d[e] = i
            for e, i in b.r.items():
                if d.get(e, 0) < i:
                    d[e] = i
        return d

    def _waits(self, eng, d):
        for src, idx in d.items():
            if src is eng and eng is self.PE:
                continue
            if eng.known.get(src, 0) >= idx:
                continue
            assert idx <= src.count, "dependency on un-signalled instruction of %s" % src.name
            eng.h.wait_ge(src.sem, idx * src.inc)
            eng.known[src] = idx

    def _mark(self, p, R, W):
        e, i = p
        for b in R:
            if b.r.get(e, 0) < i:
                b.r[e] = i
        for b in W:
            b.w = p
            b.r = {}

    def emit(self, eng, fn, R=(), W=(), inc=True):
        self._waits(eng, self._deps(R, W))
        ins = fn(eng.h)
        self.ninstr += 1
        if inc:
            eng.count += 1
            ins.then_inc(eng.sem, 1)
            idx = eng.count
        else:
            idx = eng.count + 1
        self._mark((eng, idx), R, W)
        return ins

    def dma(self, q, out, in_, R=(), W=()):
        self._waits(q, self._deps(R, W))
        if q is self.POOL:
            ds = self.dsems_sw[self.dnext_sw]
            self.dnext_sw = (self.dnext_sw + 1) % len(self.dsems_sw)
        else:
            ds = self.dsems[self.dnext]
            self.dnext = (self.dnext + 1) % len(self.dsems)
        if q.known.get(ds, 0) < ds.count:
            q.h.wait_ge(ds.sem, ds.count * 16)
            q.known[ds] = ds.count
        q.h.dma_start(out=out, in_=in_).then_inc(ds.sem, 16)
        ds.count += 1
        self.ninstr += 1
        self._mark((ds, ds.count), R, W)

    def finish(self):
        for ds in self.dsems + self.dsems_sw:
            if ds.count and self.SP.known.get(ds, 0) < ds.count:
                self.SP.h.wait_ge(ds.sem, ds.count * 16)


class Rot:
    def __init__(self, name, tens, n):
        self.t = tens
        self.n = n
        self.bufs = [Buf("%s%d" % (name, i)) for i in range(n)]
        self.i = -1

    def next(self):
        self.i = (self.i + 1) % self.n
        return self.i, self.bufs[self.i]


class _Stop(Exception):
    pass


def build(n_x=4096, dbg=False, stop_after=None):
    nc = bass.Bass("TRN2", target_bir_lowering=False)
    es = ExitStack()
    cx = Ctx(nc, es)
    PE, ACT, DVE, POOL, SP = cx.PE, cx.ACT, cx.DVE, cx.POOL, cx.SP
    E = cx.emit
    NS = T // 128
    NCK = T // 64
    ntile = n_x // T

    x_d = nc.dram_tensor("x", [n_x, D], F32, kind="ExternalInput").ap()
    meta_d = nc.dram_tensor("meta", [NMETA, D], F32, kind="ExternalInput").ap()
    wpack_d = nc.dram_tensor("wpack", [NBLK, 128, 8 * 512], F32, kind="ExternalInput").ap()
    wab_d = nc.dram_tensor("wab", [128, 8 * 16], F32, kind="ExternalInput").ap()
    par_d = nc.dram_tensor("params", [128, NPAR], F32, kind="ExternalInput").ap()
    tokp_d = nc.dram_tensor("tokpar", [128, NTOKP], F32, kind="ExternalInput").ap()
    out_d = nc.dram_tensor("out", [n_x, D], F32, kind="ExternalOutput").ap()
    wbf_d = nc.dram_tensor("wbf", [NBLK, 128, 8 * 512], BF16, kind="Internal").ap()
    dbg_d = {}

    def sb(name, shape, dt):
        return es.enter_context(nc.sbuf_tensor(name, shape, dt))

    htok = sb("htok", [128, 2, NS, D], F32)
    b_htok = [[Buf("htok%d_%d" % (i, s)) for s in range(NS)] for i in range(2)]
    xn = Rot("xn", sb("xn", [128, 1, D], F32), 1)
    junk = sb("junk", [128, D], BF16)
    b_junk = Buf("junk")
    stat = Rot("stat", sb("stat", [128, 8, 4], F32), 8)
    uT = sb("uT", [128, 8, T], BF16)
    b_uT = [Buf("uT%d" % c) for c in range(8)]
    cbuf = sb("cbuf", [128, 8, 30 + T], F32)
    b_c = [Buf("c%d" % c) for c in range(8)]
    ctmp = sb("ctmp", [128, 32], F32)
    b_ctmp = Buf("ctmp")
    ybuf = sb("ybuf", [128, 8, T], F32)
    b_y = [Buf("y%d" % c) for c in range(8)]
    accA = Rot("accA", sb("accA", [128, 2, T], F32), 2)
    accB = Rot("accB", sb("accB", [128, 1, T], F32), 1)
    ybf = Rot("ybf", sb("ybf", [128, 2, T], BF16), 2)
    ysq = Rot("ysq", sb("ysq", [128, 2, T], BF16), 2)
    lnt = sb("lnt", [128, 5, T], F32)
    b_lnt = [Buf("lnt%d" % i) for i in range(5)]
    cact = sb("cact", [128, 8, T], BF16)
    b_cact = [Buf("cact%d" % c) for c in range(8)]
    sig = Rot("sig", sb("sig", [128, 2, T], F32), 2)
    ptmp = Rot("ptmp", sb("ptmp", [128, 1, T], F32), 1)
    pre = Rot("pre", sb("pre", [128, 3, 3 + T], F32), 3)
    halq = sb("halq", [128, 24, 3], F32)
    b_halq = [Buf("halq%d" % c) for c in range(24)]
    cacc = Rot("cacc", sb("cacc", [128, 2, T], F32), 2)
    s32 = Rot("s32", sb("s32", [128, 2, T], F32), 2)
    sqb = Rot("sqb", sb("sqb", [128, 2, T], BF16), 2)
    rr = Rot("rr", sb("rr", [128, 2, T], F32), 2)
    qT = sb("qT", [128, 8, T], BF16)
    kT = sb("kT", [128, 8, T], BF16)
    vT = sb("vT", [128, 8, T], BF16)
    qdT = sb("qdT", [128, 8, T], BF16)
    b_qT = [Buf("qT%d" % c) for c in range(8)]
    b_kT = [Buf("kT%d" % c) for c in range(8)]
    b_vT = [Buf("vT%d" % c) for c in range(8)]
    b_qdT = [Buf("qdT%d" % c) for c in range(NCK)]
    sz = sb("sz", [128, 8, T], BF16)
    b_sz = [Buf("sz%d" % c) for c in range(8)]
    gts = sb("gts", [128, 16, T], BF16)
    b_gts = [Buf("gts%d" % c) for c in range(16)]
    oT = sb("oT", [128, 8, T], F32)
    b_oT = [Buf("oT%d" % c) for c in range(NCK)]
    od = sb("od", [128, 8, T], BF16)
    b_od = [Buf("od%d" % c) for c in range(8)]
    mT = sb("mT", [128, 8, T], BF16)
    b_mT = [Buf("mT%d" % c) for c in range(8)]
    mtmp = Rot("mtmp", sb("mtmp", [128, 2, T], F32), 2)
    actb = sb("actb", [128, NFF, T], BF16)
    b_act = [Buf("act%d" % c) for c in range(NFF)]
    preu = Rot("preu", sb("preu", [128, 4, 2 + T], F32), 4)
    halu = sb("halu", [128, 44, 2], F32)
    b_halu = [Buf("halu%d" % c) for c in range(44)]
    uacc = Rot("uacc", sb("uacc", [128, 4, T], F32), 4)
    sg = Rot("sg", sb("sg", [128, 2, T], F32), 2)
    abt = sb("abt", [64, NCK, 16], F32)
    b_abt = Buf("abt")
    gtok = sb("gtok", [64, NCK, 8], F32)
    beta = sb("beta", [64, NCK, 8], F32)
    b_g = Buf("g")
    b_beta = Buf("beta")
    GU = Rot("GU", sb("GU", [64, 1, 512], F32), 1)
    gcol = Rot("gcol", sb("gcol", [64, 2, 32], F32), 2)
    D1 = Rot("D1", sb("D1", [64, 1, 512], F32), 1)
    D2 = Rot("D2", sb("D2", [64, 1, 512], F32), 1)
    Abf = Rot("Abf", sb("Abf", [64, 3, 512], BF16), 3)
    Mbf = Rot("Mbf", sb("Mbf", [64, 3, 512], BF16), 3)
    Pbf = Rot("Pbf", sb("Pbf", [64, 2, 512], BF16), 2)
    Pf = Rot("Pf", sb("Pf", [64, 1, 512], F32), 1)
    AqkT = Rot("AqkT", sb("AqkT", [64, 1, 512], BF16), 1)
    TTb = Rot("TTb", sb("TTb", [64, 1, 512], BF16), 1)
    kw = Rot("kw", sb("kw", [64, 1, 1024], BF16), 1)
    kd = Rot("kd", sb("kd", [64, 1, 1024], BF16), 1)
    vb = Rot("vb", sb("vb", [64, 1, 1024], BF16), 1)
    vn = Rot("vn", sb("vn", [64, 1, 1024], BF16), 1)
    wTn = Rot("wTn", sb("wTn", [128, 1, 512], BF16), 1)
    Eq = Rot("Eq", sb("Eq", [128, 1, 512], F32), 1)
    S = sb("S", [128, 8, 128], F32)
    Sbf = sb("Sbf", [128, 8, 128], BF16)
    b_S = [Buf("S%d" % h) for h in range(8)]
    b_Sbf = [Buf("Sbf%d" % h) for h in range(8)]
    NRING = 3
    wring = Rot("wring", sb("wring", [128, NRING, 8 * 512], BF16), NRING)
    wab = sb("wab_sb", [128, 8, 16], BF16)
    wabf = sb("wabf", [128, 8 * 16], F32)
    b_wab = Buf("wab")
    ident_f = sb("ident_f", [128, 128], F32)
    ident_b = sb("ident_b", [128, 128], BF16)
    ones_b = sb("ones_b", [128, 128], BF16)
    ones_f = sb("ones_f", [64, 128], F32)
    Umat = sb("Umat", [64, 64], F32)
    par = sb("par_sb", [128, NPAR], F32)
    tokp = sb("tokp", [128, NTOKP], F32)
    cst = sb("cst", [128, 4], F32)
    b_const = Buf("const")
    b_wbf = [Buf("wbf%d" % i) for i in range(NBLK)]

    ps = es.enter_context(nc.psum_tensor("ps", [128, 8, 512], F32))
    b_ps = [Buf("ps%d" % i) for i in range(8)]
    ps_state = {"i": -1}

    def ps_next():
        ps_state["i"] = (ps_state["i"] + 1) % 6
        i = ps_state["i"]
        return ps[:, i, :], b_ps[i]

    cx.dma(SP, par[:, :], par_d[:, :], W=[b_const])
    cx.dma(SP, tokp[:, :], tokp_d[:, :], W=[b_const])
    cx.dma(SP, wabf[:, :], wab_d[:, :], W=[b_wab])
    for b in range(NBLK):
        cx.dma(POOL, wbf_d[b], wpack_d[b], W=[b_wbf[b]])

    def pool_c(fn):
        E(POOL, fn, W=[b_const])

    pool_c(lambda h: h.memset(ident_f[:], 0.0))
    pool_c(lambda h: h.affine_select(out=ident_f[:], in_=ident_f[:], pattern=[[-1, 128]],
                                     compare_op=ALU.not_equal, fill=1.0, base=0, channel_multiplier=1))
    pool_c(lambda h: h.tensor_copy(out=ident_b[:], in_=ident_f[:]))
    pool_c(lambda h: h.memset(ones_b[:], 1.0))
    pool_c(lambda h: h.memset(ones_f[:], 1.0))
    pool_c(lambda h: h.memset(Umat[:], 1.0))
    pool_c(lambda h: h.affine_select(out=Umat[:], in_=Umat[:], pattern=[[1, 64]],
                                     compare_op=ALU.is_ge, fill=0.0, base=0, channel_multiplier=-1))
    pool_c(lambda h: h.memset(cst[:, 0:1], EPS))
    pool_c(lambda h: h.memset(cst[:, 1:2], 1.0))
    pool_c(lambda h: h.memset(cbuf[:], 0.0))
    pool_c(lambda h: h.memset(halq[:], 0.0))
    pool_c(lambda h: h.memset(halu[:], 0.0))
    pool_c(lambda h: h.memset(S[:], 0.0))
    pool_c(lambda h: h.memset(Sbf[:], 0.0))
    E(POOL, lambda h: h.tensor_copy(out=wab[:].rearrange("p a b -> p (a b)"), in_=wabf[:]), R=[], W=[b_wab])
    E(ACT, lambda h: h.activation(out=tokp[:, TP_ALOG:TP_ALOG + 8], in_=tokp[:, TP_ALOG:TP_ALOG + 8], func=AF.Exp),
      W=[b_const])
    E(DVE, lambda h: h.tensor_scalar(out=tokp[:, TP_ALOG:TP_ALOG + 8], in0=tokp[:, TP_ALOG:TP_ALOG + 8],
                                     scalar1=-1.0, scalar2=None, op0=ALU.mult), W=[b_const])
    all_state = b_c + b_halq + b_halu + b_S + b_Sbf
    for b in all_state:
        b.w = b_const.w

    neg_reg = nc.gpsimd.to_reg(NEG)
    eps_c = cst[:, 0:1]
    one_c = cst[:, 1:2]

    def pcol(off, rows=128):
        return par[0:rows, off:off + 1]

    def wload(bidx):
        i, wb = wring.next()
        cx.dma(SP, wring.t[:, i, :], wbf_d[bidx], R=[b_wbf[bidx]], W=[wb])
        return wring.t[:, i, :].rearrange("p (k n) -> p k n", k=8), wb

    def norm_to_uT(hs, subs, nt, woff):
        xns = []
        for (s, rows) in subs:
            si, sbuf_ = stat.next()
            st = stat.t[0:rows, si, :]
            hin = htok[0:rows, hs, s, :]
            E(ACT, lambda h: h.activation(out=junk[0:rows, :], in_=hin, func=AF.Square, accum_out=st[:, 0:1]),
              R=[b_htok[hs][s]], W=[b_junk, sbuf_])
            E(ACT, lambda h: h.activation(out=st[:, 1:2], in_=st[:, 0:1], func=AF.Sqrt, bias=eps_c[0:rows], scale=1.0 / D),
              R=[b_const], W=[sbuf_])
            E(DVE, lambda h: h.reciprocal(out=st[:, 2:3], in_=st[:, 1:2]), W=[sbuf_])
            xi, xb = xn.next()
            xt = xn.t[0:rows, xi, :]
            E(DVE, lambda h: h.tensor_scalar(out=xt, in0=hin, scalar1=st[:, 2:3], scalar2=None, op0=ALU.mult),
              R=[b_htok[hs][s], sbuf_], W=[xb])
            for c in range(8):
                pt, pb = ps_next()
                E(PE, lambda h: h.transpose(pt[:, 0:rows], xt[:, c * 128:(c + 1) * 128], ident_f[0:rows, 0:rows]),
                  R=[xb, b_const], W=[pb])
                tgt = uT[:, c, s * 128:s * 128 + rows]
                if c % 2 == 0:
                    E(ACT, lambda h: h.activation(out=tgt, in_=pt[:, 0:rows], func=AF.Copy, scale=pcol(woff + c)),
                      R=[pb, b_const], W=[b_uT[c]])
                else:
                    E(DVE, lambda h: h.tensor_scalar(out=tgt, in0=pt[:, 0:rows], scalar1=pcol(woff + c), scalar2=None,
                                                     op0=ALU.mult), R=[pb, b_const], W=[b_uT[c]])

    def proj_chunk(wt, wb, g, nt, src, b_src, nk=8):
        pt, pb = ps_next()
        for kc in range(nk):
            E(PE, lambda h: h.matmul(pt[:, 0:nt], wt[:, kc, g * 128:(g + 1) * 128], src[:, kc, 0:nt],
                                     start=(kc == 0), stop=(kc == nk - 1)),
              R=[wb, b_src[kc]], W=[pb], inc=(kc == nk - 1))
        return pt[:, 0:nt], pb

    def l2_rstd(src_ap, b_src, nt, scale_in):
        qi, qb = sqb.next()
        sq = sqb.t[:, qi, 0:nt]
        E(ACT, lambda h: h.activation(out=sq, in_=src_ap, func=AF.Square), R=[b_src], W=[qb])
        pt, pb = ps_next()
        E(PE, lambda h: h.matmul(pt[:, 0:nt], ones_b[:, :], sq, start=True, stop=True), R=[qb, b_const], W=[pb])
        ri, rb = rr.next()
        r = rr.t[:, ri, 0:nt]
        E(ACT, lambda h: h.activation(out=r, in_=pt[:, 0:nt], func=AF.Sqrt, bias=eps_c, scale=scale_in),
          R=[pb, b_const], W=[rb])
        E(DVE, lambda h: h.reciprocal(out=r, in_=r), W=[rb])
        return r, rb

    def conv_taps(eng, acc, accb, taps, Rb, nt, bias=None):
        for ti_, (src, wcol) in enumerate(taps):
            if ti_ == 0:
                if bias is not None:
                    E(eng, lambda h: h.tensor_scalar(out=acc, in0=src, scalar1=wcol, scalar2=bias, op0=ALU.mult,
                                                     op1=ALU.add), R=Rb + [b_const], W=[accb])
                else:
                    E(eng, lambda h: h.tensor_scalar(out=acc, in0=src, scalar1=wcol, scalar2=None, op0=ALU.mult),
                      R=Rb + [b_const], W=[accb])
            elif eng is DVE:
                E(DVE, lambda h: h.scalar_tensor_tensor(out=acc, in0=src, scalar=wcol, in1=acc, op0=ALU.mult,
                                                        op1=ALU.add), R=Rb + [b_const], W=[accb])
            else:
                pi_, pb_ = ptmp.next()
                tmp = ptmp.t[:, pi_, 0:nt]
                E(POOL, lambda h: h.tensor_scalar(out=tmp, in0=src, scalar1=wcol, scalar2=None, op0=ALU.mult),
                  R=Rb + [b_const], W=[pb_])
                E(POOL, lambda h: h.tensor_tensor(out=acc, in0=acc, in1=tmp, op=ALU.add), R=[pb_], W=[accb])

    def chk(name):
        if stop_after == name:
            raise _Stop()

    def do_tile(ti, is_meta):
        hs = ti % 2
        if is_meta:
            nt = NMETA
            subs = [(0, NMETA)]
            chunks = [(0, NMETA)]
            cx.dma(SP, htok[0:NMETA, hs, 0, :], meta_d[:, :], W=[b_htok[hs][0]])
        else:
            nt = T
            subs = [(s, 128) for s in range(NS)]
            chunks = [(c * 64, 64) for c in range(NCK)]
            x0 = (ti - 1) * T
            for (s, rows) in subs:
                cx.dma(SP, htok[:, hs, s, :], x_d[x0 + s * 128:x0 + (s + 1) * 128, :], W=[b_htok[hs][s]])

        norm_to_uT(hs, subs, nt, P_MIXW)

        chk("ph1")
        for jp in range(4):
            wt, wb = wload(jp)
            for jj in range(2):
                j = 2 * jp + jj
                pg, pgb = proj_chunk(wt, wb, 2 * jj, nt, uT, b_uT)
                gi, gb = sig.next()
                sg_ap = sig.t[:, gi, 0:nt]
                E(ACT, lambda h: h.activation(out=sg_ap, in_=pg, func=AF.Sigmoid), R=[pgb], W=[gb])
                pv, pvb = proj_chunk(wt, wb, 2 * jj + 1, nt, uT, b_uT)
                E(DVE, lambda h: h.tensor_tensor(out=cbuf[:, j, 30:30 + nt], in0=pv, in1=sg_ap, op=ALU.mult),
                  R=[pvb, gb], W=[b_c[j]])
        for j in range(8):
            ai, ab = accA.next()
            bi, bb = accB.next()
            aa = accA.t[:, ai, 0:nt]
            ba = accB.t[:, bi, 0:nt]
            NDVE = 21
            conv_taps(DVE, aa, ab, [(cbuf[:, j, t_:t_ + nt], pcol(P_CDW + j * CK + t_)) for t_ in range(0, NDVE)],
                      [b_c[j]], nt, bias=pcol(P_CDB + j))
            conv_taps(POOL, ba, bb, [(cbuf[:, j, t_:t_ + nt], pcol(P_CDW + j * CK + t_)) for t_ in range(NDVE, CK)],
                      [b_c[j]], nt)
            E(DVE, lambda h: h.tensor_tensor(out=ybuf[:, j, 0:nt], in0=aa, in1=ba, op=ALU.add), R=[ab, bb], W=[b_y[j]])
            E(POOL, lambda h: h.tensor_copy(out=ctmp[:, 0:30], in_=cbuf[:, j, nt:nt + 30]), R=[b_c[j]], W=[b_ctmp])
            E(POOL, lambda h: h.tensor_copy(out=cbuf[:, j, 0:30], in_=ctmp[:, 0:30]), R=[b_ctmp], W=[b_c[j]])

        def qkv_block(bidx, kind, h0):
            wt, wb = wload(bidx)
            for g in range(4):
                hh = h0 + g
                ch = {"q": 0, "k": 8, "v": 16}[kind] + hh
                pp, ppb = proj_chunk(wt, wb, g, nt, uT, b_uT)
                pi, prb = pre.next()
                pr = pre.t[:, pi, :]
                E(POOL, lambda h: h.tensor_copy(out=pr[:, 0:3], in_=halq[:, ch, :]), R=[b_halq[ch]], W=[prb])
                E(ACT, lambda h: h.activation(out=pr[:, 3:3 + nt], in_=pp, func=AF.Copy), R=[ppb], W=[prb])
                E(POOL, lambda h: h.tensor_copy(out=halq[:, ch, :], in_=pr[:, nt:nt + 3]), R=[prb], W=[b_halq[ch]])
                ci, cb = cacc.next()
                ca = cacc.t[:, ci, 0:nt]
                eng = DVE if (g != 3) else POOL
                conv_taps(eng, ca, cb, [(pr[:, t_:t_ + nt], pcol(P_DNC + ch * 4 + t_)) for t_ in range(4)], [prb], nt)
                if kind == "v":
                    E(ACT, lambda h: h.activation(out=vT[:, hh, 0:nt], in_=ca, func=AF.Silu), R=[cb], W=[b_vT[hh]])
                else:
                    si_, sb_ = s32.next()
                    sa = s32.t[:, si_, 0:nt]
                    E(ACT, lambda h: h.activation(out=sa, in_=ca, func=AF.Silu), R=[cb], W=[sb_])
                    r, rb = l2_rstd(sa, sb_, nt, 1.0)
                    if kind == "k":
                        E(DVE, lambda h: h.tensor_tensor(out=kT[:, hh, 0:nt], in0=sa, in1=r, op=ALU.mult),
                          R=[sb_, rb], W=[b_kT[hh]])
                    else:
                        E(DVE, lambda h: h.scalar_tensor_tensor(out=qT[:, hh, 0:nt], in0=sa, scalar=128.0 ** -0.5, in1=r,
                                                                op0=ALU.mult, op1=ALU.mult), R=[sb_, rb], W=[b_qT[hh]])

        qkv_block(4, "k", 0)
        qkv_block(5, "k", 4)
        qkv_block(6, "q", 0)
        qkv_block(7, "q", 4)
        qkv_block(8, "v", 0)
        qkv_block(9, "v", 4)
        for bb_ in range(2):
            wt, wb = wload(10 + bb_)
            for g in range(4):
                hh = bb_ * 4 + g
                pp, ppb = proj_chunk(wt, wb, g, nt, uT, b_uT)
                E(ACT, lambda h: h.activation(out=sz[:, hh, 0:nt], in_=pp, func=AF.Silu), R=[ppb], W=[b_sz[hh]])
        for bb_ in range(4):
            wt, wb = wload(12 + bb_)
            for g in range(4):
                jg = bb_ * 4 + g
                pp, ppb = proj_chunk(wt, wb, g, nt, uT, b_uT)
                E(ACT, lambda h: h.activation(out=gts[:, jg, 0:nt], in_=pp, func=AF.Sigmoid, bias=pcol(P_BGATE + jg)),
                  R=[ppb, b_const], W=[b_gts[jg]])
        pab, pabb = ps_next()
        pab3 = pab.rearrange("p (c n) -> p c n", n=16)
        for ck, (o, cl) in enumerate(chunks):
            for kc in range(8):
                E(PE, lambda h: h.matmul(pab3[0:cl, ck, :], uT[:, kc, o:o + cl], wab[:, kc, :], start=(kc == 0),
                                         stop=(kc == 7)), R=[b_uT[kc], b_wab], W=[pabb], inc=(kc == 7))
        nck = len(chunks)
        cl0 = chunks[0][1]
        dtb_b = tokp[0:cl0, TP_DTB:TP_DTB + 8].unsqueeze(1).to_broadcast([cl0, nck, 8])
        negA_b = tokp[0:cl0, TP_ALOG:TP_ALOG + 8].unsqueeze(1).to_broadcast([cl0, nck, 8])
        E(DVE, lambda h: h.tensor_tensor(out=abt[0:cl0, 0:nck, 0:8], in0=pab3[0:cl0, 0:nck, 0:8], in1=dtb_b, op=ALU.add),
          R=[pabb, b_const], W=[b_abt])
        E(ACT, lambda h: h.activation(out=abt[0:cl0, 0:nck, 0:8], in_=abt[0:cl0, 0:nck, 0:8], func=AF.Exp), W=[b_abt])
        E(ACT, lambda h: h.activation(out=abt[0:cl0, 0:nck, 0:8], in_=abt[0:cl0, 0:nck, 0:8], func=AF.Ln,
                                      bias=one_c[0:cl0]), R=[b_const], W=[b_abt])
        E(DVE, lambda h: h.tensor_tensor(out=gtok[0:cl0, 0:nck, :], in0=abt[0:cl0, 0:nck, 0:8], in1=negA_b, op=ALU.mult),
          R=[b_abt, b_const], W=[b_g])
        E(ACT, lambda h: h.activation(out=beta[0:cl0, 0:nck, :], in_=pab3[0:cl0, 0:nck, 8:16], func=AF.Sigmoid),
          R=[pabb], W=[b_beta])

        chk("ph2")
        pm, pmb = ps_next()
        pq, pqb = ps_next()
        for j in range(8):
            yi, yb_ = ybf.next()
            qi, qb_ = ysq.next()
            E(ACT, lambda h: h.activation(out=ybf.t[:, yi, 0:nt], in_=ybuf[:, j, 0:nt], func=AF.Copy), R=[b_y[j]], W=[yb_])
            E(ACT, lambda h: h.activation(out=ysq.t[:, qi, 0:nt], in_=ybuf[:, j, 0:nt], func=AF.Square), R=[b_y[j]], W=[qb_])
            E(PE, lambda h: h.matmul(pm[:, 0:nt], ones_b[:, :], ybf.t[:, yi, 0:nt], start=(j == 0), stop=(j == 7)),
              R=[yb_, b_const], W=[pmb])
            E(PE, lambda h: h.matmul(pq[:, 0:nt], ones_b[:, :], ysq.t[:, qi, 0:nt], start=(j == 0), stop=(j == 7)),
              R=[qb_, b_const], W=[pqb])
        m_ = lnt[:, 0, 0:nt]
        msq = lnt[:, 1, 0:nt]
        var = lnt[:, 2, 0:nt]
        rstd = lnt[:, 3, 0:nt]
        mr = lnt[:, 4, 0:nt]
        E(DVE, lambda h: h.tensor_scalar(out=m_, in0=pm[:, 0:nt], scalar1=1.0 / D, scalar2=None, op0=ALU.mult),
          R=[pmb], W=[b_lnt[0]])
        E(DVE, lambda h: h.tensor_tensor(out=msq, in0=m_, in1=m_, op=ALU.mult), R=[b_lnt[0]], W=[b_lnt[1]])
        E(DVE, lambda h: h.scalar_tensor_tensor(out=var, in0=pq[:, 0:nt], scalar=1.0 / D, in1=msq, op0=ALU.mult,
                                                op1=ALU.subtract), R=[pqb, b_lnt[1]], W=[b_lnt[2]])
        E(ACT, lambda h: h.activation(out=rstd, in_=var, func=AF.Sqrt, bias=eps_c, scale=1.0), R=[b_lnt[2], b_const],
          W=[b_lnt[3]])
        E(DVE, lambda h: h.reciprocal(out=rstd, in_=rstd), W=[b_lnt[3]])
        E(DVE, lambda h: h.tensor_tensor(out=mr, in0=m_, in1=rstd, op=ALU.mult), R=[b_lnt[0], b_lnt[3]], W=[b_lnt[4]])
        for j in range(8):
            ai, ab = accA.next()
            aa = accA.t[:, ai, 0:nt]
            E(DVE, lambda h: h.tensor_tensor(out=aa, in0=ybuf[:, j, 0:nt], in1=rstd, op=ALU.mult), R=[b_y[j], b_lnt[3]],
              W=[ab])
            E(POOL, lambda h: h.tensor_tensor(out=aa, in0=aa, in1=mr, op=ALU.subtract), R=[b_lnt[4]], W=[ab])
            E(ACT, lambda h: h.activation(out=cact[:, j, 0:nt], in_=aa, func=AF.Silu, bias=pcol(P_LNB + j),
                                          scale=pcol(P_LNW + j)), R=[ab, b_const], W=[b_cact[j]])

        chk("ph3")
        for ck, (o, cl) in enumerate(chunks):
            nlev = 5 if cl == 64 else 3
            W8 = 8 * cl
            gi, gub = GU.next()
            gu3 = GU.t[0:cl, gi, 0:W8].rearrange("p (h j) -> p h j", h=8)
            gsl = gtok[0:cl, ck, :]
            E(DVE, lambda h: h.tensor_tensor(out=gu3, in0=gsl.unsqueeze(2).to_broadcast([cl, 8, cl]),
                                             in1=Umat[0:cl, 0:cl].unsqueeze(1).to_broadcast([cl, 8, cl]), op=ALU.mult),
              R=[b_g, b_const], W=[gub])
            bps = ps[:, 6, 0:W8]
            bps3 = bps.rearrange("p (h j) -> p h j", h=8)
            E(PE, lambda h: h.matmul(bps, ones_f[0:cl, :], GU.t[0:cl, gi, 0:W8], start=True, stop=True),
              R=[gub, b_const], W=[b_ps[6]])
            pc, pcb = ps_next()
            E(PE, lambda h: h.matmul(pc[0:cl, 0:8], Umat[0:cl, 0:cl], gsl, start=True, stop=True), R=[b_g, b_const], W=[pcb])
            ci_, gcb = gcol.next()
            gc = gcol.t[0:cl, ci_, :]
            E(ACT, lambda h: h.activation(out=gc[:, 0:8], in_=pc[0:cl, 0:8], func=AF.Copy), R=[pcb], W=[gcb])
            E(ACT, lambda h: h.activation(out=gc[:, 8:16], in_=pc[0:cl, 0:8], func=AF.Exp), R=[pcb], W=[gcb])
            E(DVE, lambda h: h.tensor_tensor(out=gc[:, 16:24], in0=gc[:, 8:16], in1=beta[0:cl, ck, :], op=ALU.mult),
              R=[b_beta], W=[gcb])
            E(DVE, lambda h: h.tensor_tensor(out=gc[:, 24:32], in0=bps3[0:cl, :, cl - 1], in1=gc[:, 0:8], op=ALU.subtract),
              R=[b_ps[6]], W=[gcb])
            E(ACT, lambda h: h.activation(out=gc[:, 24:32], in_=gc[:, 24:32], func=AF.Exp), W=[gcb])
            d1i, d1b = D1.next()
            d2i, d2b = D2.next()
            d1 = D1.t[0:cl, d1i, 0:W8].rearrange("p (h j) -> p h j", h=8)
            d2 = D2.t[0:cl, d2i, 0:W8].rearrange("p (h j) -> p h j", h=8)
            gcb3 = gc[:, 0:8].unsqueeze(2).to_broadcast([cl, 8, cl])
            E(DVE, lambda h: h.tensor_tensor(out=d1, in0=gcb3, in1=bps3[0:cl], op=ALU.subtract), R=[gcb, b_ps[6]], W=[d1b])
            E(DVE, lambda h: h.tensor_tensor(out=d2, in0=bps3[0:cl], in1=gcb3, op=ALU.subtract), R=[gcb, b_ps[6]], W=[d2b])
            E(POOL, lambda h: h.affine_select(out=d1, in_=d1, pattern=[[0, 8], [-1, cl]], compare_op=ALU.is_gt, fill=neg_reg,
                                              base=0, channel_multiplier=1), W=[d1b])
            E(POOL, lambda h: h.affine_select(out=d2, in_=d2, pattern=[[0, 8], [1, cl]], compare_op=ALU.is_ge, fill=neg_reg,
                                              base=0, channel_multiplier=-1), W=[d2b])
            E(ACT, lambda h: h.activation(out=d1, in_=d1, func=AF.Exp), W=[d1b])
            E(ACT, lambda h: h.activation(out=d2, in_=d2, func=AF.Exp), W=[d2b])
            E(POOL, lambda h: h.tensor_tensor(out=d1, in0=d1, in1=beta[0:cl, ck, :].unsqueeze(2).to_broadcast([cl, 8, cl]),
                                              op=ALU.mult), R=[b_beta], W=[d1b])
            chk("d1")
            pkk, pkkb = ps_next()
            pqk, pqkb = ps_next()
            for hh in range(8):
                E(PE, lambda h: h.matmul(pkk[0:cl, hh * cl:(hh + 1) * cl], kT[:, hh, o:o + cl], kT[:, hh, o:o + cl],
                                         start=True, stop=True), R=[b_kT[hh]], W=[pkkb], inc=(hh == 7))
            for hh in range(8):
                E(PE, lambda h: h.matmul(pqk[0:cl, hh * cl:(hh + 1) * cl], kT[:, hh, o:o + cl], qT[:, hh, o:o + cl],
                                         start=True, stop=True), R=[b_kT[hh], b_qT[hh]], W=[pqkb], inc=(hh == 7))
            a_i, a_b = Abf.next()
            A0 = Abf.t[0:cl, a_i, 0:W8]
            E(DVE, lambda h: h.tensor_tensor(out=A0, in0=pkk[0:cl, 0:W8], in1=D1.t[0:cl, d1i, 0:W8], op=ALU.mult),
              R=[pkkb, d1b], W=[a_b])
            q_i, q_b = AqkT.next()
            AQ = AqkT.t[0:cl, q_i, 0:W8]
            E(DVE, lambda h: h.tensor_tensor(out=AQ, in0=pqk[0:cl, 0:W8], in1=D2.t[0:cl, d2i, 0:W8], op=ALU.mult),
              R=[pqkb, d2b], W=[q_b])
            chk("d2")
            pmt, pmtb = ps_next()
            for hh in range(8):
                E(PE, lambda h: h.matmul(pmt[0:cl, hh * cl:(hh + 1) * cl], A0[:, hh * cl:(hh + 1) * cl], ident_b[0:cl, 0:cl],
                                         start=True, stop=True), R=[a_b, b_const], W=[pmtb], inc=(hh == 7))
            m_i, m_b = Mbf.next()
            M0 = Mbf.t[0:cl, m_i, 0:W8]
            E(ACT, lambda h: h.activation(out=M0, in_=pmt[0:cl, 0:W8], func=AF.Copy), R=[pmtb], W=[m_b])
            chk("d3")
            pf_i, pf_b = Pf.next()
            PF = Pf.t[0:cl, pf_i, 0:W8]
            E(DVE, lambda h: h.tensor_tensor(out=PF.rearrange("p (h j) -> p h j", h=8),
                                             in0=ident_f[0:cl, 0:cl].unsqueeze(1).to_broadcast([cl, 8, cl]),
                                             in1=M0.rearrange("p (h j) -> p h j", h=8), op=ALU.subtract),
              R=[m_b, b_const], W=[pf_b])
            p_i, p_b = Pbf.next()
            Pk = Pbf.t[0:cl, p_i, 0:W8]
            E(POOL, lambda h: h.tensor_copy(out=Pk, in_=PF), R=[pf_b], W=[p_b])
            chk("e0")
            Ak, Ak_b, Mk, Mk_b = A0, a_b, M0, m_b
            for lev in range(1, nlev + 1):
                last = (lev == nlev)
                if lev == 2:
                    chk("e2")
                pa, pab_ = ps_next()
                for hh in range(8):
                    sl = slice(hh * cl, (hh + 1) * cl)
                    E(PE, lambda h: h.matmul(pa[0:cl, sl], Mk[:, sl], Ak[:, sl], start=True, stop=True),
                      R=[Mk_b, Ak_b], W=[pab_], inc=(hh == 7))
                if not last:
                    pm2, pm2b = ps_next()
                    for hh in range(8):
                        sl = slice(hh * cl, (hh + 1) * cl)
                        E(PE, lambda h: h.matmul(pm2[0:cl, sl], Ak[:, sl], Mk[:, sl], start=True, stop=True),
                          R=[Mk_b, Ak_b], W=[pm2b], inc=(hh == 7))
                chk("e1")
                na_i, na_b = Abf.next()
                An = Abf.t[0:cl, na_i, 0:W8]
                E(ACT, lambda h: h.activation(out=An, in_=pa[0:cl, 0:W8], func=AF.Copy), R=[pab_], W=[na_b])
                if not last:
                    nm_i, nm_b = Mbf.next()
                    Mn = Mbf.t[0:cl, nm_i, 0:W8]
                    E(DVE, lambda h: h.tensor_copy(out=Mn, in_=pm2[0:cl, 0:W8]), R=[pm2b], W=[nm_b])
                pp_, ppb_ = ps_next()
                for hh in range(8):
                    sl = slice(hh * cl, (hh + 1) * cl)
                    E(PE, lambda h: h.matmul(pp_[0:cl, sl], An[:, sl], Pk[:, sl], start=True, stop=True),
                      R=[na_b, p_b], W=[ppb_], inc=(hh == 7))
                E(DVE, lambda h: h.tensor_tensor(out=PF, in0=PF, in1=pp_[0:cl, 0:W8], op=ALU.add), R=[ppb_], W=[pf_b])
                if not last:
                    np_i, np_b = Pbf.next()
                    Pn = Pbf.t[0:cl, np_i, 0:W8]
                    E(POOL, lambda h: h.tensor_copy(out=Pn, in_=PF), R=[pf_b], W=[np_b])
                    Pk, p_b = Pn, np_b
                    Mk, Mk_b = Mn, nm_b
                Ak, Ak_b = An, na_b
            t_i, t_b = TTb.next()
            TT = TTb.t[0:cl, t_i, 0:W8]
            E(POOL, lambda h: h.tensor_copy(out=TT, in_=PF), R=[pf_b], W=[t_b])
            chk("d4")
            kw_i, kw_b = kw.next()
            kd_i, kd_b = kd.next()
            vb_i, vb_b = vb.next()

            def sc(col0, g0):
                return gc[:, col0 + g0:col0 + g0 + 4].unsqueeze(2).to_broadcast([cl, 4, 128])

            for grp in range(2):
                pkt, pktb = ps_next()
                pvt, pvtb = ps_next()
                for g in range(4):
                    hh = grp * 4 + g
                    E(PE, lambda h: h.matmul(pkt[0:cl, g * 128:(g + 1) * 128], kT[:, hh, o:o + cl], ident_b[:, :],
                                             start=True, stop=True), R=[b_kT[hh], b_const], W=[pktb], inc=(g == 3))
                for g in range(4):
                    hh = grp * 4 + g
                    E(PE, lambda h: h.matmul(pvt[0:cl, g * 128:(g + 1) * 128], vT[:, hh, o:o + cl], ident_b[:, :],
                                             start=True, stop=True), R=[b_vT[hh], b_const], W=[pvtb], inc=(g == 3))
                k3 = pkt[0:cl, :].rearrange("p (h d) -> p h d", h=4)
                v3 = pvt[0:cl, :].rearrange("p (h d) -> p h d", h=4)
                csl = slice(grp * 512, (grp + 1) * 512)
                E(DVE, lambda h: h.tensor_tensor(out=kw.t[0:cl, kw_i, csl].rearrange("p (h d) -> p h d", h=4), in0=k3,
                                                 in1=sc(16, grp * 4), op=ALU.mult), R=[pktb, gcb], W=[kw_b])
                E(DVE, lambda h: h.tensor_tensor(out=kd.t[0:cl, kd_i, csl].rearrange("p (h d) -> p h d", h=4), in0=k3,
                                                 in1=sc(24, grp * 4), op=ALU.mult), R=[pktb, gcb], W=[kd_b])
                E(DVE, lambda h: h.tensor_tensor(out=vb.t[0:cl, vb_i, csl].rearrange("p (h d) -> p h d", h=4), in0=v3,
                                                 in1=beta[0:cl, ck, grp * 4:grp * 4 + 4].unsqueeze(2).to_broadcast([cl, 4, 128]),
                                                 op=ALU.mult), R=[pvtb, b_beta], W=[vb_b])
            chk("d5")
            pw, pwb = ps_next()
            for hh in range(8):
                E(PE, lambda h: h.matmul(pw[:, hh * cl:(hh + 1) * cl], kw.t[0:cl, kw_i, hh * 128:(hh + 1) * 128],
                                         TT[:, hh * cl:(hh + 1) * cl], start=True, stop=True), R=[kw_b, t_b], W=[pwb],
                  inc=(hh == 7))
            w_i, w_b = wTn.next()
            WT = wTn.t[:, w_i, 0:W8]
            E(ACT, lambda h: h.activation(out=WT, in_=pw[:, 0:W8], func=AF.Copy, scale=-1.0), R=[pwb], W=[w_b])
            e_i, e_b = Eq.next()
            EQ = Eq.t[:, e_i, 0:W8]
            E(ACT, lambda h: h.activation(out=EQ, in_=bps, func=AF.Exp), R=[b_ps[6]], W=[e_b])
            EQ3 = EQ.rearrange("p (h j) -> p h j", h=8)
            E(POOL, lambda h: h.tensor_tensor(out=qdT[:, :, o:o + cl], in0=qT[:, :, o:o + cl], in1=EQ3, op=ALU.mult),
              R=b_qT + [e_b], W=[b_qdT[ck]])
            chk("d6")
            vn_i, vn_b = vn.next()
            VN = vn.t[0:cl, vn_i, :]
            for grp in range(2):
                pv_, pvb_ = ps_next()
                for g in range(4):
                    hh = grp * 4 + g
                    E(PE, lambda h: h.matmul(pv_[0:cl, g * 128:(g + 1) * 128], TT[:, hh * cl:(hh + 1) * cl],
                                             vb.t[0:cl, vb_i, hh * 128:(hh + 1) * 128], start=True, stop=False),
                      R=[t_b, vb_b], W=[pvb_], inc=False)
                    E(PE, lambda h: h.matmul(pv_[0:cl, g * 128:(g + 1) * 128], WT[:, hh * cl:(hh + 1) * cl],
                                             Sbf[:, hh, :], start=False, stop=True), R=[w_b, b_Sbf[hh]], W=[pvb_],
                      inc=(g == 3))
                if grp == 0:
                    E(ACT, lambda h: h.activation(out=VN[:, 0:512], in_=pv_[0:cl, :], func=AF.Copy), R=[pvb_], W=[vn_b])
                else:
                    E(DVE, lambda h: h.tensor_copy(out=VN[:, 512:1024], in_=pv_[0:cl, :]), R=[pvb_], W=[vn_b])
            po, pob = ps_next()
            for hh in range(8):
                E(PE, lambda h: h.matmul(po[:, hh * cl:(hh + 1) * cl], Sbf[:, hh, :], qdT[:, hh, o:o + cl], start=True,
                                         stop=False), R=[b_Sbf[hh], b_qdT[ck]], W=[pob], inc=False)
                E(PE, lambda h: h.matmul(po[:, hh * cl:(hh + 1) * cl], VN[:, hh * 128:(hh + 1) * 128],
                                         AQ[:, hh * cl:(hh + 1) * cl], start=False, stop=True), R=[vn_b, q_b], W=[pob],
                  inc=(hh == 7))
            E(ACT, lambda h: h.activation(out=oT[:, :, o:o + cl], in_=po[:, 0:W8].rearrange("p (h j) -> p h j", h=8),
                                          func=AF.Copy), R=[pob], W=[b_oT[ck]])
            for grp in range(2):
                pd, pdb = ps_next()
                for g in range(4):
                    hh = grp * 4 + g
                    E(PE, lambda h: h.matmul(pd[:, g * 128:(g + 1) * 128], kd.t[0:cl, kd_i, hh * 128:(hh + 1) * 128],
                                             VN[:, hh * 128:(hh + 1) * 128], start=True, stop=True), R=[kd_b, vn_b],
                      W=[pdb], inc=(g == 3))
                for g in range(4):
                    hh = grp * 4 + g
                    E(DVE, lambda h: h.scalar_tensor_tensor(out=S[:, hh, :], in0=S[:, hh, :],
                                                            scalar=EQ[:, hh * cl + cl - 1:hh * cl + cl],
                                                            in1=pd[:, g * 128:(g + 1) * 128], op0=ALU.mult, op1=ALU.add),
                      R=[e_b, pdb], W=[b_S[hh]])
                    E(POOL, lambda h: h.tensor_copy(out=Sbf[:, hh, :], in_=S[:, hh, :]), R=[b_S[hh]], W=[b_Sbf[hh]])
        for hh in range(8):
            qi, qb = sqb.next()
            sq = sqb.t[:, qi, 0:nt]
            E(ACT, lambda h: h.activation(out=sq, in_=oT[:, hh, 0:nt], func=AF.Square), R=b_oT[0:nck], W=[qb])
            pt, pb = ps_next()
            E(PE, lambda h: h.matmul(pt[:, 0:nt], ones_b[:, :], sq, start=True, stop=True), R=[qb, b_const], W=[pb])
            ri, rb = rr.next()
            r = rr.t[:, ri, 0:nt]
            E(ACT, lambda h: h.activation(out=r, in_=pt[:, 0:nt], func=AF.Sqrt, bias=eps_c, scale=1.0 / 128.0),
              R=[pb, b_const], W=[rb])
            E(DVE, lambda h: h.reciprocal(out=r, in_=r), W=[rb])
            E(DVE, lambda h: h.tensor_tensor(out=r, in0=r, in1=oT[:, hh, 0:nt], op=ALU.mult), R=b_oT[0:nck], W=[rb])
            E(DVE, lambda h: h.scalar_tensor_tensor(out=od[:, hh, 0:nt], in0=r, scalar=pcol(P_DNW), in1=sz[:, hh, 0:nt],
                                                    op0=ALU.mult, op1=ALU.mult), R=[rb, b_sz[hh], b_const], W=[b_od[hh]])

        chk("ph4")
        for j in range(8):
            if j % 4 == 0:
                wco_ = wload(16 + j // 4)
                wdn_ = wload(18 + j // 4)
            wt, wb = wco_
            pa_, pab2 = proj_chunk(wt, wb, j % 4, nt, cact, b_cact)
            wt, wb = wdn_
            pb_, pbb2 = proj_chunk(wt, wb, j % 4, nt, od, b_od)
            mi, mb = mtmp.next()
            mt_ = mtmp.t[:, mi, 0:nt]
            E(DVE, lambda h: h.tensor_tensor(out=mt_, in0=pa_, in1=gts[:, j, 0:nt], op=ALU.mult), R=[pab2, b_gts[j]], W=[mb])
            mi2, mb2 = mtmp.next()
            mt2 = mtmp.t[:, mi2, 0:nt]
            E(DVE, lambda h: h.tensor_tensor(out=mt2, in0=pb_, in1=gts[:, 8 + j, 0:nt], op=ALU.mult),
              R=[pbb2, b_gts[8 + j]], W=[mb2])
            E(POOL, lambda h: h.tensor_tensor(out=mT[:, j, 0:nt], in0=mt_, in1=mt2, op=ALU.add), R=[mb, mb2], W=[b_mT[j]])
        for half in range(2):
            wt, wb = wload(20 + half)
            for (s, rows) in subs:
                pt, pb = ps_next()
                for kc in range(8):
                    E(PE, lambda h: h.matmul(pt[0:rows, :], mT[:, kc, s * 128:s * 128 + rows], wt[:, kc, :],
                                             start=(kc == 0), stop=(kc == 7)), R=[b_mT[kc], wb], W=[pb], inc=(kc == 7))
                hsl = htok[0:rows, hs, s, half * 512:(half + 1) * 512]
                E(DVE, lambda h: h.tensor_tensor(out=hsl, in0=hsl, in1=pt[0:rows, :], op=ALU.add), R=[pb],
                  W=[b_htok[hs][s]])

        chk("ph5")
        norm_to_uT(hs, subs, nt, P_FFW)

        chk("ph6")
        for jp in range(11):
            wt, wb = wload(22 + jp)
            for jj in range(2):
                j = 2 * jp + jj
                accs = []
                for part in range(2):
                    ch = j if part == 0 else NFF + j
                    pp, ppb = proj_chunk(wt, wb, 2 * jj + part, nt, uT, b_uT)
                    pi, prb = preu.next()
                    pr = preu.t[:, pi, :]
                    E(POOL, lambda h: h.tensor_copy(out=pr[:, 0:2], in_=halu[:, ch, :]), R=[b_halu[ch]], W=[prb])
                    E(ACT, lambda h: h.activation(out=pr[:, 2:2 + nt], in_=pp, func=AF.Copy), R=[ppb], W=[prb])
                    E(POOL, lambda h: h.tensor_copy(out=halu[:, ch, :], in_=pr[:, nt:nt + 2]), R=[prb], W=[b_halu[ch]])
                    ui, ub = uacc.next()
                    ua = uacc.t[:, ui, 0:nt]
                    eng = DVE if (part == 0 or jj == 0) else POOL
                    conv_taps(eng, ua, ub, [(pr[:, t_:t_ + nt], pcol(P_FDW + ch * 3 + t_)) for t_ in range(3)], [prb], nt,
                              bias=pcol(P_FDB + ch))
                    accs.append((ua, ub))
                si_, sgb = sg.next()
                sga = sg.t[:, si_, 0:nt]
                E(ACT, lambda h: h.activation(out=sga, in_=accs[0][0], func=AF.Silu), R=[accs[0][1]], W=[sgb])
                E(DVE, lambda h: h.tensor_tensor(out=actb[:, j, 0:nt], in0=sga, in1=accs[1][0], op=ALU.mult),
                  R=[sgb, accs[1][1]], W=[b_act[j]])

        chk("ph7")
        for half in range(2):
            pts = [ps_next() for _ in subs]
            for grp in range(3):
                wt, wb = wload(33 + half * 3 + grp)
                nk = 8 if grp < 2 else NFF - 16
                for si_, (s, rows) in enumerate(subs):
                    pt, pb = pts[si_]
                    for kk in range(nk):
                        kc = grp * 8 + kk
                        last = (kc == NFF - 1)
                        E(PE, lambda h: h.matmul(pt[0:rows, :], actb[:, kc, s * 128:s * 128 + rows], wt[:, kk, :],
                                                 start=(kc == 0), stop=last), R=[b_act[kc], wb], W=[pb],
                          inc=(kk == nk - 1))
            for si_, (s, rows) in enumerate(subs):
                pt, pb = pts[si_]
                hsl = htok[0:rows, hs, s, half * 512:(half + 1) * 512]
                E(DVE, lambda h: h.tensor_tensor(out=hsl, in0=hsl, in1=pt[0:rows, :], op=ALU.add), R=[pb],
                  W=[b_htok[hs][s]])

        chk("ph8")
        if not is_meta:
            x0 = (ti - 1) * T
            for (s, rows) in subs:
                si, sbuf_ = stat.next()
                st = stat.t[0:rows, si, :]
                hin = htok[0:rows, hs, s, :]
                E(ACT, lambda h: h.activation(out=junk[0:rows, :], in_=hin, func=AF.Square, accum_out=st[:, 0:1]),
                  R=[b_htok[hs][s]], W=[b_junk, sbuf_])
                E(ACT, lambda h: h.activation(out=st[:, 1:2], in_=st[:, 0:1], func=AF.Sqrt, bias=eps_c[0:rows],
                                              scale=1.0 / D), R=[b_const], W=[sbuf_])
                E(DVE, lambda h: h.reciprocal(out=st[:, 2:3], in_=st[:, 1:2]), W=[sbuf_])
                E(DVE, lambda h: h.scalar_tensor_tensor(out=hin, in0=hin, scalar=st[:, 2:3],
                                                        in1=tokp[0:rows, TP_NFW:TP_NFW + D], op0=ALU.mult, op1=ALU.mult),
                  R=[sbuf_, b_const], W=[b_htok[hs][s]])
                cx.dma(ACT, out_d[x0 + s * 128:x0 + s * 128 + rows, :], hin, R=[b_htok[hs][s]])

    try:
        chk("setup")
        do_tile(0, True)
        chk("meta")
        for ti in range(1, ntile + 1):
            do_tile(ti, False)
    except _Stop:
        pass
    cx.finish()
    es.close()
    return nc, cx


def _pack_weights(w_in, w_conf_out, w_dn_out, w_out, w_up, w_down):
    blocks = np.zeros((NBLK, 128, 8, 512), np.float32)

    def colblock(W, c0s):
        W3 = W.reshape(8, 128, W.shape[1])
        out = np.empty((128, 8, 512), np.float32)
        for g, c0 in enumerate(c0s):
            out[:, :, g * 128:(g + 1) * 128] = W3[:, :, c0:c0 + 128].transpose(1, 0, 2)
        return out

    b = 0
    for jp in range(4):
        j0, j1 = 2 * jp, 2 * jp + 1
        blocks[b] = colblock(w_in, [1024 + j0 * 128, j0 * 128, 1024 + j1 * 128, j1 * 128]); b += 1
    for base in (3072, 2048, 4096, 5120):
        for hb in range(2):
            blocks[b] = colblock(w_in, [base + (hb * 4 + g) * 128 for g in range(4)]); b += 1
    for base in (6160, 7184):
        for hb in range(2):
            blocks[b] = colblock(w_in, [base + (hb * 4 + g) * 128 for g in range(4)]); b += 1
    for W in (w_conf_out, w_dn_out):
        for hb in range(2):
            blocks[b] = colblock(W, [(hb * 4 + g) * 128 for g in range(4)]); b += 1
    for half in range(2):
        blocks[b] = w_out.reshape(8, 128, 1024)[:, :, half * 512:(half + 1) * 512].transpose(1, 0, 2); b += 1
    for jp in range(11):
        j0, j1 = 2 * jp, 2 * jp + 1
        blocks[b] = colblock(w_up, [j0 * 128, DFF + j0 * 128, j1 * 128, DFF + j1 * 128]); b += 1
    Wd = w_down.reshape(NFF, 128, 1024)
    for half in range(2):
        for grp in range(3):
            nk = 8 if grp < 2 else NFF - 16
            blocks[b][:, 0:nk, :] = Wd[grp * 8:grp * 8 + nk, :, half * 512:(half + 1) * 512].transpose(1, 0, 2); b += 1
    assert b == NBLK
    return blocks.reshape(NBLK, 128, 8 * 512)


def _pack_params(inp):
    par = np.zeros((128, NPAR), np.float32)

    def cols(v):
        return v.reshape(-1, 128).T

    par[:, P_MIXW:P_MIXW + 8] = cols(inp["norm_mix_w"][0])
    par[:, P_BGATE:P_BGATE + 16] = cols(inp["b_gate"][0])
    cdw = inp["conf_dw_w"][0]
    par[:, P_CDW:P_CDW + 8 * CK] = cdw.reshape(CK, 8, 128).transpose(2, 1, 0).reshape(128, 8 * CK)
    par[:, P_CDB:P_CDB + 8] = cols(inp["conf_dw_b"][0])
    par[:, P_LNW:P_LNW + 8] = cols(inp["conf_ln_w"][0])
    par[:, P_LNB:P_LNB + 8] = cols(inp["conf_ln_b"][0])
    dnc = inp["dn_conv_w"][0]
    par[:, P_DNC:P_DNC + 96] = dnc.reshape(4, 24, 128).transpose(2, 1, 0).reshape(128, 96)
    par[:, P_DNW] = inp["dn_norm_w"][0]
    par[:, P_FFW:P_FFW + 8] = cols(inp["norm_ffn_w"][0])
    fdw = inp["ffn_dw_w"][0]
    par[:, P_FDW:P_FDW + 132] = fdw.reshape(3, 44, 128).transpose(2, 1, 0).reshape(128, 132)
    par[:, P_FDB:P_FDB + 44] = cols(inp["ffn_dw_b"][0])
    tokp = np.zeros((128, NTOKP), np.float32)
    tokp[:, TP_DTB:TP_DTB + 8] = inp["dn_dt_bias"][0][None, :]
    tokp[:, TP_ALOG:TP_ALOG + 8] = inp["dn_A_log"][0][None, :]
    tokp[:, TP_NFW:] = inp["norm_final_w"][None, :]
    return par, tokp


_CACHE = {}


def kernel(**inputs):
    inp = {k: np.asarray(v, dtype=np.float32) for k, v in inputs.items()}
    x = inp["x"]
    B, n_x, _ = x.shape
    wpack = _pack_weights(inp["w_in"][0], inp["w_conf_out"][0], inp["w_dn_out"][0], inp["w_out"][0], inp["w_up"][0],
                          inp["w_down"][0])
    wab = np.ascontiguousarray(inp["w_in"][0][:, 6144:6160].reshape(8, 128, 16).transpose(1, 0, 2)).reshape(128, 128)
    par, tokp = _pack_params(inp)
    if n_x not in _CACHE:
        _CACHE[n_x] = build(n_x)[0]
    nc = _CACHE[n_x]
    in_maps = []
    for b in range(B):
        in_maps.append({"x": np.ascontiguousarray(x[b]), "meta": inp["meta_tokens"], "wpack": wpack, "wab": wab,
                        "params": par, "tokpar": tokp})
    res = run_bass_kernel_spmd(nc, in_maps, core_ids=list(range(B)))
    return np.stack([np.asarray(r["out"], dtype=np.float32) for r in res.results], axis=0)
```

```python
from contextlib import ExitStack

import numpy as np
import concourse.bass as bass
import concourse.mybir as mybir
from concourse.bass_utils import run_bass_kernel_spmd

F32 = mybir.dt.float32
BF16 = mybir.dt.bfloat16
ALU = mybir.AluOpType
AF = mybir.ActivationFunctionType

D = 1024
NMETA = 16
CK = 31
DFF = 2816
NFF = DFF // 128
EPS = 1e-6
T = 256
NBLK = 39
NEG = -30000.0

P_MIXW = 0
P_BGATE = P_MIXW + 8
P_CDW = P_BGATE + 16
P_CDB = P_CDW + 8 * CK
P_LNW = P_CDB + 8
P_LNB = P_LNW + 8
P_DNC = P_LNB + 8
P_DNW = P_DNC + 24 * 4
P_FFW = P_DNW + 1
P_FDW = P_FFW + 8
P_FDB = P_FDW + 44 * 3
P_FDWB = P_FDB + 44
NPAR = P_FDWB + 132
NDG = 17
TP_DTB = 0
TP_ALOG = 8
TP_NFW = 16
NTOKP = 16 + D


class Eng:
    def __init__(self, name, h, sem, inc):
        self.name = name
        self.h = h
        self.sem = sem
        self.inc = inc
        self.count = 0
        self.known = {}


class Buf:
    __slots__ = ("name", "w", "r")

    def __init__(self, name):
        self.name = name
        self.w = None
        self.r = {}


class Ctx:
    def __init__(self, nc, es, ndma=24, nsw=40):
        self.nc = nc
        self.es = es
        mk = lambda n: es.enter_context(nc.semaphore(n))
        self.PE = Eng("PE", nc.tensor, mk("s_pe"), 1)
        self.ACT = Eng("ACT", nc.scalar, mk("s_act"), 1)
        self.DVE = Eng("DVE", nc.vector, mk("s_dve"), 1)
        self.POOL = Eng("POOL", nc.gpsimd, mk("s_pool"), 1)
        self.SP = Eng("SP", nc.sync, None, 0)
        self.dsems = [Eng("D%d" % i, None, mk("s_d%d" % i), 16) for i in range(ndma)]
        self.dsems_sw = [Eng("W%d" % i, None, mk("s_w%d" % i), 16) for i in range(nsw)]
        self.dnext = 0
        self.dnext_sw = 0
        self.ninstr = 0

    def _deps(self, R, W):
        d = {}
        for b in R:
            if b.w is not None:
                e, i = b.w
                if d.get(e, 0) < i:
                    d[e] = i
        for b in W:
            if b.w is not None:
                e, i = b.w
                if d.get(e, 0) < i:
                    d[e] = i
            for e, i in b.r.items():
                if d.get(e, 0) < i:
                    d[e] = i
        return d

    def _waits(self, eng, d):
        for src, idx in d.items():
            if src is eng and eng is self.PE:
                continue
            if eng.known.get(src, 0) >= idx:
                continue
            assert idx <= src.count, "dependency on un-signalled instruction of %s" % src.name
            eng.h.wait_ge(src.sem, idx * src.inc)
            eng.known[src] = idx

    def _mark(self, p, R, W):
        e, i = p
        for b in R:
            if b.r.get(e, 0) < i:
                b.r[e] = i
        for b in W:
            b.w = p
            b.r = {}

    def emit(self, eng, fn, R=(), W=(), inc=True):
        self._waits(eng, self._deps(R, W))
        ins = fn(eng.h)
        self.ninstr += 1
        if inc:
            eng.count += 1
            ins.then_inc(eng.sem, 1)
            idx = eng.count
        else:
            idx = eng.count + 1
        self._mark((eng, idx), R, W)
        return ins

    def dma(self, q, out, in_, R=(), W=()):
        self._waits(q, self._deps(R, W))
        if q is self.POOL:
            ds = self.dsems_sw[self.dnext_sw]
            self.dnext_sw = (self.dnext_sw + 1) % len(self.dsems_sw)
        else:
            ds = self.dsems[self.dnext]
            self.dnext = (self.dnext + 1) % len(self.dsems)
        if q.known.get(ds, 0) < ds.count:
            q.h.wait_ge(ds.sem, ds.count * 16)
            q.known[ds] = ds.count
        q.h.dma_start(out=out, in_=in_).then_inc(ds.sem, 16)
        ds.count += 1
        self.ninstr += 1
        self._mark((ds, ds.count), R, W)

    def finish(self):
        for ds in self.dsems + self.dsems_sw:
            if ds.count and self.SP.known.get(ds, 0) < ds.count:
                self.SP.h.wait_ge(ds.sem, ds.count * 16)


class Rot:
    def __init__(self, name, tens, n):
        self.t = tens
        self.n = n
        self.bufs = [Buf("%s%d" % (name, i)) for i in range(n)]
        self.i = -1

    def next(self):
        self.i = (self.i + 1) % self.n
        return self.i, self.bufs[self.i]


class _Stop(Exception):
    pass


def build(n_x=4096, dbg=False, stop_after=None):
    nc = bass.Bass("TRN2", target_bir_lowering=False)
    es = ExitStack()
    cx = Ctx(nc, es)
    PE, ACT, DVE, POOL, SP = cx.PE, cx.ACT, cx.DVE, cx.POOL, cx.SP
    E = cx.emit
    NS = T // 128
    NCK = T // 64
    ntile = n_x // T

    x_d = nc.dram_tensor("x", [n_x, D], F32, kind="ExternalInput").ap()
    meta_d = nc.dram_tensor("meta", [NMETA, D], F32, kind="ExternalInput").ap()
    wpack_d = nc.dram_tensor("wpack", [NBLK, 128, 8 * 512], F32, kind="ExternalInput").ap()
    wab_d = nc.dram_tensor("wab", [128, 8 * 16], F32, kind="ExternalInput").ap()
    par_d = nc.dram_tensor("params", [128, NPAR], F32, kind="ExternalInput").ap()
    tokp_d = nc.dram_tensor("tokpar", [128, NTOKP], F32, kind="ExternalInput").ap()
    out_d = nc.dram_tensor("out", [n_x, D], F32, kind="ExternalOutput").ap()
    wbf_d = nc.dram_tensor("wbf", [NBLK, 128, 8 * 512], BF16, kind="Internal").ap()
    dg_d = nc.dram_tensor("dgd", [NDG, 128, 8 * 512], BF16, kind="Internal").ap()
    dbg_d = {}

    def sb(name, shape, dt):
        return es.enter_context(nc.sbuf_tensor(name, shape, dt))

    htok = sb("htok", [128, 2, NS, D], F32)
    b_htok = [[Buf("htok%d_%d" % (i, s)) for s in range(NS)] for i in range(2)]
    xn = Rot("xn", sb("xn", [128, 1, D], F32), 1)
    junk = sb("junk", [128, D], BF16)
    b_junk = Buf("junk")
    stat = Rot("stat", sb("stat", [128, 8, 4], F32), 8)
    uT = sb("uT", [128, 8, T], BF16)
    b_uT = [Buf("uT%d" % c) for c in range(8)]
    cbuf = sb("cbuf", [128, 8, 30 + T], BF16)
    b_c = [Buf("c%d" % c) for c in range(8)]
    ctmp = sb("ctmp", [128, 32], BF16)
    b_ctmp = Buf("ctmp")
    ybuf = sb("ybuf", [128, 8, T], F32)
    b_y = [Buf("y%d" % c) for c in range(8)]
    accA = Rot("accA", sb("accA", [128, 2, T], F32), 2)
    ybf = Rot("ybf", sb("ybf", [128, 2, T], BF16), 2)
    ysq = Rot("ysq", sb("ysq", [128, 2, T], BF16), 2)
    lnt = sb("lnt", [128, 5, T], F32)
    b_lnt = [Buf("lnt%d" % i) for i in range(5)]
    cact = sb("cact", [128, 8, T], BF16)
    b_cact = [Buf("cact%d" % c) for c in range(8)]
    sig = Rot("sig", sb("sig", [128, 2, T], F32), 2)
    pre = Rot("pre", sb("pre", [128, 3, 3 + T], BF16), 3)
    halq = sb("halq", [128, 24, 3], BF16)
    b_halq = [Buf("halq%d" % c) for c in range(24)]
    s32 = Rot("s32", sb("s32", [128, 2, T], F32), 2)
    sqb = Rot("sqb", sb("sqb", [128, 2, T], BF16), 2)
    rr = Rot("rr", sb("rr", [128, 2, T], F32), 2)
    qT = sb("qT", [128, 8, T], BF16)
    kT = sb("kT", [128, 8, T], BF16)
    vT = sb("vT", [128, 8, T], BF16)
    qdT = sb("qdT", [128, 8, T], BF16)
    b_qT = [Buf("qT%d" % c) for c in range(8)]
    b_kT = [Buf("kT%d" % c) for c in range(8)]
    b_vT = [Buf("vT%d" % c) for c in range(8)]
    b_qdT = [Buf("qdT%d" % c) for c in range(NCK)]
    sz = sb("sz", [128, 8, T], BF16)
    b_sz = [Buf("sz%d" % c) for c in range(8)]
    gts = sb("gts", [128, 16, T], BF16)
    b_gts = [Buf("gts%d" % c) for c in range(16)]
    oT = sb("oT", [128, 8, T], F32)
    b_oT = [Buf("oT%d" % c) for c in range(NCK)]
    od = sb("od", [128, 8, T], BF16)
    b_od = [Buf("od%d" % c) for c in range(8)]
    mT = sb("mT", [128, 8, T], BF16)
    b_mT = [Buf("mT%d" % c) for c in range(8)]
    mtmp = Rot("mtmp", sb("mtmp", [128, 2, T], F32), 2)
    actb = sb("actb", [128, NFF, T], BF16)
    b_act = [Buf("act%d" % c) for c in range(NFF)]
    preu = Rot("preu", sb("preu", [128, 4, 2 + T], BF16), 4)
    halu = sb("halu", [128, 44, 2], BF16)
    b_halu = [Buf("halu%d" % c) for c in range(44)]
    sg = Rot("sg", sb("sg", [128, 2, T], F32), 2)
    abt = sb("abt", [64, NCK, 16], F32)
    b_abt = Buf("abt")
    gtok = sb("gtok", [64, NCK, 8], F32)
    beta = sb("beta", [64, NCK, 8], F32)
    b_g = Buf("g")
    b_beta = Buf("beta")
    GU = Rot("GU", sb("GU", [64, 1, 512], F32), 1)
    gcol = Rot("gcol", sb("gcol", [64, 2, 32], F32), 2)
    D1 = Rot("D1", sb("D1", [64, 1, 512], F32), 1)
    D2 = Rot("D2", sb("D2", [64, 1, 512], F32), 1)
    Abf = Rot("Abf", sb("Abf", [64, 3, 512], BF16), 3)
    Mbf = Rot("Mbf", sb("Mbf", [64, 3, 512], BF16), 3)
    Pbf = Rot("Pbf", sb("Pbf", [64, 2, 512], BF16), 2)
    Pf = Rot("Pf", sb("Pf", [64, 1, 512], F32), 1)
    AqkT = Rot("AqkT", sb("AqkT", [64, 1, 512], BF16), 1)
    TTb = Rot("TTb", sb("TTb", [64, 1, 512], BF16), 1)
    kw = Rot("kw", sb("kw", [64, 1, 1024], BF16), 1)
    kd = Rot("kd", sb("kd", [64, 1, 1024], BF16), 1)
    vb = Rot("vb", sb("vb", [64, 1, 1024], BF16), 1)
    vn = Rot("vn", sb("vn", [64, 1, 1024], BF16), 1)
    wTn = Rot("wTn", sb("wTn", [128, 1, 512], BF16), 1)
    Eq = Rot("Eq", sb("Eq", [128, 1, 512], F32), 1)
    S = sb("S", [128, 8, 128], F32)
    Sbf = sb("Sbf", [128, 8, 128], BF16)
    b_S = [Buf("S%d" % h) for h in range(8)]
    b_Sbf = [Buf("Sbf%d" % h) for h in range(8)]
    NRING = 4
    wring = Rot("wring", sb("wring", [128, NRING, 8 * 512], BF16), NRING)
    wab = sb("wab_sb", [128, 8, 16], BF16)
    wabf = sb("wabf", [128, 8 * 16], F32)
    b_wab = Buf("wab")
    ident_f = sb("ident_f", [128, 128], F32)
    ident_b = sb("ident_b", [128, 128], BF16)
    ones_b = sb("ones_b", [128, 128], BF16)
    ones_f = sb("ones_f", [64, 128], F32)
    Umat = sb("Umat", [64, 64], F32)
    par = sb("par_sb", [128, NPAR], F32)
    tokp = sb("tokp", [128, NTOKP], F32)
    cst = sb("cst", [128, 4], F32)
    b_const = Buf("const")
    b_wbf = [Buf("wbf%d" % i) for i in range(NBLK)]
    b_dg = [Buf("dg%d" % i) for i in range(NDG)]

    ps = es.enter_context(nc.psum_tensor("ps", [128, 8, 512], F32))
    b_ps = [Buf("ps%d" % i) for i in range(8)]
    ps_state = {"i": -1}

    def ps_next():
        ps_state["i"] = (ps_state["i"] + 1) % 6
        i = ps_state["i"]
        return ps[:, i, :], b_ps[i]

    cx.dma(SP, par[:, :], par_d[:, :], W=[b_const])
    cx.dma(SP, tokp[:, :], tokp_d[:, :], W=[b_const])
    cx.dma(SP, wabf[:, :], wab_d[:, :], W=[b_wab])
    for b in range(NBLK):
        cx.dma(POOL, wbf_d[b], wpack_d[b], W=[b_wbf[b]])

    def pool_c(fn):
        E(POOL, fn, W=[b_const])

    pool_c(lambda h: h.memset(ident_f[:], 0.0))
    pool_c(lambda h: h.affine_select(out=ident_f[:], in_=ident_f[:], pattern=[[-1, 128]],
                                     compare_op=ALU.not_equal, fill=1.0, base=0, channel_multiplier=1))
    pool_c(lambda h: h.tensor_copy(out=ident_b[:], in_=ident_f[:]))
    pool_c(lambda h: h.memset(ones_b[:], 1.0))
    pool_c(lambda h: h.memset(ones_f[:], 1.0))
    pool_c(lambda h: h.memset(Umat[:], 1.0))
    pool_c(lambda h: h.affine_select(out=Umat[:], in_=Umat[:], pattern=[[1, 64]],
                                     compare_op=ALU.is_ge, fill=0.0, base=0, channel_multiplier=-1))
    pool_c(lambda h: h.memset(cst[:, 0:1], EPS))
    pool_c(lambda h: h.memset(cst[:, 1:2], 1.0))
    pool_c(lambda h: h.memset(cbuf[:], 0.0))
    pool_c(lambda h: h.memset(halq[:], 0.0))
    pool_c(lambda h: h.memset(halu[:], 0.0))
    pool_c(lambda h: h.memset(S[:], 0.0))
    pool_c(lambda h: h.memset(Sbf[:], 0.0))
    E(POOL, lambda h: h.tensor_copy(out=wab[:].rearrange("p a b -> p (a b)"), in_=wabf[:]), R=[], W=[b_wab])
    E(ACT, lambda h: h.activation(out=tokp[:, TP_ALOG:TP_ALOG + 8], in_=tokp[:, TP_ALOG:TP_ALOG + 8], func=AF.Exp),
      W=[b_const])
    E(DVE, lambda h: h.tensor_scalar(out=tokp[:, TP_ALOG:TP_ALOG + 8], in0=tokp[:, TP_ALOG:TP_ALOG + 8],
                                     scalar1=-1.0, scalar2=None, op0=ALU.mult), W=[b_const])
    all_state = b_c + b_halq + b_halu + b_S + b_Sbf
    for b in all_state:
        b.w = b_const.w

    neg_reg = nc.gpsimd.to_reg(NEG)
    def build_dg(d, poff, nm):
        i, wb_ = wring.next()
        slot = wring.t[:, i, 0:nm * 128].rearrange("p (m c) -> p m c", c=128)
        E(DVE, lambda h: h.tensor_tensor(out=slot, in0=ident_b[:, :].unsqueeze(1).to_broadcast([128, nm, 128]),
                                         in1=par[:, poff:poff + nm].unsqueeze(2).to_broadcast([128, nm, 128]), op=ALU.mult),
          R=[b_const], W=[wb_])
        cx.dma(SP, dg_d[d][:, 0:nm * 128], wring.t[:, i, 0:nm * 128], R=[wb_], W=[b_dg[d]])

    for j in range(8):
        build_dg(j, P_CDW + j * CK, CK)
    for kk_ in range(3):
        build_dg(8 + kk_, P_DNC + kk_ * 32, 32)
    for d_ in range(6):
        build_dg(11 + d_, P_FDWB + d_ * 24, min(24, 132 - d_ * 24))

    def dload(d):
        nm = CK if d < 8 else (32 if d < 11 else min(24, 132 - (d - 11) * 24))
        i, wb_ = wring.next()
        cx.dma(SP, wring.t[:, i, 0:nm * 128], dg_d[d][:, 0:nm * 128], R=[b_dg[d]], W=[wb_])
        return wring.t[:, i, :], wb_

    eps_c = cst[:, 0:1]
    one_c = cst[:, 1:2]

    def pcol(off, rows=128):
        return par[0:rows, off:off + 1]

    def wload(bidx):
        i, wb = wring.next()
        cx.dma(SP, wring.t[:, i, :], wbf_d[bidx], R=[b_wbf[bidx]], W=[wb])
        return wring.t[:, i, :].rearrange("p (k n) -> p k n", k=8), wb

    def norm_to_uT(hs, subs, nt, woff):
        xns = []
        for (s, rows) in subs:
            si, sbuf_ = stat.next()
            st = stat.t[0:rows, si, :]
            hin = htok[0:rows, hs, s, :]
            E(ACT, lambda h: h.activation(out=junk[0:rows, :], in_=hin, func=AF.Square, accum_out=st[:, 0:1]),
              R=[b_htok[hs][s]], W=[b_junk, sbuf_])
            E(ACT, lambda h: h.activation(out=st[:, 1:2], in_=st[:, 0:1], func=AF.Sqrt, bias=eps_c[0:rows], scale=1.0 / D),
              R=[b_const], W=[sbuf_])
            E(DVE, lambda h: h.reciprocal(out=st[:, 2:3], in_=st[:, 1:2]), W=[sbuf_])
            xi, xb = xn.next()
            xt = xn.t[0:rows, xi, :]
            E(DVE, lambda h: h.tensor_scalar(out=xt, in0=hin, scalar1=st[:, 2:3], scalar2=None, op0=ALU.mult),
              R=[b_htok[hs][s], sbuf_], W=[xb])
            for c in range(8):
                pt, pb = ps_next()
                E(PE, lambda h: h.transpose(pt[:, 0:rows], xt[:, c * 128:(c + 1) * 128], ident_f[0:rows, 0:rows]),
                  R=[xb, b_const], W=[pb])
                tgt = uT[:, c, s * 128:s * 128 + rows]
                if c % 2 == 0:
                    E(ACT, lambda h: h.activation(out=tgt, in_=pt[:, 0:rows], func=AF.Copy, scale=pcol(woff + c)),
                      R=[pb, b_const], W=[b_uT[c]])
                else:
                    E(DVE, lambda h: h.tensor_scalar(out=tgt, in0=pt[:, 0:rows], scalar1=pcol(woff + c), scalar2=None,
                                                     op0=ALU.mult), R=[pb, b_const], W=[b_uT[c]])

    def proj_chunk(wt, wb, g, nt, src, b_src, nk=8):
        pt, pb = ps_next()
        for kc in range(nk):
            E(PE, lambda h: h.matmul(pt[:, 0:nt], wt[:, kc, g * 128:(g + 1) * 128], src[:, kc, 0:nt],
                                     start=(kc == 0), stop=(kc == nk - 1)),
              R=[wb, b_src[kc]], W=[pb], inc=(kc == nk - 1))
        return pt[:, 0:nt], pb

    def l2_rstd(src_ap, b_src, nt, scale_in):
        qi, qb = sqb.next()
        sq = sqb.t[:, qi, 0:nt]
        E(ACT, lambda h: h.activation(out=sq, in_=src_ap, func=AF.Square), R=[b_src], W=[qb])
        pt, pb = ps_next()
        E(PE, lambda h: h.matmul(pt[:, 0:nt], ones_b[:, :], sq, start=True, stop=True), R=[qb, b_const], W=[pb])
        ri, rb = rr.next()
        r = rr.t[:, ri, 0:nt]
        E(ACT, lambda h: h.activation(out=r, in_=pt[:, 0:nt], func=AF.Sqrt, bias=eps_c, scale=scale_in),
          R=[pb, b_const], W=[rb])
        E(DVE, lambda h: h.reciprocal(out=r, in_=r), W=[rb])
        return r, rb

    def chk(name):
        if stop_after == name:
            raise _Stop()

    def do_tile(ti, is_meta):
        hs = ti % 2
        if is_meta:
            nt = NMETA
            subs = [(0, NMETA)]
            chunks = [(0, NMETA)]
            cx.dma(SP, htok[0:NMETA, hs, 0, :], meta_d[:, :], W=[b_htok[hs][0]])
        else:
            nt = T
            subs = [(s, 128) for s in range(NS)]
            chunks = [(c * 64, 64) for c in range(NCK)]
            x0 = (ti - 1) * T
            for (s, rows) in subs:
                cx.dma(SP, htok[:, hs, s, :], x_d[x0 + s * 128:x0 + (s + 1) * 128, :], W=[b_htok[hs][s]])

        norm_to_uT(hs, subs, nt, P_MIXW)

        chk("ph1")
        for jp in range(4):
            wt, wb = wload(jp)
            for jj in range(2):
                j = 2 * jp + jj
                pg, pgb = proj_chunk(wt, wb, 2 * jj, nt, uT, b_uT)
                gi, gb = sig.next()
                sg_ap = sig.t[:, gi, 0:nt]
                E(ACT, lambda h: h.activation(out=sg_ap, in_=pg, func=AF.Sigmoid), R=[pgb], W=[gb])
                pv, pvb = proj_chunk(wt, wb, 2 * jj + 1, nt, uT, b_uT)
                E(DVE, lambda h: h.tensor_tensor(out=cbuf[:, j, 30:30 + nt], in0=pv, in1=sg_ap, op=ALU.mult),
                  R=[pvb, gb], W=[b_c[j]])
        for j in range(8):
            dg, dgb = dload(j)
            pt, pb = ps_next()
            for t_ in range(CK):
                E(PE, lambda h: h.matmul(pt[:, 0:nt], dg[:, t_ * 128:(t_ + 1) * 128], cbuf[:, j, t_:t_ + nt],
                                         start=(t_ == 0), stop=(t_ == CK - 1)), R=[dgb, b_c[j]], W=[pb], inc=(t_ == CK - 1))
            E(ACT, lambda h: h.activation(out=ybuf[:, j, 0:nt], in_=pt[:, 0:nt], func=AF.Identity, bias=pcol(P_CDB + j)),
              R=[pb, b_const], W=[b_y[j]])
            E(POOL, lambda h: h.tensor_copy(out=ctmp[:, 0:30], in_=cbuf[:, j, nt:nt + 30]), R=[b_c[j]], W=[b_ctmp])
            E(POOL, lambda h: h.tensor_copy(out=cbuf[:, j, 0:30], in_=ctmp[:, 0:30]), R=[b_ctmp], W=[b_c[j]])

        dgq_state = {}

        def qkv_block(bidx, kind, h0):
            if h0 == 0:
                dgq_state[kind] = dload(8 + {"q": 0, "k": 1, "v": 2}[kind])
            dgq, dgqb = dgq_state[kind]
            wt, wb = wload(bidx)
            for g in range(4):
                hh = h0 + g
                ch = {"q": 0, "k": 8, "v": 16}[kind] + hh
                pp, ppb = proj_chunk(wt, wb, g, nt, uT, b_uT)
                pi, prb = pre.next()
                pr = pre.t[:, pi, :]
                E(POOL, lambda h: h.tensor_copy(out=pr[:, 0:3], in_=halq[:, ch, :]), R=[b_halq[ch]], W=[prb])
                if kind == "v":
                    E(ACT, lambda h: h.activation(out=pr[:, 3:3 + nt], in_=pp, func=AF.Copy), R=[ppb], W=[prb])
                else:
                    E(DVE, lambda h: h.tensor_copy(out=pr[:, 3:3 + nt], in_=pp), R=[ppb], W=[prb])
                E(POOL, lambda h: h.tensor_copy(out=halq[:, ch, :], in_=pr[:, nt:nt + 3]), R=[prb], W=[b_halq[ch]])
                pc4, cb = ps_next()
                ca = pc4[:, 0:nt]
                for t_ in range(4):
                    mi_ = hh * 4 + t_
                    E(PE, lambda h: h.matmul(ca, dgq[:, mi_ * 128:(mi_ + 1) * 128], pr[:, t_:t_ + nt], start=(t_ == 0),
                                             stop=(t_ == 3)), R=[dgqb, prb], W=[cb], inc=(t_ == 3))
                if kind == "v":
                    E(ACT, lambda h: h.activation(out=vT[:, hh, 0:nt], in_=ca, func=AF.Silu), R=[cb], W=[b_vT[hh]])
                else:
                    si_, sb_ = s32.next()
                    sa = s32.t[:, si_, 0:nt]
                    E(ACT, lambda h: h.activation(out=sa, in_=ca, func=AF.Silu), R=[cb], W=[sb_])
                    r, rb = l2_rstd(sa, sb_, nt, 1.0)
                    if kind == "k":
                        E(DVE, lambda h: h.tensor_tensor(out=kT[:, hh, 0:nt], in0=sa, in1=r, op=ALU.mult),
                          R=[sb_, rb], W=[b_kT[hh]])
                    else:
                        E(DVE, lambda h: h.scalar_tensor_tensor(out=qT[:, hh, 0:nt], in0=sa, scalar=128.0 ** -0.5, in1=r,
                                                                op0=ALU.mult, op1=ALU.mult), R=[sb_, rb], W=[b_qT[hh]])

        qkv_block(4, "k", 0)
        qkv_block(5, "k", 4)
        qkv_block(6, "q", 0)
        qkv_block(7, "q", 4)
        qkv_block(8, "v", 0)
        qkv_block(9, "v", 4)
        for bb_ in range(2):
            wt, wb = wload(10 + bb_)
            for g in range(4):
                hh = bb_ * 4 + g
                pp, ppb = proj_chunk(wt, wb, g, nt, uT, b_uT)
                E(ACT, lambda h: h.activation(out=sz[:, hh, 0:nt], in_=pp, func=AF.Silu), R=[ppb], W=[b_sz[hh]])
        for bb_ in range(4):
            wt, wb = wload(12 + bb_)
            for g in range(4):
                jg = bb_ * 4 + g
                pp, ppb = proj_chunk(wt, wb, g, nt, uT, b_uT)
                E(ACT, lambda h: h.activation(out=gts[:, jg, 0:nt], in_=pp, func=AF.Sigmoid, bias=pcol(P_BGATE + jg)),
                  R=[ppb, b_const], W=[b_gts[jg]])
        pab, pabb = ps_next()
        pab3 = pab.rearrange("p (c n) -> p c n", n=16)
        for ck, (o, cl) in enumerate(chunks):
            for kc in range(8):
                E(PE, lambda h: h.matmul(pab3[0:cl, ck, :], uT[:, kc, o:o + cl], wab[:, kc, :], start=(kc == 0),
                                         stop=(kc == 7)), R=[b_uT[kc], b_wab], W=[pabb], inc=(kc == 7))
        nck = len(chunks)
        cl0 = chunks[0][1]
        dtb_b = tokp[0:cl0, TP_DTB:TP_DTB + 8].unsqueeze(1).to_broadcast([cl0, nck, 8])
        negA_b = tokp[0:cl0, TP_ALOG:TP_ALOG + 8].unsqueeze(1).to_broadcast([cl0, nck, 8])
        E(DVE, lambda h: h.tensor_tensor(out=abt[0:cl0, 0:nck, 0:8], in0=pab3[0:cl0, 0:nck, 0:8], in1=dtb_b, op=ALU.add),
          R=[pabb, b_const], W=[b_abt])
        E(ACT, lambda h: h.activation(out=abt[0:cl0, 0:nck, 0:8], in_=abt[0:cl0, 0:nck, 0:8], func=AF.Exp), W=[b_abt])
        E(ACT, lambda h: h.activation(out=abt[0:cl0, 0:nck, 0:8], in_=abt[0:cl0, 0:nck, 0:8], func=AF.Ln,
                                      bias=one_c[0:cl0]), R=[b_const], W=[b_abt])
        E(DVE, lambda h: h.tensor_tensor(out=gtok[0:cl0, 0:nck, :], in0=abt[0:cl0, 0:nck, 0:8], in1=negA_b, op=ALU.mult),
          R=[b_abt, b_const], W=[b_g])
        E(ACT, lambda h: h.activation(out=beta[0:cl0, 0:nck, :], in_=pab3[0:cl0, 0:nck, 8:16], func=AF.Sigmoid),
          R=[pabb], W=[b_beta])

        chk("ph2")
        pm, pmb = ps_next()
        pq, pqb = ps_next()
        for j in range(8):
            yi, yb_ = ybf.next()
            qi, qb_ = ysq.next()
            E(ACT, lambda h: h.activation(out=ybf.t[:, yi, 0:nt], in_=ybuf[:, j, 0:nt], func=AF.Copy), R=[b_y[j]], W=[yb_])
            E(ACT, lambda h: h.activation(out=ysq.t[:, qi, 0:nt], in_=ybuf[:, j, 0:nt], func=AF.Square), R=[b_y[j]], W=[qb_])
            E(PE, lambda h: h.matmul(pm[:, 0:nt], ones_b[:, :], ybf.t[:, yi, 0:nt], start=(j == 0), stop=(j == 7)),
              R=[yb_, b_const], W=[pmb])
            E(PE, lambda h: h.matmul(pq[:, 0:nt], ones_b[:, :], ysq.t[:, qi, 0:nt], start=(j == 0), stop=(j == 7)),
              R=[qb_, b_const], W=[pqb])
        m_ = lnt[:, 0, 0:nt]
        msq = lnt[:, 1, 0:nt]
        var = lnt[:, 2, 0:nt]
        rstd = lnt[:, 3, 0:nt]
        mr = lnt[:, 4, 0:nt]
        E(DVE, lambda h: h.tensor_scalar(out=m_, in0=pm[:, 0:nt], scalar1=1.0 / D, scalar2=None, op0=ALU.mult),
          R=[pmb], W=[b_lnt[0]])
        E(DVE, lambda h: h.tensor_tensor(out=msq, in0=m_, in1=m_, op=ALU.mult), R=[b_lnt[0]], W=[b_lnt[1]])
        E(DVE, lambda h: h.scalar_tensor_tensor(out=var, in0=pq[:, 0:nt], scalar=1.0 / D, in1=msq, op0=ALU.mult,
                                                op1=ALU.subtract), R=[pqb, b_lnt[1]], W=[b_lnt[2]])
        E(ACT, lambda h: h.activation(out=rstd, in_=var, func=AF.Sqrt, bias=eps_c, scale=1.0), R=[b_lnt[2], b_const],
          W=[b_lnt[3]])
        E(DVE, lambda h: h.reciprocal(out=rstd, in_=rstd), W=[b_lnt[3]])
        E(DVE, lambda h: h.tensor_tensor(out=mr, in0=m_, in1=rstd, op=ALU.mult), R=[b_lnt[0], b_lnt[3]], W=[b_lnt[4]])
        for j in range(8):
            ai, ab = accA.next()
            aa = accA.t[:, ai, 0:nt]
            E(DVE, lambda h: h.tensor_tensor(out=aa, in0=ybuf[:, j, 0:nt], in1=rstd, op=ALU.mult), R=[b_y[j], b_lnt[3]],
              W=[ab])
            E(DVE, lambda h: h.tensor_tensor(out=aa, in0=aa, in1=mr, op=ALU.subtract), R=[b_lnt[4]], W=[ab])
            E(ACT, lambda h: h.activation(out=cact[:, j, 0:nt], in_=aa, func=AF.Silu, bias=pcol(P_LNB + j),
                                          scale=pcol(P_LNW + j)), R=[ab, b_const], W=[b_cact[j]])

        chk("ph3")
        for ck, (o, cl) in enumerate(chunks):
            nlev = 5 if cl == 64 else 3
            W8 = 8 * cl
            gi, gub = GU.next()
            gu3 = GU.t[0:cl, gi, 0:W8].rearrange("p (h j) -> p h j", h=8)
            gsl = gtok[0:cl, ck, :]
            E(DVE, lambda h: h.tensor_tensor(out=gu3, in0=gsl.unsqueeze(2).to_broadcast([cl, 8, cl]),
                                             in1=Umat[0:cl, 0:cl].unsqueeze(1).to_broadcast([cl, 8, cl]), op=ALU.mult),
              R=[b_g, b_const], W=[gub])
            bps = ps[:, 6, 0:W8]
            bps3 = bps.rearrange("p (h j) -> p h j", h=8)
            E(PE, lambda h: h.matmul(bps, ones_f[0:cl, :], GU.t[0:cl, gi, 0:W8], start=True, stop=True),
              R=[gub, b_const], W=[b_ps[6]])
            pc, pcb = ps_next()
            E(PE, lambda h: h.matmul(pc[0:cl, 0:8], Umat[0:cl, 0:cl], gsl, start=True, stop=True), R=[b_g, b_const], W=[pcb])
            ci_, gcb = gcol.next()
            gc = gcol.t[0:cl, ci_, :]
            E(ACT, lambda h: h.activation(out=gc[:, 0:8], in_=pc[0:cl, 0:8], func=AF.Copy), R=[pcb], W=[gcb])
            E(ACT, lambda h: h.activation(out=gc[:, 8:16], in_=pc[0:cl, 0:8], func=AF.Exp), R=[pcb], W=[gcb])
            E(DVE, lambda h: h.tensor_tensor(out=gc[:, 16:24], in0=gc[:, 8:16], in1=beta[0:cl, ck, :], op=ALU.mult),
              R=[b_beta], W=[gcb])
            E(DVE, lambda h: h.tensor_tensor(out=gc[:, 24:32], in0=bps3[0:cl, :, cl - 1], in1=gc[:, 0:8], op=ALU.subtract),
              R=[b_ps[6]], W=[gcb])
            E(ACT, lambda h: h.activation(out=gc[:, 24:32], in_=gc[:, 24:32], func=AF.Exp), W=[gcb])
            d1i, d1b = D1.next()
            d2i, d2b = D2.next()
            d1 = D1.t[0:cl, d1i, 0:W8].rearrange("p (h j) -> p h j", h=8)
            d2 = D2.t[0:cl, d2i, 0:W8].rearrange("p (h j) -> p h j", h=8)
            gcb3 = gc[:, 0:8].unsqueeze(2).to_broadcast([cl, 8, cl])
            E(DVE, lambda h: h.tensor_tensor(out=d1, in0=gcb3, in1=bps3[0:cl], op=ALU.subtract), R=[gcb, b_ps[6]], W=[d1b])
            E(DVE, lambda h: h.tensor_tensor(out=d2, in0=bps3[0:cl], in1=gcb3, op=ALU.subtract), R=[gcb, b_ps[6]], W=[d2b])
            E(POOL, lambda h: h.affine_select(out=d1, in_=d1, pattern=[[0, 8], [-1, cl]], compare_op=ALU.is_gt, fill=neg_reg,
                                              base=0, channel_multiplier=1), W=[d1b])
            E(POOL, lambda h: h.affine_select(out=d2, in_=d2, pattern=[[0, 8], [1, cl]], compare_op=ALU.is_ge, fill=neg_reg,
                                              base=0, channel_multiplier=-1), W=[d2b])
            E(ACT, lambda h: h.activation(out=d1, in_=d1, func=AF.Exp), W=[d1b])
            E(ACT, lambda h: h.activation(out=d2, in_=d2, func=AF.Exp), W=[d2b])
            E(DVE, lambda h: h.tensor_tensor(out=d1, in0=d1, in1=beta[0:cl, ck, :].unsqueeze(2).to_broadcast([cl, 8, cl]),
                                              op=ALU.mult), R=[b_beta], W=[d1b])
            chk("d1")
            pkk, pkkb = ps_next()
            pqk, pqkb = ps_next()
            for hh in range(8):
                E(PE, lambda h: h.matmul(pkk[0:cl, hh * cl:(hh + 1) * cl], kT[:, hh, o:o + cl], kT[:, hh, o:o + cl],
                                         start=True, stop=True), R=[b_kT[hh]], W=[pkkb], inc=(hh == 7))
            for hh in range(8):
                E(PE, lambda h: h.matmul(pqk[0:cl, hh * cl:(hh + 1) * cl], kT[:, hh, o:o + cl], qT[:, hh, o:o + cl],
                                         start=True, stop=True), R=[b_kT[hh], b_qT[hh]], W=[pqkb], inc=(hh == 7))
            a_i, a_b = Abf.next()
            A0 = Abf.t[0:cl, a_i, 0:W8]
            E(DVE, lambda h: h.tensor_tensor(out=A0, in0=pkk[0:cl, 0:W8], in1=D1.t[0:cl, d1i, 0:W8], op=ALU.mult),
              R=[pkkb, d1b], W=[a_b])
            q_i, q_b = AqkT.next()
            AQ = AqkT.t[0:cl, q_i, 0:W8]
            E(DVE, lambda h: h.tensor_tensor(out=AQ, in0=pqk[0:cl, 0:W8], in1=D2.t[0:cl, d2i, 0:W8], op=ALU.mult),
              R=[pqkb, d2b], W=[q_b])
            chk("d2")
            pmt, pmtb = ps_next()
            for hh in range(8):
                E(PE, lambda h: h.matmul(pmt[0:cl, hh * cl:(hh + 1) * cl], A0[:, hh * cl:(hh + 1) * cl], ident_b[0:cl, 0:cl],
                                         start=True, stop=True), R=[a_b, b_const], W=[pmtb], inc=(hh == 7))
            m_i, m_b = Mbf.next()
            M0 = Mbf.t[0:cl, m_i, 0:W8]
            E(ACT, lambda h: h.activation(out=M0, in_=pmt[0:cl, 0:W8], func=AF.Copy), R=[pmtb], W=[m_b])
            chk("d3")
            pf_i, pf_b = Pf.next()
            PF = Pf.t[0:cl, pf_i, 0:W8]
            E(DVE, lambda h: h.tensor_tensor(out=PF.rearrange("p (h j) -> p h j", h=8),
                                             in0=ident_f[0:cl, 0:cl].unsqueeze(1).to_broadcast([cl, 8, cl]),
                                             in1=M0.rearrange("p (h j) -> p h j", h=8), op=ALU.subtract),
              R=[m_b, b_const], W=[pf_b])
            p_i, p_b = Pbf.next()
            Pk = Pbf.t[0:cl, p_i, 0:W8]
            E(ACT, lambda h: h.activation(out=Pk, in_=PF, func=AF.Copy), R=[pf_b], W=[p_b])
            chk("e0")
            Ak, Ak_b, Mk, Mk_b = A0, a_b, M0, m_b
            for lev in range(1, nlev + 1):
                last = (lev == nlev)
                if lev == 2:
                    chk("e2")
                pa, pab_ = ps_next()
                for hh in range(8):
                    sl = slice(hh * cl, (hh + 1) * cl)
                    E(PE, lambda h: h.matmul(pa[0:cl, sl], Mk[:, sl], Ak[:, sl], start=True, stop=True),
                      R=[Mk_b, Ak_b], W=[pab_], inc=(hh == 7))
                if not last:
                    pm2, pm2b = ps_next()
                    for hh in range(8):
                        sl = slice(hh * cl, (hh + 1) * cl)
                        E(PE, lambda h: h.matmul(pm2[0:cl, sl], Ak[:, sl], Mk[:, sl], start=True, stop=True),
                          R=[Mk_b, Ak_b], W=[pm2b], inc=(hh == 7))
                chk("e1")
                na_i, na_b = Abf.next()
                An = Abf.t[0:cl, na_i, 0:W8]
                E(ACT, lambda h: h.activation(out=An, in_=pa[0:cl, 0:W8], func=AF.Copy), R=[pab_], W=[na_b])
                if not last:
                    nm_i, nm_b = Mbf.next()
                    Mn = Mbf.t[0:cl, nm_i, 0:W8]
                    E(DVE, lambda h: h.tensor_copy(out=Mn, in_=pm2[0:cl, 0:W8]), R=[pm2b], W=[nm_b])
                pp_, ppb_ = ps_next()
                for hh in range(8):
                    sl = slice(hh * cl, (hh + 1) * cl)
                    E(PE, lambda h: h.matmul(pp_[0:cl, sl], An[:, sl], Pk[:, sl], start=True, stop=True),
                      R=[na_b, p_b], W=[ppb_], inc=(hh == 7))
                E(DVE, lambda h: h.tensor_tensor(out=PF, in0=PF, in1=pp_[0:cl, 0:W8], op=ALU.add), R=[ppb_], W=[pf_b])
                if not last:
                    np_i, np_b = Pbf.next()
                    Pn = Pbf.t[0:cl, np_i, 0:W8]
                    E(ACT, lambda h: h.activation(out=Pn, in_=PF, func=AF.Copy), R=[pf_b], W=[np_b])
                    Pk, p_b = Pn, np_b
                    Mk, Mk_b = Mn, nm_b
                Ak, Ak_b = An, na_b
            t_i, t_b = TTb.next()
            TT = TTb.t[0:cl, t_i, 0:W8]
            E(ACT, lambda h: h.activation(out=TT, in_=PF, func=AF.Copy), R=[pf_b], W=[t_b])
            chk("d4")
            kw_i, kw_b = kw.next()
            kd_i, kd_b = kd.next()
            vb_i, vb_b = vb.next()

            def sc(col0, g0):
                return gc[:, col0 + g0:col0 + g0 + 4].unsqueeze(2).to_broadcast([cl, 4, 128])

            for grp in range(2):
                pkt, pktb = ps_next()
                pvt, pvtb = ps_next()
                for g in range(4):
                    hh = grp * 4 + g
                    E(PE, lambda h: h.matmul(pkt[0:cl, g * 128:(g + 1) * 128], kT[:, hh, o:o + cl], ident_b[:, :],
                                             start=True, stop=True), R=[b_kT[hh], b_const], W=[pktb], inc=(g == 3))
                for g in range(4):
                    hh = grp * 4 + g
                    E(PE, lambda h: h.matmul(pvt[0:cl, g * 128:(g + 1) * 128], vT[:, hh, o:o + cl], ident_b[:, :],
                                             start=True, stop=True), R=[b_vT[hh], b_const], W=[pvtb], inc=(g == 3))
                k3 = pkt[0:cl, :].rearrange("p (h d) -> p h d", h=4)
                v3 = pvt[0:cl, :].rearrange("p (h d) -> p h d", h=4)
                csl = slice(grp * 512, (grp + 1) * 512)
                E(DVE, lambda h: h.tensor_tensor(out=kw.t[0:cl, kw_i, csl].rearrange("p (h d) -> p h d", h=4), in0=k3,
                                                 in1=sc(16, grp * 4), op=ALU.mult), R=[pktb, gcb], W=[kw_b])
                E(DVE, lambda h: h.tensor_tensor(out=kd.t[0:cl, kd_i, csl].rearrange("p (h d) -> p h d", h=4), in0=k3,
                                                 in1=sc(24, grp * 4), op=ALU.mult), R=[pktb, gcb], W=[kd_b])
                E(DVE, lambda h: h.tensor_tensor(out=vb.t[0:cl, vb_i, csl].rearrange("p (h d) -> p h d", h=4), in0=v3,
                                                 in1=beta[0:cl, ck, grp * 4:grp * 4 + 4].unsqueeze(2).to_broadcast([cl, 4, 128]),
                                                 op=ALU.mult), R=[pvtb, b_beta], W=[vb_b])
            chk("d5")
            pw, pwb = ps_next()
            for hh in range(8):
                E(PE, lambda h: h.matmul(pw[:, hh * cl:(hh + 1) * cl], kw.t[0:cl, kw_i, hh * 128:(hh + 1) * 128],
                                         TT[:, hh * cl:(hh + 1) * cl], start=True, stop=True), R=[kw_b, t_b], W=[pwb],
                  inc=(hh == 7))
            w_i, w_b = wTn.next()
            WT = wTn.t[:, w_i, 0:W8]
            E(ACT, lambda h: h.activation(out=WT, in_=pw[:, 0:W8], func=AF.Copy, scale=-1.0), R=[pwb], W=[w_b])
            e_i, e_b = Eq.next()
            EQ = Eq.t[:, e_i, 0:W8]
            E(ACT, lambda h: h.activation(out=EQ, in_=bps, func=AF.Exp), R=[b_ps[6]], W=[e_b])
            EQ3 = EQ.rearrange("p (h j) -> p h j", h=8)
            E(DVE, lambda h: h.tensor_tensor(out=qdT[:, :, o:o + cl], in0=qT[:, :, o:o + cl], in1=EQ3, op=ALU.mult),
              R=b_qT + [e_b], W=[b_qdT[ck]])
            chk("d6")
            vn_i, vn_b = vn.next()
            VN = vn.t[0:cl, vn_i, :]
            for grp in range(2):
                pv_, pvb_ = ps_next()
                for g in range(4):
                    hh = grp * 4 + g
                    E(PE, lambda h: h.matmul(pv_[0:cl, g * 128:(g + 1) * 128], TT[:, hh * cl:(hh + 1) * cl],
                                             vb.t[0:cl, vb_i, hh * 128:(hh + 1) * 128], start=True, stop=False),
                      R=[t_b, vb_b], W=[pvb_], inc=False)
                    E(PE, lambda h: h.matmul(pv_[0:cl, g * 128:(g + 1) * 128], WT[:, hh * cl:(hh + 1) * cl],
                                             Sbf[:, hh, :], start=False, stop=True), R=[w_b, b_Sbf[hh]], W=[pvb_],
                      inc=(g == 3))
                if grp == 0:
                    E(ACT, lambda h: h.activation(out=VN[:, 0:512], in_=pv_[0:cl, :], func=AF.Copy), R=[pvb_], W=[vn_b])
                else:
                    E(DVE, lambda h: h.tensor_copy(out=VN[:, 512:1024], in_=pv_[0:cl, :]), R=[pvb_], W=[vn_b])
            po, pob = ps_next()
            for hh in range(8):
                E(PE, lambda h: h.matmul(po[:, hh * cl:(hh + 1) * cl], Sbf[:, hh, :], qdT[:, hh, o:o + cl], start=True,
                                         stop=False), R=[b_Sbf[hh], b_qdT[ck]], W=[pob], inc=False)
                E(PE, lambda h: h.matmul(po[:, hh * cl:(hh + 1) * cl], VN[:, hh * 128:(hh + 1) * 128],
                                         AQ[:, hh * cl:(hh + 1) * cl], start=False, stop=True), R=[vn_b, q_b], W=[pob],
                  inc=(hh == 7))
            E(ACT, lambda h: h.activation(out=oT[:, :, o:o + cl], in_=po[:, 0:W8].rearrange("p (h j) -> p h j", h=8),
                                          func=AF.Copy), R=[pob], W=[b_oT[ck]])
            for grp in range(2):
                pd, pdb = ps_next()
                for g in range(4):
                    hh = grp * 4 + g
                    E(PE, lambda h: h.matmul(pd[:, g * 128:(g + 1) * 128], kd.t[0:cl, kd_i, hh * 128:(hh + 1) * 128],
                                             VN[:, hh * 128:(hh + 1) * 128], start=True, stop=True), R=[kd_b, vn_b],
                      W=[pdb], inc=(g == 3))
                for g in range(4):
                    hh = grp * 4 + g
                    E(DVE, lambda h: h.scalar_tensor_tensor(out=S[:, hh, :], in0=S[:, hh, :],
                                                            scalar=EQ[:, hh * cl + cl - 1:hh * cl + cl],
                                                            in1=pd[:, g * 128:(g + 1) * 128], op0=ALU.mult, op1=ALU.add),
                      R=[e_b, pdb], W=[b_S[hh]])
                g0 = grp * 4
                E(ACT, lambda h: h.activation(out=Sbf[:, g0:g0 + 4, :], in_=S[:, g0:g0 + 4, :], func=AF.Copy),
                  R=b_S[g0:g0 + 4], W=b_Sbf[g0:g0 + 4])
        for hh in range(8):
            qi, qb = sqb.next()
            sq = sqb.t[:, qi, 0:nt]
            E(ACT, lambda h: h.activation(out=sq, in_=oT[:, hh, 0:nt], func=AF.Square), R=b_oT[0:nck], W=[qb])
            pt, pb = ps_next()
            E(PE, lambda h: h.matmul(pt[:, 0:nt], ones_b[:, :], sq, start=True, stop=True), R=[qb, b_const], W=[pb])
            ri, rb = rr.next()
            r = rr.t[:, ri, 0:nt]
            E(ACT, lambda h: h.activation(out=r, in_=pt[:, 0:nt], func=AF.Sqrt, bias=eps_c, scale=1.0 / 128.0),
              R=[pb, b_const], W=[rb])
            E(DVE, lambda h: h.reciprocal(out=r, in_=r), W=[rb])
            E(DVE, lambda h: h.tensor_tensor(out=r, in0=r, in1=oT[:, hh, 0:nt], op=ALU.mult), R=b_oT[0:nck], W=[rb])
            E(DVE, lambda h: h.scalar_tensor_tensor(out=od[:, hh, 0:nt], in0=r, scalar=pcol(P_DNW), in1=sz[:, hh, 0:nt],
                                                    op0=ALU.mult, op1=ALU.mult), R=[rb, b_sz[hh], b_const], W=[b_od[hh]])

        chk("ph4")
        for j in range(8):
            if j % 4 == 0:
                wco_ = wload(16 + j // 4)
                wdn_ = wload(18 + j // 4)
            wt, wb = wco_
            pa_, pab2 = proj_chunk(wt, wb, j % 4, nt, cact, b_cact)
            wt, wb = wdn_
            pb_, pbb2 = proj_chunk(wt, wb, j % 4, nt, od, b_od)
            mi, mb = mtmp.next()
            mt_ = mtmp.t[:, mi, 0:nt]
            E(DVE, lambda h: h.tensor_tensor(out=mt_, in0=pa_, in1=gts[:, j, 0:nt], op=ALU.mult), R=[pab2, b_gts[j]], W=[mb])
            mi2, mb2 = mtmp.next()
            mt2 = mtmp.t[:, mi2, 0:nt]
            E(DVE, lambda h: h.tensor_tensor(out=mt2, in0=pb_, in1=gts[:, 8 + j, 0:nt], op=ALU.mult),
              R=[pbb2, b_gts[8 + j]], W=[mb2])
            E(POOL, lambda h: h.tensor_tensor(out=mT[:, j, 0:nt], in0=mt_, in1=mt2, op=ALU.add), R=[mb, mb2], W=[b_mT[j]])
        for half in range(2):
            wt, wb = wload(20 + half)
            for (s, rows) in subs:
                pt, pb = ps_next()
                for kc in range(8):
                    E(PE, lambda h: h.matmul(pt[0:rows, :], mT[:, kc, s * 128:s * 128 + rows], wt[:, kc, :],
                                             start=(kc == 0), stop=(kc == 7)), R=[b_mT[kc], wb], W=[pb], inc=(kc == 7))
                hsl = htok[0:rows, hs, s, half * 512:(half + 1) * 512]
                E(DVE, lambda h: h.tensor_tensor(out=hsl, in0=hsl, in1=pt[0:rows, :], op=ALU.add), R=[pb],
                  W=[b_htok[hs][s]])

        chk("ph5")
        norm_to_uT(hs, subs, nt, P_FFW)

        chk("ph6")
        dgu = None
        for jp in range(11):
            if jp % 2 == 0:
                dgu = dload(11 + jp // 2)
            wt, wb = wload(22 + jp)
            for jj in range(2):
                j = 2 * jp + jj
                sga = sgb = None
                for part in range(2):
                    ch = j if part == 0 else NFF + j
                    g = 2 * jj + part
                    pp, ppb = proj_chunk(wt, wb, g, nt, uT, b_uT)
                    pi, prb = preu.next()
                    pr = preu.t[:, pi, :]
                    E(POOL, lambda h: h.tensor_copy(out=pr[:, 0:2], in_=halu[:, ch, :]), R=[b_halu[ch]], W=[prb])
                    if part == 0:
                        E(ACT, lambda h: h.activation(out=pr[:, 2:2 + nt], in_=pp, func=AF.Copy), R=[ppb], W=[prb])
                    else:
                        E(DVE, lambda h: h.tensor_copy(out=pr[:, 2:2 + nt], in_=pp), R=[ppb], W=[prb])
                    E(POOL, lambda h: h.tensor_copy(out=halu[:, ch, :], in_=pr[:, nt:nt + 2]), R=[prb], W=[b_halu[ch]])
                    pc3, pc3b = ps_next()
                    for t_ in range(3):
                        mi_ = ((jp % 2) * 4 + g) * 3 + t_
                        E(PE, lambda h: h.matmul(pc3[:, 0:nt], dgu[0][:, mi_ * 128:(mi_ + 1) * 128], pr[:, t_:t_ + nt],
                                                 start=(t_ == 0), stop=(t_ == 2)), R=[dgu[1], prb], W=[pc3b], inc=(t_ == 2))
                    if part == 0:
                        si_, sgb = sg.next()
                        sga = sg.t[:, si_, 0:nt]
                        E(ACT, lambda h: h.activation(out=sga, in_=pc3[:, 0:nt], func=AF.Silu, bias=pcol(P_FDB + ch)),
                          R=[pc3b, b_const], W=[sgb])
                    else:
                        E(DVE, lambda h: h.scalar_tensor_tensor(out=actb[:, j, 0:nt], in0=pc3[:, 0:nt],
                                                                scalar=pcol(P_FDB + ch), in1=sga, op0=ALU.add,
                                                                op1=ALU.mult), R=[pc3b, sgb, b_const], W=[b_act[j]])

        chk("ph7")
        for half in range(2):
            pts = [ps_next() for _ in subs]
            for grp in range(3):
                wt, wb = wload(33 + half * 3 + grp)
                nk = 8 if grp < 2 else NFF - 16
                for si_, (s, rows) in enumerate(subs):
                    pt, pb = pts[si_]
                    for kk in range(nk):
                        kc = grp * 8 + kk
                        last = (kc == NFF - 1)
                        E(PE, lambda h: h.matmul(pt[0:rows, :], actb[:, kc, s * 128:s * 128 + rows], wt[:, kk, :],
                                                 start=(kc == 0), stop=last), R=[b_act[kc], wb], W=[pb],
                          inc=(kk == nk - 1))
            for si_, (s, rows) in enumerate(subs):
                pt, pb = pts[si_]
                hsl = htok[0:rows, hs, s, half * 512:(half + 1) * 512]
                E(DVE, lambda h: h.tensor_tensor(out=hsl, in0=hsl, in1=pt[0:rows, :], op=ALU.add), R=[pb],
                  W=[b_htok[hs][s]])

        chk("ph8")
        if not is_meta:
            x0 = (ti - 1) * T
            for (s, rows) in subs:
                si, sbuf_ = stat.next()
                st = stat.t[0:rows, si, :]
                hin = htok[0:rows, hs, s, :]
                E(ACT, lambda h: h.activation(out=junk[0:rows, :], in_=hin, func=AF.Square, accum_out=st[:, 0:1]),
                  R=[b_htok[hs][s]], W=[b_junk, sbuf_])
                E(ACT, lambda h: h.activation(out=st[:, 1:2], in_=st[:, 0:1], func=AF.Sqrt, bias=eps_c[0:rows],
                                              scale=1.0 / D), R=[b_const], W=[sbuf_])
                E(DVE, lambda h: h.reciprocal(out=st[:, 2:3], in_=st[:, 1:2]), W=[sbuf_])
                E(DVE, lambda h: h.scalar_tensor_tensor(out=hin, in0=hin, scalar=st[:, 2:3],
                                                        in1=tokp[0:rows, TP_NFW:TP_NFW + D], op0=ALU.mult, op1=ALU.mult),
                  R=[sbuf_, b_const], W=[b_htok[hs][s]])
                cx.dma(ACT, out_d[x0 + s * 128:x0 + s * 128 + rows, :], hin, R=[b_htok[hs][s]])

    try:
        chk("setup")
        do_tile(0, True)
        chk("meta")
        for ti in range(1, ntile + 1):
            do_tile(ti, False)
    except _Stop:
        pass
    cx.finish()
    es.close()
    return nc, cx


def _pack_weights(w_in, w_conf_out, w_dn_out, w_out, w_up, w_down):
    blocks = np.zeros((NBLK, 128, 8, 512), np.float32)

    def colblock(W, c0s):
        W3 = W.reshape(8, 128, W.shape[1])
        out = np.empty((128, 8, 512), np.float32)
        for g, c0 in enumerate(c0s):
            out[:, :, g * 128:(g + 1) * 128] = W3[:, :, c0:c0 + 128].transpose(1, 0, 2)
        return out

    b = 0
    for jp in range(4):
        j0, j1 = 2 * jp, 2 * jp + 1
        blocks[b] = colblock(w_in, [1024 + j0 * 128, j0 * 128, 1024 + j1 * 128, j1 * 128]); b += 1
    for base in (3072, 2048, 4096, 5120):
        for hb in range(2):
            blocks[b] = colblock(w_in, [base + (hb * 4 + g) * 128 for g in range(4)]); b += 1
    for base in (6160, 7184):
        for hb in range(2):
            blocks[b] = colblock(w_in, [base + (hb * 4 + g) * 128 for g in range(4)]); b += 1
    for W in (w_conf_out, w_dn_out):
        for hb in range(2):
            blocks[b] = colblock(W, [(hb * 4 + g) * 128 for g in range(4)]); b += 1
    for half in range(2):
        blocks[b] = w_out.reshape(8, 128, 1024)[:, :, half * 512:(half + 1) * 512].transpose(1, 0, 2); b += 1
    for jp in range(11):
        j0, j1 = 2 * jp, 2 * jp + 1
        blocks[b] = colblock(w_up, [j0 * 128, DFF + j0 * 128, j1 * 128, DFF + j1 * 128]); b += 1
    Wd = w_down.reshape(NFF, 128, 1024)
    for half in range(2):
        for grp in range(3):
            nk = 8 if grp < 2 else NFF - 16
            blocks[b][:, 0:nk, :] = Wd[grp * 8:grp * 8 + nk, :, half * 512:(half + 1) * 512].transpose(1, 0, 2); b += 1
    assert b == NBLK
    return blocks.reshape(NBLK, 128, 8 * 512)


def _pack_params(inp):
    par = np.zeros((128, NPAR), np.float32)

    def cols(v):
        return v.reshape(-1, 128).T

    par[:, P_MIXW:P_MIXW + 8] = cols(inp["norm_mix_w"][0])
    par[:, P_BGATE:P_BGATE + 16] = cols(inp["b_gate"][0])
    cdw = inp["conf_dw_w"][0]
    par[:, P_CDW:P_CDW + 8 * CK] = cdw.reshape(CK, 8, 128).transpose(2, 1, 0).reshape(128, 8 * CK)
    par[:, P_CDB:P_CDB + 8] = cols(inp["conf_dw_b"][0])
    par[:, P_LNW:P_LNW + 8] = cols(inp["conf_ln_w"][0])
    par[:, P_LNB:P_LNB + 8] = cols(inp["conf_ln_b"][0])
    dnc = inp["dn_conv_w"][0]
    par[:, P_DNC:P_DNC + 96] = dnc.reshape(4, 24, 128).transpose(2, 1, 0).reshape(128, 96)
    par[:, P_DNW] = inp["dn_norm_w"][0]
    par[:, P_FFW:P_FFW + 8] = cols(inp["norm_ffn_w"][0])
    fdw = inp["ffn_dw_w"][0]
    par[:, P_FDW:P_FDW + 132] = fdw.reshape(3, 44, 128).transpose(2, 1, 0).reshape(128, 132)
    par[:, P_FDB:P_FDB + 44] = cols(inp["ffn_dw_b"][0])
    fcols = par[:, P_FDW:P_FDW + 132].copy()
    for jp in range(11):
        for g in range(4):
            j = 2 * jp + g // 2
            ch = j if g % 2 == 0 else NFF + j
            par[:, P_FDWB + (jp * 4 + g) * 3:P_FDWB + (jp * 4 + g) * 3 + 3] = fcols[:, ch * 3:ch * 3 + 3]
    tokp = np.zeros((128, NTOKP), np.float32)
    tokp[:, TP_DTB:TP_DTB + 8] = inp["dn_dt_bias"][0][None, :]
    tokp[:, TP_ALOG:TP_ALOG + 8] = inp["dn_A_log"][0][None, :]
    tokp[:, TP_NFW:] = inp["norm_final_w"][None, :]
    return par, tokp


_CACHE = {}


def kernel(**inputs):
    inp = {k: np.asarray(v, dtype=np.float32) for k, v in inputs.items()}
    x = inp["x"]
    B, n_x, _ = x.shape
    wpack = _pack_weights(inp["w_in"][0], inp["w_conf_out"][0], inp["w_dn_out"][0], inp["w_out"][0], inp["w_up"][0],
                          inp["w_down"][0])
    wab = np.ascontiguousarray(inp["w_in"][0][:, 6144:6160].reshape(8, 128, 16).transpose(1, 0, 2)).reshape(128, 128)
    par, tokp = _pack_params(inp)
    if n_x not in _CACHE:
        _CACHE[n_x] = build(n_x)[0]
    nc = _CACHE[n_x]
    in_maps = []
    for b in range(B):
        in_maps.append({"x": np.ascontiguousarray(x[b]), "meta": inp["meta_tokens"], "wpack": wpack, "wab": wab,
                        "params": par, "tokpar": tokp})
    res = run_bass_kernel_spmd(nc, in_maps, core_ids=list(range(B)))
    return np.stack([np.asarray(r["out"], dtype=np.float32) for r in res.results], axis=0)
```

```python
from contextlib import ExitStack

import numpy as np
import concourse.bass as bass
import concourse.mybir as mybir
from concourse.bass_utils import run_bass_kernel_spmd

F32 = mybir.dt.float32
BF16 = mybir.dt.bfloat16
ALU = mybir.AluOpType
AF = mybir.ActivationFunctionType

D = 1024
NMETA = 16
CK = 31
DFF = 2816
NFF = DFF // 128
EPS = 1e-6
T = 256
NBLK = 39
NEG = -30000.0

P_MIXW = 0
P_BGATE = P_MIXW + 8
P_CDW = P_BGATE + 16
P_CDB = P_CDW + 8 * CK
P_LNW = P_CDB + 8
P_LNB = P_LNW + 8
P_DNC = P_LNB + 8
P_DNW = P_DNC + 24 * 4
P_FFW = P_DNW + 1
P_FDW = P_FFW + 8
P_FDB = P_FDW + 44 * 3
P_FDWB = P_FDB + 44
NPAR = P_FDWB + 132
NDG = 17
TP_DTB = 0
TP_ALOG = 8
TP_NFW = 16
NTOKP = 16 + D


class Eng:
    def __init__(self, name, h, sem, inc):
        self.name = name
        self.h = h
        self.sem = sem
        self.inc = inc
        self.count = 0
        self.known = {}


class Buf:
    __slots__ = ("name", "w", "r")

    def __init__(self, name):
        self.name = name
        self.w = None
        self.r = {}


class Ctx:
    def __init__(self, nc, es, ndma=24, nsw=40):
        self.nc = nc
        self.es = es
        mk = lambda n: es.enter_context(nc.semaphore(n))
        self.PE = Eng("PE", nc.tensor, mk("s_pe"), 1)
        self.ACT = Eng("ACT", nc.scalar, mk("s_act"), 1)
        self.DVE = Eng("DVE", nc.vector, mk("s_dve"), 1)
        self.POOL = Eng("POOL", nc.gpsimd, mk("s_pool"), 1)
        self.SP = Eng("SP", nc.sync, None, 0)
        self.dsems = [Eng("D%d" % i, None, mk("s_d%d" % i), 16) for i in range(ndma)]
        self.dsems_sw = [Eng("W%d" % i, None, mk("s_w%d" % i), 16) for i in range(nsw)]
        self.dnext = 0
        self.dnext_sw = 0
        self.ninstr = 0

    def _deps(self, R, W):
        d = {}
        for b in R:
            if b.w is not None:
                e, i = b.w
                if d.get(e, 0) < i:
                    d[e] = i
        for b in W:
            if b.w is not None:
                e, i = b.w
                if d.get(e, 0) < i:
                    d[e] = i
            for e, i in b.r.items():
                if d.get(e, 0) < i:
                    d[e] = i
        return d

    def _waits(self, eng, d):
        for src, idx in d.items():
            if src is eng and eng is self.PE:
                continue
            if eng.known.get(src, 0) >= idx:
                continue
            assert idx <= src.count, "dependency on un-signalled instruction of %s" % src.name
            eng.h.wait_ge(src.sem, idx * src.inc)
            eng.known[src] = idx

    def _mark(self, p, R, W):
        e, i = p
        for b in R:
            if b.r.get(e, 0) < i:
                b.r[e] = i
        for b in W:
            b.w = p
            b.r = {}

    def emit(self, eng, fn, R=(), W=(), inc=True):
        self._waits(eng, self._deps(R, W))
        ins = fn(eng.h)
        self.ninstr += 1
        if inc:
            eng.count += 1
            ins.then_inc(eng.sem, 1)
            idx = eng.count
        else:
            idx = eng.count + 1
        self._mark((eng, idx), R, W)
        return ins

    def dma(self, q, out, in_, R=(), W=()):
        self._waits(q, self._deps(R, W))
        if q is self.POOL:
            ds = self.dsems_sw[self.dnext_sw]
            self.dnext_sw = (self.dnext_sw + 1) % len(self.dsems_sw)
        else:
            ds = self.dsems[self.dnext]
            self.dnext = (self.dnext + 1) % len(self.dsems)
        if q.known.get(ds, 0) < ds.count:
            q.h.wait_ge(ds.sem, ds.count * 16)
            q.known[ds] = ds.count
        q.h.dma_start(out=out, in_=in_).then_inc(ds.sem, 16)
        ds.count += 1
        self.ninstr += 1
        self._mark((ds, ds.count), R, W)

    def finish(self):
        for ds in self.dsems + self.dsems_sw:
            if ds.count and self.SP.known.get(ds, 0) < ds.count:
                self.SP.h.wait_ge(ds.sem, ds.count * 16)


class Rot:
    def __init__(self, name, tens, n):
        self.t = tens
        self.n = n
        self.bufs = [Buf("%s%d" % (name, i)) for i in range(n)]
        self.i = -1

    def next(self):
        self.i = (self.i + 1) % self.n
        return self.i, self.bufs[self.i]


class _Stop(Exception):
    pass


def build(n_x=4096, dbg=False, stop_after=None):
    nc = bass.Bass("TRN2", target_bir_lowering=False)
    es = ExitStack()
    cx = Ctx(nc, es)
    PE, ACT, DVE, POOL, SP = cx.PE, cx.ACT, cx.DVE, cx.POOL, cx.SP
    E = cx.emit
    NS = T // 128
    NCK = T // 64
    ntile = n_x // T

    x_d = nc.dram_tensor("x", [n_x, D], F32, kind="ExternalInput").ap()
    meta_d = nc.dram_tensor("meta", [NMETA, D], F32, kind="ExternalInput").ap()
    wpack_d = nc.dram_tensor("wpack", [NBLK, 128, 8 * 512], F32, kind="ExternalInput").ap()
    wab_d = nc.dram_tensor("wab", [128, 8 * 16], F32, kind="ExternalInput").ap()
    par_d = nc.dram_tensor("params", [128, NPAR], F32, kind="ExternalInput").ap()
    tokp_d = nc.dram_tensor("tokpar", [128, NTOKP], F32, kind="ExternalInput").ap()
    out_d = nc.dram_tensor("out", [n_x, D], F32, kind="ExternalOutput").ap()
    wbf_d = nc.dram_tensor("wbf", [NBLK, 128, 8 * 512], BF16, kind="Internal").ap()
    dg_d = nc.dram_tensor("dgd", [NDG, 128, 8 * 512], BF16, kind="Internal").ap()
    dbg_d = {}

    def sb(name, shape, dt):
        return es.enter_context(nc.sbuf_tensor(name, shape, dt))

    htok = sb("htok", [128, 2, NS, D], F32)
    b_htok = [[Buf("htok%d_%d" % (i, s)) for s in range(NS)] for i in range(2)]
    xn = Rot("xn", sb("xn", [128, 1, D], F32), 1)
    junk = sb("junk", [128, D], BF16)
    b_junk = Buf("junk")
    stat = Rot("stat", sb("stat", [128, 8, 4], F32), 8)
    uT = sb("uT", [128, 8, T], BF16)
    b_uT = [Buf("uT%d" % c) for c in range(8)]
    cbuf = sb("cbuf", [128, 8, 30 + T], BF16)
    b_c = [Buf("c%d" % c) for c in range(8)]
    ctmp = sb("ctmp", [128, 32], BF16)
    b_ctmp = Buf("ctmp")
    ybuf = sb("ybuf", [128, 8, T], F32)
    b_y = [Buf("y%d" % c) for c in range(8)]
    accA = Rot("accA", sb("accA", [128, 2, T], F32), 2)
    ybf = Rot("ybf", sb("ybf", [128, 2, T], BF16), 2)
    ysq = Rot("ysq", sb("ysq", [128, 2, T], BF16), 2)
    lnt = sb("lnt", [128, 5, T], F32)
    b_lnt = [Buf("lnt%d" % i) for i in range(5)]
    cact = sb("cact", [128, 8, T], BF16)
    b_cact = [Buf("cact%d" % c) for c in range(8)]
    sig = Rot("sig", sb("sig", [128, 2, T], F32), 2)
    pre = Rot("pre", sb("pre", [128, 3, 3 + T], BF16), 3)
    halq = sb("halq", [128, 24, 3], BF16)
    b_halq = [Buf("halq%d" % c) for c in range(24)]
    b_preh = [Buf("preh%d" % c) for c in range(3)]
    b_preuh = [Buf("preuh%d" % c) for c in range(4)]
    s32 = Rot("s32", sb("s32", [128, 2, T], F32), 2)
    sqb = Rot("sqb", sb("sqb", [128, 2, T], BF16), 2)
    rr = Rot("rr", sb("rr", [128, 2, T], F32), 2)
    qT = sb("qT", [128, 8, T], BF16)
    kT = sb("kT", [128, 8, T], BF16)
    vT = sb("vT", [128, 8, T], BF16)
    qdT = sb("qdT", [128, 8, T], BF16)
    b_qT = [Buf("qT%d" % c) for c in range(8)]
    b_kT = [Buf("kT%d" % c) for c in range(8)]
    b_vT = [Buf("vT%d" % c) for c in range(8)]
    b_qdT = [Buf("qdT%d" % c) for c in range(NCK)]
    sz = sb("sz", [128, 8, T], BF16)
    b_sz = [Buf("sz%d" % c) for c in range(8)]
    gts = sb("gts", [128, 16, T], BF16)
    b_gts = [Buf("gts%d" % c) for c in range(16)]
    oT = sb("oT", [128, 8, T], F32)
    b_oT = [Buf("oT%d" % c) for c in range(NCK)]
    od = sb("od", [128, 8, T], BF16)
    b_od = [Buf("od%d" % c) for c in range(8)]
    mT = sb("mT", [128, 8, T], BF16)
    b_mT = [Buf("mT%d" % c) for c in range(8)]
    mtmp = Rot("mtmp", sb("mtmp", [128, 2, T], F32), 2)
    actb = sb("actb", [128, NFF, T], BF16)
    b_act = [Buf("act%d" % c) for c in range(NFF)]
    preu = Rot("preu", sb("preu", [128, 4, 2 + T], BF16), 4)
    halu = sb("halu", [128, 44, 2], BF16)
    b_halu = [Buf("halu%d" % c) for c in range(44)]
    sg = Rot("sg", sb("sg", [128, 2, T], F32), 2)
    abt = sb("abt", [64, NCK, 16], F32)
    b_abt = Buf("abt")
    gtok = sb("gtok", [64, NCK, 8], F32)
    beta = sb("beta", [64, NCK, 8], F32)
    b_g = Buf("g")
    b_beta = Buf("beta")
    GU = Rot("GU", sb("GU", [64, 1, 512], F32), 1)
    gcol = Rot("gcol", sb("gcol", [64, 2, 32], F32), 2)
    D1 = Rot("D1", sb("D1", [64, 1, 512], F32), 1)
    D2 = Rot("D2", sb("D2", [64, 1, 512], F32), 1)
    Abf = Rot("Abf", sb("Abf", [64, 3, 512], BF16), 3)
    Mbf = Rot("Mbf", sb("Mbf", [64, 3, 512], BF16), 3)
    Pbf = Rot("Pbf", sb("Pbf", [64, 2, 512], BF16), 2)
    Pf = Rot("Pf", sb("Pf", [64, 1, 512], F32), 1)
    AqkT = Rot("AqkT", sb("AqkT", [64, 1, 512], BF16), 1)
    TTb = Rot("TTb", sb("TTb", [64, 1, 512], BF16), 1)
    kw = Rot("kw", sb("kw", [64, 1, 1024], BF16), 1)
    kd = Rot("kd", sb("kd", [64, 1, 1024], BF16), 1)
    vb = Rot("vb", sb("vb", [64, 1, 1024], BF16), 1)
    vn = Rot("vn", sb("vn", [64, 1, 1024], BF16), 1)
    wTn = Rot("wTn", sb("wTn", [128, 1, 512], BF16), 1)
    Eq = Rot("Eq", sb("Eq", [128, 1, 512], F32), 1)
    S = sb("S", [128, 8, 128], F32)
    Sbf = sb("Sbf", [128, 8, 128], BF16)
    b_S = [Buf("S%d" % h) for h in range(8)]
    b_Sbf = [Buf("Sbf%d" % h) for h in range(8)]
    NRING = 5
    wring = Rot("wring", sb("wring", [128, NRING, 8 * 512], BF16), NRING)
    wab = sb("wab_sb", [128, 8, 16], BF16)
    wabf = sb("wabf", [128, 8 * 16], F32)
    b_wab = Buf("wab")
    ident_f = sb("ident_f", [128, 128], F32)
    ident_b = sb("ident_b", [128, 128], BF16)
    ones_b = sb("ones_b", [128, 128], BF16)
    ones_f = sb("ones_f", [64, 128], F32)
    Umat = sb("Umat", [64, 64], F32)
    par = sb("par_sb", [128, NPAR], F32)
    tokp = sb("tokp", [128, NTOKP], F32)
    cst = sb("cst", [128, 4], F32)
    b_const = Buf("const")
    b_wbf = [Buf("wbf%d" % i) for i in range(NBLK)]
    b_dg = [Buf("dg%d" % i) for i in range(NDG)]

    ps = es.enter_context(nc.psum_tensor("ps", [128, 8, 512], F32))
    b_ps = [Buf("ps%d" % i) for i in range(8)]
    ps_state = {"i": -1}

    def ps_next():
        ps_state["i"] = (ps_state["i"] + 1) % 6
        i = ps_state["i"]
        return ps[:, i, :], b_ps[i]

    cx.dma(SP, par[:, :], par_d[:, :], W=[b_const])
    cx.dma(SP, tokp[:, :], tokp_d[:, :], W=[b_const])
    cx.dma(SP, wabf[:, :], wab_d[:, :], W=[b_wab])
    for b in range(NBLK):
        cx.dma(POOL, wbf_d[b], wpack_d[b], W=[b_wbf[b]])

    def pool_c(fn):
        E(POOL, fn, W=[b_const])

    pool_c(lambda h: h.memset(ident_f[:], 0.0))
    pool_c(lambda h: h.affine_select(out=ident_f[:], in_=ident_f[:], pattern=[[-1, 128]],
                                     compare_op=ALU.not_equal, fill=1.0, base=0, channel_multiplier=1))
    pool_c(lambda h: h.tensor_copy(out=ident_b[:], in_=ident_f[:]))
    pool_c(lambda h: h.memset(ones_b[:], 1.0))
    pool_c(lambda h: h.memset(ones_f[:], 1.0))
    pool_c(lambda h: h.memset(Umat[:], 1.0))
    pool_c(lambda h: h.affine_select(out=Umat[:], in_=Umat[:], pattern=[[1, 64]],
                                     compare_op=ALU.is_ge, fill=0.0, base=0, channel_multiplier=-1))
    pool_c(lambda h: h.memset(cst[:, 0:1], EPS))
    pool_c(lambda h: h.memset(cst[:, 1:2], 1.0))
    pool_c(lambda h: h.memset(cbuf[:], 0.0))
    pool_c(lambda h: h.memset(halq[:], 0.0))
    pool_c(lambda h: h.memset(halu[:], 0.0))
    pool_c(lambda h: h.memset(S[:], 0.0))
    pool_c(lambda h: h.memset(Sbf[:], 0.0))
    E(POOL, lambda h: h.tensor_copy(out=wab[:].rearrange("p a b -> p (a b)"), in_=wabf[:]), R=[], W=[b_wab])
    E(ACT, lambda h: h.activation(out=tokp[:, TP_ALOG:TP_ALOG + 8], in_=tokp[:, TP_ALOG:TP_ALOG + 8], func=AF.Exp),
      W=[b_const])
    E(DVE, lambda h: h.tensor_scalar(out=tokp[:, TP_ALOG:TP_ALOG + 8], in0=tokp[:, TP_ALOG:TP_ALOG + 8],
                                     scalar1=-1.0, scalar2=None, op0=ALU.mult), W=[b_const])
    all_state = b_c + b_halq + b_halu + b_S + b_Sbf
    for b in all_state:
        b.w = b_const.w

    neg_reg = nc.gpsimd.to_reg(NEG)
    def build_dg(d, poff, nm):
        i, wb_ = wring.next()
        slot = wring.t[:, i, 0:nm * 128].rearrange("p (m c) -> p m c", c=128)
        E(DVE, lambda h: h.tensor_tensor(out=slot, in0=ident_b[:, :].unsqueeze(1).to_broadcast([128, nm, 128]),
                                         in1=par[:, poff:poff + nm].unsqueeze(2).to_broadcast([128, nm, 128]), op=ALU.mult),
          R=[b_const], W=[wb_])
        cx.dma(SP, dg_d[d][:, 0:nm * 128], wring.t[:, i, 0:nm * 128], R=[wb_], W=[b_dg[d]])

    for j in range(8):
        build_dg(j, P_CDW + j * CK, CK)
    for kk_ in range(3):
        build_dg(8 + kk_, P_DNC + kk_ * 32, 32)
    for d_ in range(6):
        build_dg(11 + d_, P_FDWB + d_ * 24, min(24, 132 - d_ * 24))

    def dload(d):
        nm = CK if d < 8 else (32 if d < 11 else min(24, 132 - (d - 11) * 24))
        i, wb_ = wring.next()
        cx.dma(SP, wring.t[:, i, 0:nm * 128], dg_d[d][:, 0:nm * 128], R=[b_dg[d]], W=[wb_])
        return wring.t[:, i, :], wb_

    eps_c = cst[:, 0:1]
    one_c = cst[:, 1:2]

    def pcol(off, rows=128):
        return par[0:rows, off:off + 1]

    def wload(bidx):
        i, wb = wring.next()
        cx.dma(SP, wring.t[:, i, :], wbf_d[bidx], R=[b_wbf[bidx]], W=[wb])
        return wring.t[:, i, :].rearrange("p (k n) -> p k n", k=8), wb

    def norm_to_uT(hs, subs, nt, woff):
        xns = []
        for (s, rows) in subs:
            si, sbuf_ = stat.next()
            st = stat.t[0:rows, si, :]
            hin = htok[0:rows, hs, s, :]
            E(ACT, lambda h: h.activation(out=junk[0:rows, :], in_=hin, func=AF.Square, accum_out=st[:, 0:1]),
              R=[b_htok[hs][s]], W=[b_junk, sbuf_])
            E(ACT, lambda h: h.activation(out=st[:, 1:2], in_=st[:, 0:1], func=AF.Sqrt, bias=eps_c[0:rows], scale=1.0 / D),
              R=[b_const], W=[sbuf_])
            E(DVE, lambda h: h.reciprocal(out=st[:, 2:3], in_=st[:, 1:2]), W=[sbuf_])
            xi, xb = xn.next()
            xt = xn.t[0:rows, xi, :]
            E(DVE, lambda h: h.tensor_scalar(out=xt, in0=hin, scalar1=st[:, 2:3], scalar2=None, op0=ALU.mult),
              R=[b_htok[hs][s], sbuf_], W=[xb])
            for c in range(8):
                pt, pb = ps_next()
                E(PE, lambda h: h.transpose(pt[:, 0:rows], xt[:, c * 128:(c + 1) * 128], ident_f[0:rows, 0:rows]),
                  R=[xb, b_const], W=[pb])
                tgt = uT[:, c, s * 128:s * 128 + rows]
                if c % 2 == 0:
                    E(ACT, lambda h: h.activation(out=tgt, in_=pt[:, 0:rows], func=AF.Copy, scale=pcol(woff + c)),
                      R=[pb, b_const], W=[b_uT[c]])
                else:
                    E(DVE, lambda h: h.tensor_scalar(out=tgt, in0=pt[:, 0:rows], scalar1=pcol(woff + c), scalar2=None,
                                                     op0=ALU.mult), R=[pb, b_const], W=[b_uT[c]])

    def proj_chunk(wt, wb, g, nt, src, b_src, nk=8):
        pt, pb = ps_next()
        for kc in range(nk):
            E(PE, lambda h: h.matmul(pt[:, 0:nt], wt[:, kc, g * 128:(g + 1) * 128], src[:, kc, 0:nt],
                                     start=(kc == 0), stop=(kc == nk - 1)),
              R=[wb, b_src[kc]], W=[pb], inc=(kc == nk - 1))
        return pt[:, 0:nt], pb

    def l2_rstd(src_ap, b_src, nt, scale_in):
        qi, qb = sqb.next()
        sq = sqb.t[:, qi, 0:nt]
        E(ACT, lambda h: h.activation(out=sq, in_=src_ap, func=AF.Square), R=[b_src], W=[qb])
        pt, pb = ps_next()
        E(PE, lambda h: h.matmul(pt[:, 0:nt], ones_b[:, :], sq, start=True, stop=True), R=[qb, b_const], W=[pb])
        ri, rb = rr.next()
        r = rr.t[:, ri, 0:nt]
        E(ACT, lambda h: h.activation(out=r, in_=pt[:, 0:nt], func=AF.Sqrt, bias=eps_c, scale=scale_in),
          R=[pb, b_const], W=[rb])
        E(DVE, lambda h: h.reciprocal(out=r, in_=r), W=[rb])
        return r, rb

    def chk(name):
        if stop_after == name:
            raise _Stop()

    def do_tile(ti, is_meta):
        hs = ti % 2
        if is_meta:
            nt = NMETA
            subs = [(0, NMETA)]
            chunks = [(0, NMETA)]
            cx.dma(SP, htok[0:NMETA, hs, 0, :], meta_d[:, :], W=[b_htok[hs][0]])
        else:
            nt = T
            subs = [(s, 128) for s in range(NS)]
            chunks = [(c * 64, 64) for c in range(NCK)]
            x0 = (ti - 1) * T
            for (s, rows) in subs:
                cx.dma(SP, htok[:, hs, s, :], x_d[x0 + s * 128:x0 + (s + 1) * 128, :], W=[b_htok[hs][s]])

        norm_to_uT(hs, subs, nt, P_MIXW)

        chk("ph1")
        for jp in range(4):
            wt, wb = wload(jp)
            for jj in range(2):
                j = 2 * jp + jj
                pg, pgb = proj_chunk(wt, wb, 2 * jj, nt, uT, b_uT)
                gi, gb = sig.next()
                sg_ap = sig.t[:, gi, 0:nt]
                E(ACT, lambda h: h.activation(out=sg_ap, in_=pg, func=AF.Sigmoid), R=[pgb], W=[gb])
                pv, pvb = proj_chunk(wt, wb, 2 * jj + 1, nt, uT, b_uT)
                E(DVE, lambda h: h.tensor_tensor(out=cbuf[:, j, 30:30 + nt], in0=pv, in1=sg_ap, op=ALU.mult),
                  R=[pvb, gb], W=[b_c[j]])
        for j in range(8):
            dg, dgb = dload(j)
            pt, pb = ps_next()
            for t_ in range(CK):
                E(PE, lambda h: h.matmul(pt[:, 0:nt], dg[:, t_ * 128:(t_ + 1) * 128], cbuf[:, j, t_:t_ + nt],
                                         start=(t_ == 0), stop=(t_ == CK - 1)), R=[dgb, b_c[j]], W=[pb], inc=(t_ == CK - 1))
            E(ACT, lambda h: h.activation(out=ybuf[:, j, 0:nt], in_=pt[:, 0:nt], func=AF.Identity, bias=pcol(P_CDB + j)),
              R=[pb, b_const], W=[b_y[j]])
            E(POOL, lambda h: h.tensor_copy(out=ctmp[:, 0:30], in_=cbuf[:, j, nt:nt + 30]), R=[b_c[j]], W=[b_ctmp])
            E(POOL, lambda h: h.tensor_copy(out=cbuf[:, j, 0:30], in_=ctmp[:, 0:30]), R=[b_ctmp], W=[b_c[j]])

        qkv_list = [(kind, hh) for kind in ("k", "q", "v") for hh in range(8)]
        kbase = {"q": 0, "k": 8, "v": 16}
        wblk0 = {"k": 4, "q": 6, "v": 8}
        dgidx = {"q": 8, "k": 9, "v": 10}
        qs = {}

        def q_st0(i):
            kind, hh = qkv_list[i]
            if hh == 0:
                qs["dg", kind] = dload(dgidx[kind])
            if hh % 4 == 0:
                qs["w"] = wload(wblk0[kind] + hh // 4)
            wt, wb = qs["w"]
            ch = kbase[kind] + hh
            pp, ppb = proj_chunk(wt, wb, hh % 4, nt, uT, b_uT)
            pi, prb = pre.next()
            phb = b_preh[pi]
            pr = pre.t[:, pi, :]
            E(POOL, lambda h: h.tensor_copy(out=pr[:, 0:3], in_=halq[:, ch, :]), R=[b_halq[ch]], W=[phb])
            if kind == "v":
                E(ACT, lambda h: h.activation(out=pr[:, 3:3 + nt], in_=pp, func=AF.Copy), R=[ppb], W=[prb])
            else:
                E(DVE, lambda h: h.tensor_copy(out=pr[:, 3:3 + nt], in_=pp), R=[ppb], W=[prb])
            E(POOL, lambda h: h.tensor_copy(out=halq[:, ch, :], in_=pr[:, nt:nt + 3]), R=[prb], W=[b_halq[ch]])
            qs[i] = (pr, prb, phb)

        def q_st1(i):
            kind, hh = qkv_list[i]
            pr, prb, phb = qs[i]
            dgq, dgqb = qs["dg", kind]
            pc4, cb = ps_next()
            ca = pc4[:, 0:nt]
            for t_ in range(4):
                mi_ = hh * 4 + t_
                E(PE, lambda h: h.matmul(ca, dgq[:, mi_ * 128:(mi_ + 1) * 128], pr[:, t_:t_ + nt], start=(t_ == 0),
                                         stop=(t_ == 3)), R=[dgqb, prb, phb], W=[cb], inc=(t_ == 3))
            if kind == "v":
                E(ACT, lambda h: h.activation(out=vT[:, hh, 0:nt], in_=ca, func=AF.Silu), R=[cb], W=[b_vT[hh]])
            else:
                si_, sb_ = s32.next()
                sa = s32.t[:, si_, 0:nt]
                E(ACT, lambda h: h.activation(out=sa, in_=ca, func=AF.Silu), R=[cb], W=[sb_])
                qi, qb = sqb.next()
                sq = sqb.t[:, qi, 0:nt]
                E(ACT, lambda h: h.activation(out=sq, in_=sa, func=AF.Square), R=[sb_], W=[qb])
                qs[i] = (sa, sb_, sq, qb)

        def q_st2(i):
            kind, hh = qkv_list[i]
            if kind == "v":
                return
            sa, sb_, sq, qb = qs[i]
            pt, pb = ps_next()
            E(PE, lambda h: h.matmul(pt[:, 0:nt], ones_b[:, :], sq, start=True, stop=True), R=[qb, b_const], W=[pb])
            ri, rb = rr.next()
            r = rr.t[:, ri, 0:nt]
            E(ACT, lambda h: h.activation(out=r, in_=pt[:, 0:nt], func=AF.Sqrt, bias=eps_c, scale=1.0), R=[pb, b_const],
              W=[rb])
            E(DVE, lambda h: h.reciprocal(out=r, in_=r), W=[rb])
            if kind == "k":
                E(DVE, lambda h: h.tensor_tensor(out=kT[:, hh, 0:nt], in0=sa, in1=r, op=ALU.mult), R=[sb_, rb],
                  W=[b_kT[hh]])
            else:
                E(DVE, lambda h: h.scalar_tensor_tensor(out=qT[:, hh, 0:nt], in0=sa, scalar=128.0 ** -0.5, in1=r,
                                                        op0=ALU.mult, op1=ALU.mult), R=[sb_, rb], W=[b_qT[hh]])

        for i in range(24 + 3):
            if 0 <= i - 2 < 24:
                q_st1(i - 2)
            if 0 <= i - 3 < 24:
                q_st2(i - 3)
            if i < 24:
                q_st0(i)
        for bb_ in range(2):
            wt, wb = wload(10 + bb_)
            for g in range(4):
                hh = bb_ * 4 + g
                pp, ppb = proj_chunk(wt, wb, g, nt, uT, b_uT)
                E(ACT, lambda h: h.activation(out=sz[:, hh, 0:nt], in_=pp, func=AF.Silu), R=[ppb], W=[b_sz[hh]])
        for bb_ in range(4):
            wt, wb = wload(12 + bb_)
            for g in range(4):
                jg = bb_ * 4 + g
                pp, ppb = proj_chunk(wt, wb, g, nt, uT, b_uT)
                E(ACT, lambda h: h.activation(out=gts[:, jg, 0:nt], in_=pp, func=AF.Sigmoid, bias=pcol(P_BGATE + jg)),
                  R=[ppb, b_const], W=[b_gts[jg]])
        pab, pabb = ps_next()
        pab3 = pab.rearrange("p (c n) -> p c n", n=16)
        for ck, (o, cl) in enumerate(chunks):
            for kc in range(8):
                E(PE, lambda h: h.matmul(pab3[0:cl, ck, :], uT[:, kc, o:o + cl], wab[:, kc, :], start=(kc == 0),
                                         stop=(kc == 7)), R=[b_uT[kc], b_wab], W=[pabb], inc=(kc == 7))
        nck = len(chunks)
        cl0 = chunks[0][1]
        dtb_b = tokp[0:cl0, TP_DTB:TP_DTB + 8].unsqueeze(1).to_broadcast([cl0, nck, 8])
        negA_b = tokp[0:cl0, TP_ALOG:TP_ALOG + 8].unsqueeze(1).to_broadcast([cl0, nck, 8])
        E(DVE, lambda h: h.tensor_tensor(out=abt[0:cl0, 0:nck, 0:8], in0=pab3[0:cl0, 0:nck, 0:8], in1=dtb_b, op=ALU.add),
          R=[pabb, b_const], W=[b_abt])
        E(ACT, lambda h: h.activation(out=abt[0:cl0, 0:nck, 0:8], in_=abt[0:cl0, 0:nck, 0:8], func=AF.Exp), W=[b_abt])
        E(ACT, lambda h: h.activation(out=abt[0:cl0, 0:nck, 0:8], in_=abt[0:cl0, 0:nck, 0:8], func=AF.Ln,
                                      bias=one_c[0:cl0]), R=[b_const], W=[b_abt])
        E(DVE, lambda h: h.tensor_tensor(out=gtok[0:cl0, 0:nck, :], in0=abt[0:cl0, 0:nck, 0:8], in1=negA_b, op=ALU.mult),
          R=[b_abt, b_const], W=[b_g])
        E(ACT, lambda h: h.activation(out=beta[0:cl0, 0:nck, :], in_=pab3[0:cl0, 0:nck, 8:16], func=AF.Sigmoid),
          R=[pabb], W=[b_beta])

        chk("ph2")
        pm, pmb = ps_next()
        pq, pqb = ps_next()
        for j in range(8):
            yi, yb_ = ybf.next()
            qi, qb_ = ysq.next()
            E(ACT, lambda h: h.activation(out=ybf.t[:, yi, 0:nt], in_=ybuf[:, j, 0:nt], func=AF.Copy), R=[b_y[j]], W=[yb_])
            E(ACT, lambda h: h.activation(out=ysq.t[:, qi, 0:nt], in_=ybuf[:, j, 0:nt], func=AF.Square), R=[b_y[j]], W=[qb_])
            E(PE, lambda h: h.matmul(pm[:, 0:nt], ones_b[:, :], ybf.t[:, yi, 0:nt], start=(j == 0), stop=(j == 7)),
              R=[yb_, b_const], W=[pmb])
            E(PE, lambda h: h.matmul(pq[:, 0:nt], ones_b[:, :], ysq.t[:, qi, 0:nt], start=(j == 0), stop=(j == 7)),
              R=[qb_, b_const], W=[pqb])
        m_ = lnt[:, 0, 0:nt]
        msq = lnt[:, 1, 0:nt]
        var = lnt[:, 2, 0:nt]
        rstd = lnt[:, 3, 0:nt]
        mr = lnt[:, 4, 0:nt]
        E(DVE, lambda h: h.tensor_scalar(out=m_, in0=pm[:, 0:nt], scalar1=1.0 / D, scalar2=None, op0=ALU.mult),
          R=[pmb], W=[b_lnt[0]])
        E(DVE, lambda h: h.tensor_tensor(out=msq, in0=m_, in1=m_, op=ALU.mult), R=[b_lnt[0]], W=[b_lnt[1]])
        E(DVE, lambda h: h.scalar_tensor_tensor(out=var, in0=pq[:, 0:nt], scalar=1.0 / D, in1=msq, op0=ALU.mult,
                                                op1=ALU.subtract), R=[pqb, b_lnt[1]], W=[b_lnt[2]])
        E(ACT, lambda h: h.activation(out=rstd, in_=var, func=AF.Sqrt, bias=eps_c, scale=1.0), R=[b_lnt[2], b_const],
          W=[b_lnt[3]])
        E(DVE, lambda h: h.reciprocal(out=rstd, in_=rstd), W=[b_lnt[3]])
        E(DVE, lambda h: h.tensor_tensor(out=mr, in0=m_, in1=rstd, op=ALU.mult), R=[b_lnt[0], b_lnt[3]], W=[b_lnt[4]])
        for j in range(8):
            ai, ab = accA.next()
            aa = accA.t[:, ai, 0:nt]
            E(DVE, lambda h: h.tensor_tensor(out=aa, in0=ybuf[:, j, 0:nt], in1=rstd, op=ALU.mult), R=[b_y[j], b_lnt[3]],
              W=[ab])
            E(DVE, lambda h: h.tensor_tensor(out=aa, in0=aa, in1=mr, op=ALU.subtract), R=[b_lnt[4]], W=[ab])
            E(ACT, lambda h: h.activation(out=cact[:, j, 0:nt], in_=aa, func=AF.Silu, bias=pcol(P_LNB + j),
                                          scale=pcol(P_LNW + j)), R=[ab, b_const], W=[b_cact[j]])

        chk("ph3")
        for ck, (o, cl) in enumerate(chunks):
            nlev = 5 if cl == 64 else 3
            W8 = 8 * cl
            gi, gub = GU.next()
            gu3 = GU.t[0:cl, gi, 0:W8].rearrange("p (h j) -> p h j", h=8)
            gsl = gtok[0:cl, ck, :]
            E(DVE, lambda h: h.tensor_tensor(out=gu3, in0=gsl.unsqueeze(2).to_broadcast([cl, 8, cl]),
                                             in1=Umat[0:cl, 0:cl].unsqueeze(1).to_broadcast([cl, 8, cl]), op=ALU.mult),
              R=[b_g, b_const], W=[gub])
            bps = ps[:, 6, 0:W8]
            bps3 = bps.rearrange("p (h j) -> p h j", h=8)
            E(PE, lambda h: h.matmul(bps, ones_f[0:cl, :], GU.t[0:cl, gi, 0:W8], start=True, stop=True),
              R=[gub, b_const], W=[b_ps[6]])
            pc, pcb = ps_next()
            E(PE, lambda h: h.matmul(pc[0:cl, 0:8], Umat[0:cl, 0:cl], gsl, start=True, stop=True), R=[b_g, b_const], W=[pcb])
            ci_, gcb = gcol.next()
            gc = gcol.t[0:cl, ci_, :]
            E(ACT, lambda h: h.activation(out=gc[:, 0:8], in_=pc[0:cl, 0:8], func=AF.Copy), R=[pcb], W=[gcb])
            E(ACT, lambda h: h.activation(out=gc[:, 8:16], in_=pc[0:cl, 0:8], func=AF.Exp), R=[pcb], W=[gcb])
            E(DVE, lambda h: h.tensor_tensor(out=gc[:, 16:24], in0=gc[:, 8:16], in1=beta[0:cl, ck, :], op=ALU.mult),
              R=[b_beta], W=[gcb])
            E(DVE, lambda h: h.tensor_tensor(out=gc[:, 24:32], in0=bps3[0:cl, :, cl - 1], in1=gc[:, 0:8], op=ALU.subtract),
              R=[b_ps[6]], W=[gcb])
            E(ACT, lambda h: h.activation(out=gc[:, 24:32], in_=gc[:, 24:32], func=AF.Exp), W=[gcb])
            d1i, d1b = D1.next()
            d2i, d2b = D2.next()
            d1 = D1.t[0:cl, d1i, 0:W8].rearrange("p (h j) -> p h j", h=8)
            d2 = D2.t[0:cl, d2i, 0:W8].rearrange("p (h j) -> p h j", h=8)
            gcb3 = gc[:, 0:8].unsqueeze(2).to_broadcast([cl, 8, cl])
            E(DVE, lambda h: h.tensor_tensor(out=d1, in0=gcb3, in1=bps3[0:cl], op=ALU.subtract), R=[gcb, b_ps[6]], W=[d1b])
            E(DVE, lambda h: h.tensor_tensor(out=d2, in0=bps3[0:cl], in1=gcb3, op=ALU.subtract), R=[gcb, b_ps[6]], W=[d2b])
            E(POOL, lambda h: h.affine_select(out=d1, in_=d1, pattern=[[0, 8], [-1, cl]], compare_op=ALU.is_gt, fill=neg_reg,
                                              base=0, channel_multiplier=1), W=[d1b])
            E(POOL, lambda h: h.affine_select(out=d2, in_=d2, pattern=[[0, 8], [1, cl]], compare_op=ALU.is_ge, fill=neg_reg,
                                              base=0, channel_multiplier=-1), W=[d2b])
            E(ACT, lambda h: h.activation(out=d1, in_=d1, func=AF.Exp), W=[d1b])
            E(ACT, lambda h: h.activation(out=d2, in_=d2, func=AF.Exp), W=[d2b])
            E(DVE, lambda h: h.tensor_tensor(out=d1, in0=d1, in1=beta[0:cl, ck, :].unsqueeze(2).to_broadcast([cl, 8, cl]),
                                              op=ALU.mult), R=[b_beta], W=[d1b])
            chk("d1")
            pkk, pkkb = ps_next()
            pqk, pqkb = ps_next()
            for hh in range(8):
                E(PE, lambda h: h.matmul(pkk[0:cl, hh * cl:(hh + 1) * cl], kT[:, hh, o:o + cl], kT[:, hh, o:o + cl],
                                         start=True, stop=True), R=[b_kT[hh]], W=[pkkb], inc=(hh == 7))
            for hh in range(8):
                E(PE, lambda h: h.matmul(pqk[0:cl, hh * cl:(hh + 1) * cl], kT[:, hh, o:o + cl], qT[:, hh, o:o + cl],
                                         start=True, stop=True), R=[b_kT[hh], b_qT[hh]], W=[pqkb], inc=(hh == 7))
            a_i, a_b = Abf.next()
            A0 = Abf.t[0:cl, a_i, 0:W8]
            E(DVE, lambda h: h.tensor_tensor(out=A0, in0=pkk[0:cl, 0:W8], in1=D1.t[0:cl, d1i, 0:W8], op=ALU.mult),
              R=[pkkb, d1b], W=[a_b])
            q_i, q_b = AqkT.next()
            AQ = AqkT.t[0:cl, q_i, 0:W8]
            E(DVE, lambda h: h.tensor_tensor(out=AQ, in0=pqk[0:cl, 0:W8], in1=D2.t[0:cl, d2i, 0:W8], op=ALU.mult),
              R=[pqkb, d2b], W=[q_b])
            chk("d2")
            pmt, pmtb = ps_next()
            for hh in range(8):
                E(PE, lambda h: h.matmul(pmt[0:cl, hh * cl:(hh + 1) * cl], A0[:, hh * cl:(hh + 1) * cl], ident_b[0:cl, 0:cl],
                                         start=True, stop=True), R=[a_b, b_const], W=[pmtb], inc=(hh == 7))
            m_i, m_b = Mbf.next()
            M0 = Mbf.t[0:cl, m_i, 0:W8]
            E(ACT, lambda h: h.activation(out=M0, in_=pmt[0:cl, 0:W8], func=AF.Copy), R=[pmtb], W=[m_b])
            chk("d3")
            pf_i, pf_b = Pf.next()
            PF = Pf.t[0:cl, pf_i, 0:W8]
            E(DVE, lambda h: h.tensor_tensor(out=PF.rearrange("p (h j) -> p h j", h=8),
                                             in0=ident_f[0:cl, 0:cl].unsqueeze(1).to_broadcast([cl, 8, cl]),
                                             in1=M0.rearrange("p (h j) -> p h j", h=8), op=ALU.subtract),
              R=[m_b, b_const], W=[pf_b])
            p_i, p_b = Pbf.next()
            Pk = Pbf.t[0:cl, p_i, 0:W8]
            E(ACT, lambda h: h.activation(out=Pk, in_=PF, func=AF.Copy), R=[pf_b], W=[p_b])
            chk("e0")
            Ak, Ak_b, Mk, Mk_b = A0, a_b, M0, m_b
            for lev in range(1, nlev + 1):
                last = (lev == nlev)
                if lev == 2:
                    chk("e2")
                pa, pab_ = ps_next()
                for hh in range(8):
                    sl = slice(hh * cl, (hh + 1) * cl)
                    E(PE, lambda h: h.matmul(pa[0:cl, sl], Mk[:, sl], Ak[:, sl], start=True, stop=True),
                      R=[Mk_b, Ak_b], W=[pab_], inc=(hh == 7))
                if not last:
                    pm2, pm2b = ps_next()
                    for hh in range(8):
                        sl = slice(hh * cl, (hh + 1) * cl)
                        E(PE, lambda h: h.matmul(pm2[0:cl, sl], Ak[:, sl], Mk[:, sl], start=True, stop=True),
                          R=[Mk_b, Ak_b], W=[pm2b], inc=(hh == 7))
                chk("e1")
                na_i, na_b = Abf.next()
                An = Abf.t[0:cl, na_i, 0:W8]
                E(ACT, lambda h: h.activation(out=An, in_=pa[0:cl, 0:W8], func=AF.Copy), R=[pab_], W=[na_b])
                if not last:
                    nm_i, nm_b = Mbf.next()
                    Mn = Mbf.t[0:cl, nm_i, 0:W8]
                    E(DVE, lambda h: h.tensor_copy(out=Mn, in_=pm2[0:cl, 0:W8]), R=[pm2b], W=[nm_b])
                pp_, ppb_ = ps_next()
                for hh in range(8):
                    sl = slice(hh * cl, (hh + 1) * cl)
                    E(PE, lambda h: h.matmul(pp_[0:cl, sl], An[:, sl], Pk[:, sl], start=True, stop=True),
                      R=[na_b, p_b], W=[ppb_], inc=(hh == 7))
                E(DVE, lambda h: h.tensor_tensor(out=PF, in0=PF, in1=pp_[0:cl, 0:W8], op=ALU.add), R=[ppb_], W=[pf_b])
                if not last:
                    np_i, np_b = Pbf.next()
                    Pn = Pbf.t[0:cl, np_i, 0:W8]
                    E(ACT, lambda h: h.activation(out=Pn, in_=PF, func=AF.Copy), R=[pf_b], W=[np_b])
                    Pk, p_b = Pn, np_b
                    Mk, Mk_b = Mn, nm_b
                Ak, Ak_b = An, na_b
            t_i, t_b = TTb.next()
            TT = TTb.t[0:cl, t_i, 0:W8]
            E(ACT, lambda h: h.activation(out=TT, in_=PF, func=AF.Copy), R=[pf_b], W=[t_b])
            chk("d4")
            kw_i, kw_b = kw.next()
            kd_i, kd_b = kd.next()
            vb_i, vb_b = vb.next()

            def sc(col0, g0):
                return gc[:, col0 + g0:col0 + g0 + 4].unsqueeze(2).to_broadcast([cl, 4, 128])

            for grp in range(2):
                pkt, pktb = ps_next()
                pvt, pvtb = ps_next()
                for g in range(4):
                    hh = grp * 4 + g
                    E(PE, lambda h: h.matmul(pkt[0:cl, g * 128:(g + 1) * 128], kT[:, hh, o:o + cl], ident_b[:, :],
                                             start=True, stop=True), R=[b_kT[hh], b_const], W=[pktb], inc=(g == 3))
                for g in range(4):
                    hh = grp * 4 + g
                    E(PE, lambda h: h.matmul(pvt[0:cl, g * 128:(g + 1) * 128], vT[:, hh, o:o + cl], ident_b[:, :],
                                             start=True, stop=True), R=[b_vT[hh], b_const], W=[pvtb], inc=(g == 3))
                k3 = pkt[0:cl, :].rearrange("p (h d) -> p h d", h=4)
                v3 = pvt[0:cl, :].rearrange("p (h d) -> p h d", h=4)
                csl = slice(grp * 512, (grp + 1) * 512)
                E(DVE, lambda h: h.tensor_tensor(out=kw.t[0:cl, kw_i, csl].rearrange("p (h d) -> p h d", h=4), in0=k3,
                                                 in1=sc(16, grp * 4), op=ALU.mult), R=[pktb, gcb], W=[kw_b])
                E(DVE, lambda h: h.tensor_tensor(out=kd.t[0:cl, kd_i, csl].rearrange("p (h d) -> p h d", h=4), in0=k3,
                                                 in1=sc(24, grp * 4), op=ALU.mult), R=[pktb, gcb], W=[kd_b])
                E(DVE, lambda h: h.tensor_tensor(out=vb.t[0:cl, vb_i, csl].rearrange("p (h d) -> p h d", h=4), in0=v3,
                                                 in1=beta[0:cl, ck, grp * 4:grp * 4 + 4].unsqueeze(2).to_broadcast([cl, 4, 128]),
                                                 op=ALU.mult), R=[pvtb, b_beta], W=[vb_b])
            chk("d5")
            pw, pwb = ps_next()
            for hh in range(8):
                E(PE, lambda h: h.matmul(pw[:, hh * cl:(hh + 1) * cl], kw.t[0:cl, kw_i, hh * 128:(hh + 1) * 128],
                                         TT[:, hh * cl:(hh + 1) * cl], start=True, stop=True), R=[kw_b, t_b], W=[pwb],
                  inc=(hh == 7))
            w_i, w_b = wTn.next()
            WT = wTn.t[:, w_i, 0:W8]
            E(ACT, lambda h: h.activation(out=WT, in_=pw[:, 0:W8], func=AF.Copy, scale=-1.0), R=[pwb], W=[w_b])
            e_i, e_b = Eq.next()
            EQ = Eq.t[:, e_i, 0:W8]
            E(ACT, lambda h: h.activation(out=EQ, in_=bps, func=AF.Exp), R=[b_ps[6]], W=[e_b])
            EQ3 = EQ.rearrange("p (h j) -> p h j", h=8)
            E(DVE, lambda h: h.tensor_tensor(out=qdT[:, :, o:o + cl], in0=qT[:, :, o:o + cl], in1=EQ3, op=ALU.mult),
              R=b_qT + [e_b], W=[b_qdT[ck]])
            chk("d6")
            vn_i, vn_b = vn.next()
            VN = vn.t[0:cl, vn_i, :]
            for grp in range(2):
                pv_, pvb_ = ps_next()
                for g in range(4):
                    hh = grp * 4 + g
                    E(PE, lambda h: h.matmul(pv_[0:cl, g * 128:(g + 1) * 128], TT[:, hh * cl:(hh + 1) * cl],
                                             vb.t[0:cl, vb_i, hh * 128:(hh + 1) * 128], start=True, stop=False),
                      R=[t_b, vb_b], W=[pvb_], inc=False)
                    E(PE, lambda h: h.matmul(pv_[0:cl, g * 128:(g + 1) * 128], WT[:, hh * cl:(hh + 1) * cl],
                                             Sbf[:, hh, :], start=False, stop=True), R=[w_b, b_Sbf[hh]], W=[pvb_],
                      inc=(g == 3))
                if grp == 0:
                    E(ACT, lambda h: h.activation(out=VN[:, 0:512], in_=pv_[0:cl, :], func=AF.Copy), R=[pvb_], W=[vn_b])
                else:
                    E(DVE, lambda h: h.tensor_copy(out=VN[:, 512:1024], in_=pv_[0:cl, :]), R=[pvb_], W=[vn_b])
            po, pob = ps_next()
            for hh in range(8):
                E(PE, lambda h: h.matmul(po[:, hh * cl:(hh + 1) * cl], Sbf[:, hh, :], qdT[:, hh, o:o + cl], start=True,
                                         stop=False), R=[b_Sbf[hh], b_qdT[ck]], W=[pob], inc=False)
                E(PE, lambda h: h.matmul(po[:, hh * cl:(hh + 1) * cl], VN[:, hh * 128:(hh + 1) * 128],
                                         AQ[:, hh * cl:(hh + 1) * cl], start=False, stop=True), R=[vn_b, q_b], W=[pob],
                  inc=(hh == 7))
            E(ACT, lambda h: h.activation(out=oT[:, :, o:o + cl], in_=po[:, 0:W8].rearrange("p (h j) -> p h j", h=8),
                                          func=AF.Copy), R=[pob], W=[b_oT[ck]])
            for grp in range(2):
                pd, pdb = ps_next()
                for g in range(4):
                    hh = grp * 4 + g
                    E(PE, lambda h: h.matmul(pd[:, g * 128:(g + 1) * 128], kd.t[0:cl, kd_i, hh * 128:(hh + 1) * 128],
                                             VN[:, hh * 128:(hh + 1) * 128], start=True, stop=True), R=[kd_b, vn_b],
                      W=[pdb], inc=(g == 3))
                for g in range(4):
                    hh = grp * 4 + g
                    E(DVE, lambda h: h.scalar_tensor_tensor(out=S[:, hh, :], in0=S[:, hh, :],
                                                            scalar=EQ[:, hh * cl + cl - 1:hh * cl + cl],
                                                            in1=pd[:, g * 128:(g + 1) * 128], op0=ALU.mult, op1=ALU.add),
                      R=[e_b, pdb], W=[b_S[hh]])
                g0 = grp * 4
                E(ACT, lambda h: h.activation(out=Sbf[:, g0:g0 + 4, :], in_=S[:, g0:g0 + 4, :], func=AF.Copy),
                  R=b_S[g0:g0 + 4], W=b_Sbf[g0:g0 + 4])
        for hh in range(8):
            qi, qb = sqb.next()
            sq = sqb.t[:, qi, 0:nt]
            E(ACT, lambda h: h.activation(out=sq, in_=oT[:, hh, 0:nt], func=AF.Square), R=b_oT[0:nck], W=[qb])
            pt, pb = ps_next()
            E(PE, lambda h: h.matmul(pt[:, 0:nt], ones_b[:, :], sq, start=True, stop=True), R=[qb, b_const], W=[pb])
            ri, rb = rr.next()
            r = rr.t[:, ri, 0:nt]
            E(ACT, lambda h: h.activation(out=r, in_=pt[:, 0:nt], func=AF.Sqrt, bias=eps_c, scale=1.0 / 128.0),
              R=[pb, b_const], W=[rb])
            E(DVE, lambda h: h.reciprocal(out=r, in_=r), W=[rb])
            E(DVE, lambda h: h.tensor_tensor(out=r, in0=r, in1=oT[:, hh, 0:nt], op=ALU.mult), R=b_oT[0:nck], W=[rb])
            E(DVE, lambda h: h.scalar_tensor_tensor(out=od[:, hh, 0:nt], in0=r, scalar=pcol(P_DNW), in1=sz[:, hh, 0:nt],
                                                    op0=ALU.mult, op1=ALU.mult), R=[rb, b_sz[hh], b_const], W=[b_od[hh]])

        chk("ph4")
        for j in range(8):
            if j % 4 == 0:
                wco_ = wload(16 + j // 4)
                wdn_ = wload(18 + j // 4)
            wt, wb = wco_
            pa_, pab2 = proj_chunk(wt, wb, j % 4, nt, cact, b_cact)
            wt, wb = wdn_
            pb_, pbb2 = proj_chunk(wt, wb, j % 4, nt, od, b_od)
            mi, mb = mtmp.next()
            mt_ = mtmp.t[:, mi, 0:nt]
            E(DVE, lambda h: h.tensor_tensor(out=mt_, in0=pa_, in1=gts[:, j, 0:nt], op=ALU.mult), R=[pab2, b_gts[j]], W=[mb])
            mi2, mb2 = mtmp.next()
            mt2 = mtmp.t[:, mi2, 0:nt]
            E(DVE, lambda h: h.tensor_tensor(out=mt2, in0=pb_, in1=gts[:, 8 + j, 0:nt], op=ALU.mult),
              R=[pbb2, b_gts[8 + j]], W=[mb2])
            E(POOL, lambda h: h.tensor_tensor(out=mT[:, j, 0:nt], in0=mt_, in1=mt2, op=ALU.add), R=[mb, mb2], W=[b_mT[j]])
        for half in range(2):
            wt, wb = wload(20 + half)
            for (s, rows) in subs:
                pt, pb = ps_next()
                for kc in range(8):
                    E(PE, lambda h: h.matmul(pt[0:rows, :], mT[:, kc, s * 128:s * 128 + rows], wt[:, kc, :],
                                             start=(kc == 0), stop=(kc == 7)), R=[b_mT[kc], wb], W=[pb], inc=(kc == 7))
                hsl = htok[0:rows, hs, s, half * 512:(half + 1) * 512]
                E(DVE, lambda h: h.tensor_tensor(out=hsl, in0=hsl, in1=pt[0:rows, :], op=ALU.add), R=[pb],
                  W=[b_htok[hs][s]])

        chk("ph5")
        norm_to_uT(hs, subs, nt, P_FFW)

        chk("ph6")
        ffn_list = [(jp, g) for jp in range(11) for g in range(4)]
        fs = {}

        def f_st0(i):
            jp, g = ffn_list[i]
            if g == 0:
                if jp % 2 == 0:
                    fs["dg", jp // 2] = dload(11 + jp // 2)
                fs["w"] = wload(22 + jp)
            wt, wb = fs["w"]
            j = 2 * jp + g // 2
            ch = j if g % 2 == 0 else NFF + j
            pp, ppb = proj_chunk(wt, wb, g, nt, uT, b_uT)
            pi, prb = preu.next()
            phb = b_preuh[pi]
            pr = preu.t[:, pi, :]
            E(POOL, lambda h: h.tensor_copy(out=pr[:, 0:2], in_=halu[:, ch, :]), R=[b_halu[ch]], W=[phb])
            if g % 2 == 0:
                E(ACT, lambda h: h.activation(out=pr[:, 2:2 + nt], in_=pp, func=AF.Copy), R=[ppb], W=[prb])
            else:
                E(DVE, lambda h: h.tensor_copy(out=pr[:, 2:2 + nt], in_=pp), R=[ppb], W=[prb])
            E(POOL, lambda h: h.tensor_copy(out=halu[:, ch, :], in_=pr[:, nt:nt + 2]), R=[prb], W=[b_halu[ch]])
            fs[i] = (pr, prb, phb)

        def f_st1(i):
            jp, g = ffn_list[i]
            j = 2 * jp + g // 2
            ch = j if g % 2 == 0 else NFF + j
            pr, prb, phb = fs[i]
            dgu, dgub = fs["dg", jp // 2]
            pc3, pc3b = ps_next()
            for t_ in range(3):
                mi_ = ((jp % 2) * 4 + g) * 3 + t_
                E(PE, lambda h: h.matmul(pc3[:, 0:nt], dgu[:, mi_ * 128:(mi_ + 1) * 128], pr[:, t_:t_ + nt],
                                         start=(t_ == 0), stop=(t_ == 2)), R=[dgub, prb, phb], W=[pc3b], inc=(t_ == 2))
            if g % 2 == 0:
                si_, sgb = sg.next()
                sga = sg.t[:, si_, 0:nt]
                E(ACT, lambda h: h.activation(out=sga, in_=pc3[:, 0:nt], func=AF.Silu, bias=pcol(P_FDB + ch)),
                  R=[pc3b, b_const], W=[sgb])
                fs["sg"] = (sga, sgb)
            else:
                sga, sgb = fs["sg"]
                E(DVE, lambda h: h.scalar_tensor_tensor(out=actb[:, j, 0:nt], in0=pc3[:, 0:nt], scalar=pcol(P_FDB + ch),
                                                        in1=sga, op0=ALU.add, op1=ALU.mult), R=[pc3b, sgb, b_const],
                  W=[b_act[j]])

        for i in range(44 + 2):
            if 0 <= i - 2 < 44:
                f_st1(i - 2)
            if i < 44:
                f_st0(i)

        chk("ph7")
        for half in range(2):
            pts = [ps_next() for _ in subs]
            for grp in range(3):
                wt, wb = wload(33 + half * 3 + grp)
                nk = 8 if grp < 2 else NFF - 16
                for si_, (s, rows) in enumerate(subs):
                    pt, pb = pts[si_]
                    for kk in range(nk):
                        kc = grp * 8 + kk
                        last = (kc == NFF - 1)
                        E(PE, lambda h: h.matmul(pt[0:rows, :], actb[:, kc, s * 128:s * 128 + rows], wt[:, kk, :],
                                                 start=(kc == 0), stop=last), R=[b_act[kc], wb], W=[pb],
                          inc=(kk == nk - 1))
            for si_, (s, rows) in enumerate(subs):
                pt, pb = pts[si_]
                hsl = htok[0:rows, hs, s, half * 512:(half + 1) * 512]
                E(DVE, lambda h: h.tensor_tensor(out=hsl, in0=hsl, in1=pt[0:rows, :], op=ALU.add), R=[pb],
                  W=[b_htok[hs][s]])

        chk("ph8")
        if not is_meta:
            x0 = (ti - 1) * T
            for (s, rows) in subs:
                si, sbuf_ = stat.next()
                st = stat.t[0:rows, si, :]
                hin = htok[0:rows, hs, s, :]
                E(ACT, lambda h: h.activation(out=junk[0:rows, :], in_=hin, func=AF.Square, accum_out=st[:, 0:1]),
                  R=[b_htok[hs][s]], W=[b_junk, sbuf_])
                E(ACT, lambda h: h.activation(out=st[:, 1:2], in_=st[:, 0:1], func=AF.Sqrt, bias=eps_c[0:rows],
                                              scale=1.0 / D), R=[b_const], W=[sbuf_])
                E(DVE, lambda h: h.reciprocal(out=st[:, 2:3], in_=st[:, 1:2]), W=[sbuf_])
                E(DVE, lambda h: h.scalar_tensor_tensor(out=hin, in0=hin, scalar=st[:, 2:3],
                                                        in1=tokp[0:rows, TP_NFW:TP_NFW + D], op0=ALU.mult, op1=ALU.mult),
                  R=[sbuf_, b_const], W=[b_htok[hs][s]])
                cx.dma(ACT, out_d[x0 + s * 128:x0 + s * 128 + rows, :], hin, R=[b_htok[hs][s]])

    try:
        chk("setup")
        do_tile(0, True)
        chk("meta")
        for ti in range(1, ntile + 1):
            do_tile(ti, False)
    except _Stop:
        pass
    cx.finish()
    es.close()
    return nc, cx


def _pack_weights(w_in, w_conf_out, w_dn_out, w_out, w_up, w_down):
    blocks = np.zeros((NBLK, 128, 8, 512), np.float32)

    def colblock(W, c0s):
        W3 = W.reshape(8, 128, W.shape[1])
        out = np.empty((128, 8, 512), np.float32)
        for g, c0 in enumerate(c0s):
            out[:, :, g * 128:(g + 1) * 128] = W3[:, :, c0:c0 + 128].transpose(1, 0, 2)
        return out

    b = 0
    for jp in range(4):
        j0, j1 = 2 * jp, 2 * jp + 1
        blocks[b] = colblock(w_in, [1024 + j0 * 128, j0 * 128, 1024 + j1 * 128, j1 * 128]); b += 1
    for base in (3072, 2048, 4096, 5120):
        for hb in range(2):
            blocks[b] = colblock(w_in, [base + (hb * 4 + g) * 128 for g in range(4)]); b += 1
    for base in (6160, 7184):
        for hb in range(2):
            blocks[b] = colblock(w_in, [base + (hb * 4 + g) * 128 for g in range(4)]); b += 1
    for W in (w_conf_out, w_dn_out):
        for hb in range(2):
            blocks[b] = colblock(W, [(hb * 4 + g) * 128 for g in range(4)]); b += 1
    for half in range(2):
        blocks[b] = w_out.reshape(8, 128, 1024)[:, :, half * 512:(half + 1) * 512].transpose(1, 0, 2); b += 1
    for jp in range(11):
        j0, j1 = 2 * jp, 2 * jp + 1
        blocks[b] = colblock(w_up, [j0 * 128, DFF + j0 * 128, j1 * 128, DFF + j1 * 128]); b += 1
    Wd = w_down.reshape(NFF, 128, 1024)
    for half in range(2):
        for grp in range(3):
            nk = 8 if grp < 2 else NFF - 16
            blocks[b][:, 0:nk, :] = Wd[grp * 8:grp * 8 + nk, :, half * 512:(half + 1) * 512].transpose(1, 0, 2); b += 1
    assert b == NBLK
    return blocks.reshape(NBLK, 128, 8 * 512)


def _pack_params(inp):
    par = np.zeros((128, NPAR), np.float32)

    def cols(v):
        return v.reshape(-1, 128).T

    par[:, P_MIXW:P_MIXW + 8] = cols(inp["norm_mix_w"][0])
    par[:, P_BGATE:P_BGATE + 16] = cols(inp["b_gate"][0])
    cdw = inp["conf_dw_w"][0]
    par[:, P_CDW:P_CDW + 8 * CK] = cdw.reshape(CK, 8, 128).transpose(2, 1, 0).reshape(128, 8 * CK)
    par[:, P_CDB:P_CDB + 8] = cols(inp["conf_dw_b"][0])
    par[:, P_LNW:P_LNW + 8] = cols(inp["conf_ln_w"][0])
    par[:, P_LNB:P_LNB + 8] = cols(inp["conf_ln_b"][0])
    dnc = inp["dn_conv_w"][0]
    par[:, P_DNC:P_DNC + 96] = dnc.reshape(4, 24, 128).transpose(2, 1, 0).reshape(128, 96)
    par[:, P_DNW] = inp["dn_norm_w"][0]
    par[:, P_FFW:P_FFW + 8] = cols(inp["norm_ffn_w"][0])
    fdw = inp["ffn_dw_w"][0]
    par[:, P_FDW:P_FDW + 132] = fdw.reshape(3, 44, 128).transpose(2, 1, 0).reshape(128, 132)
    par[:, P_FDB:P_FDB + 44] = cols(inp["ffn_dw_b"][0])
    fcols = par[:, P_FDW:P_FDW + 132].copy()
    for jp in range(11):
        for g in range(4):
            j = 2 * jp + g // 2
            ch = j if g % 2 == 0 else NFF + j
            par[:, P_FDWB + (jp * 4 + g) * 3:P_FDWB + (jp * 4 + g) * 3 + 3] = fcols[:, ch * 3:ch * 3 + 3]
    tokp = np.zeros((128, NTOKP), np.float32)
    tokp[:, TP_DTB:TP_DTB + 8] = inp["dn_dt_bias"][0][None, :]
    tokp[:, TP_ALOG:TP_ALOG + 8] = inp["dn_A_log"][0][None, :]
    tokp[:, TP_NFW:] = inp["norm_final_w"][None, :]
    return par, tokp


_CACHE = {}


def kernel(**inputs):
    inp = {k: np.asarray(v, dtype=np.float32) for k, v in inputs.items()}
    x = inp["x"]
    B, n_x, _ = x.shape
    wpack = _pack_weights(inp["w_in"][0], inp["w_conf_out"][0], inp["w_dn_out"][0], inp["w_out"][0], inp["w_up"][0],
                          inp["w_down"][0])
    wab = np.ascontiguousarray(inp["w_in"][0][:, 6144:6160].reshape(8, 128, 16).transpose(1, 0, 2)).reshape(128, 128)
    par, tokp = _pack_params(inp)
    if n_x not in _CACHE:
        _CACHE[n_x] = build(n_x)[0]
    nc = _CACHE[n_x]
    in_maps = []
    for b in range(B):
        in_maps.append({"x": np.ascontiguousarray(x[b]), "meta": inp["meta_tokens"], "wpack": wpack, "wab": wab,
                        "params": par, "tokpar": tokp})
    res = run_bass_kernel_spmd(nc, in_maps, core_ids=list(range(B)))
    return np.stack([np.asarray(r["out"], dtype=np.float32) for r in res.results], axis=0)
```

```python
from contextlib import ExitStack

import numpy as np
import concourse.bass as bass
import concourse.mybir as mybir
from concourse.bass_utils import run_bass_kernel_spmd

F32 = mybir.dt.float32
BF16 = mybir.dt.bfloat16
ALU = mybir.AluOpType
AF = mybir.ActivationFunctionType

D = 1024
NMETA = 16
CK = 31
DFF = 2816
NFF = DFF // 128
EPS = 1e-6
T = 256
NBLK = 39
NEG = -30000.0

P_MIXW = 0
P_BGATE = P_MIXW + 8
P_CDW = P_BGATE + 16
P_CDB = P_CDW + 8 * CK
P_LNW = P_CDB + 8
P_LNB = P_LNW + 8
P_DNC = P_LNB + 8
P_DNW = P_DNC + 24 * 4
P_FFW = P_DNW + 1
P_FDW = P_FFW + 8
P_FDB = P_FDW + 44 * 3
P_FDWB = P_FDB + 44
NPAR = P_FDWB + 132
NDG = 17
TP_DTB = 0
TP_ALOG = 8
TP_NFW = 16
NTOKP = 16 + D


class Eng:
    def __init__(self, name, h, sem, inc):
        self.name = name
        self.h = h
        self.sem = sem
        self.inc = inc
        self.count = 0
        self.known = {}


class Buf:
    __slots__ = ("name", "w", "r")

    def __init__(self, name):
        self.name = name
        self.w = None
        self.r = {}


class Ctx:
    def __init__(self, nc, es, ndma=24, nsw=40):
        self.nc = nc
        self.es = es
        mk = lambda n: es.enter_context(nc.semaphore(n))
        self.PE = Eng("PE", nc.tensor, mk("s_pe"), 1)
        self.ACT = Eng("ACT", nc.scalar, mk("s_act"), 1)
        self.DVE = Eng("DVE", nc.vector, mk("s_dve"), 1)
        self.POOL = Eng("POOL", nc.gpsimd, mk("s_pool"), 1)
        self.SP = Eng("SP", nc.sync, None, 0)
        self.dsems = [Eng("D%d" % i, None, mk("s_d%d" % i), 16) for i in range(ndma)]
        self.dsems_sw = [Eng("W%d" % i, None, mk("s_w%d" % i), 16) for i in range(nsw)]
        self.dnext = 0
        self.dnext_sw = 0
        self.ninstr = 0

    def _deps(self, R, W):
        d = {}
        for b in R:
            if b.w is not None:
                e, i = b.w
                if d.get(e, 0) < i:
                    d[e] = i
        for b in W:
            if b.w is not None:
                e, i = b.w
                if d.get(e, 0) < i:
                    d[e] = i
            for e, i in b.r.items():
                if d.get(e, 0) < i:
                    d[e] = i
        return d

    def _waits(self, eng, d):
        for src, idx in d.items():
            if src is eng and eng is self.PE:
                continue
            if eng.known.get(src, 0) >= idx:
                continue
            assert idx <= src.count, "dependency on un-signalled instruction of %s" % src.name
            eng.h.wait_ge(src.sem, idx * src.inc)
            eng.known[src] = idx

    def _mark(self, p, R, W):
        e, i = p
        for b in R:
            if b.r.get(e, 0) < i:
                b.r[e] = i
        for b in W:
            b.w = p
            b.r = {}

    def emit(self, eng, fn, R=(), W=(), inc=True):
        self._waits(eng, self._deps(R, W))
        ins = fn(eng.h)
        self.ninstr += 1
        if inc:
            eng.count += 1
            ins.then_inc(eng.sem, 1)
            idx = eng.count
        else:
            idx = eng.count + 1
        self._mark((eng, idx), R, W)
        return ins

    def dma(self, q, out, in_, R=(), W=()):
        self._waits(q, self._deps(R, W))
        if q is self.POOL:
            ds = self.dsems_sw[self.dnext_sw]
            self.dnext_sw = (self.dnext_sw + 1) % len(self.dsems_sw)
        else:
            ds = self.dsems[self.dnext]
            self.dnext = (self.dnext + 1) % len(self.dsems)
        if q.known.get(ds, 0) < ds.count:
            q.h.wait_ge(ds.sem, ds.count * 16)
            q.known[ds] = ds.count
        q.h.dma_start(out=out, in_=in_).then_inc(ds.sem, 16)
        ds.count += 1
        self.ninstr += 1
        self._mark((ds, ds.count), R, W)

    def finish(self):
        for ds in self.dsems + self.dsems_sw:
            if ds.count and self.SP.known.get(ds, 0) < ds.count:
                self.SP.h.wait_ge(ds.sem, ds.count * 16)


class Rot:
    def __init__(self, name, tens, n):
        self.t = tens
        self.n = n
        self.bufs = [Buf("%s%d" % (name, i)) for i in range(n)]
        self.i = -1

    def next(self):
        self.i = (self.i + 1) % self.n
        return self.i, self.bufs[self.i]


class _Stop(Exception):
    pass


def build(n_x=4096, dbg=False, stop_after=None):
    nc = bass.Bass("TRN2", target_bir_lowering=False)
    es = ExitStack()
    cx = Ctx(nc, es)
    PE, ACT, DVE, POOL, SP = cx.PE, cx.ACT, cx.DVE, cx.POOL, cx.SP
    E = cx.emit
    NS = T // 128
    NCK = T // 64
    ntile = n_x // T

    x_d = nc.dram_tensor("x", [n_x, D], F32, kind="ExternalInput").ap()
    meta_d = nc.dram_tensor("meta", [NMETA, D], F32, kind="ExternalInput").ap()
    wpack_d = nc.dram_tensor("wpack", [NBLK, 128, 8 * 512], F32, kind="ExternalInput").ap()
    wab_d = nc.dram_tensor("wab", [128, 8 * 16], F32, kind="ExternalInput").ap()
    par_d = nc.dram_tensor("params", [128, NPAR], F32, kind="ExternalInput").ap()
    tokp_d = nc.dram_tensor("tokpar", [128, NTOKP], F32, kind="ExternalInput").ap()
    out_d = nc.dram_tensor("out", [n_x, D], F32, kind="ExternalOutput").ap()
    wbf_d = nc.dram_tensor("wbf", [NBLK, 128, 8 * 512], BF16, kind="Internal").ap()
    dg_d = nc.dram_tensor("dgd", [NDG, 128, 8 * 512], BF16, kind="Internal").ap()
    dbg_d = {}

    def sb(name, shape, dt):
        return es.enter_context(nc.sbuf_tensor(name, shape, dt))

    htok = sb("htok", [128, 2, NS, D], F32)
    b_htok = [[Buf("htok%d_%d" % (i, s)) for s in range(NS)] for i in range(2)]
    xn = Rot("xn", sb("xn", [128, 1, D], F32), 1)
    junk = sb("junk", [128, D], BF16)
    b_junk = Buf("junk")
    stat = Rot("stat", sb("stat", [128, 8, 4], F32), 8)
    uT = sb("uT", [128, 8, T], BF16)
    b_uT = [Buf("uT%d" % c) for c in range(8)]
    cbuf = sb("cbuf", [128, 8, 30 + T], BF16)
    b_c = [Buf("c%d" % c) for c in range(8)]
    ctmp = sb("ctmp", [128, 32], BF16)
    b_ctmp = Buf("ctmp")
    ybuf = sb("ybuf", [128, 8, T], F32)
    b_y = [Buf("y%d" % c) for c in range(8)]
    accA = Rot("accA", sb("accA", [128, 2, T], F32), 2)
    ybf = Rot("ybf", sb("ybf", [128, 2, T], BF16), 2)
    ysq = Rot("ysq", sb("ysq", [128, 2, T], BF16), 2)
    lnt = sb("lnt", [128, 5, T], F32)
    b_lnt = [Buf("lnt%d" % i) for i in range(5)]
    cact = sb("cact", [128, 8, T], BF16)
    b_cact = [Buf("cact%d" % c) for c in range(8)]
    sig = Rot("sig", sb("sig", [128, 2, T], F32), 2)
    pre = Rot("pre", sb("pre", [128, 3, 3 + T], BF16), 3)
    halq = sb("halq", [128, 24, 3], BF16)
    b_halq = [Buf("halq%d" % c) for c in range(24)]
    b_preh = [Buf("preh%d" % c) for c in range(3)]
    b_preuh = [Buf("preuh%d" % c) for c in range(4)]
    sqb = Rot("sqb", sb("sqb", [128, 2, T], BF16), 2)
    rr = Rot("rr", sb("rr", [128, 2, T], F32), 2)
    qT = sb("qT", [128, 8, T], BF16)
    kT = sb("kT", [128, 8, T], BF16)
    vT = sb("vT", [128, 8, T], BF16)
    qdT = sb("qdT", [128, 8, T], BF16)
    b_qT = [Buf("qT%d" % c) for c in range(8)]
    b_kT = [Buf("kT%d" % c) for c in range(8)]
    b_vT = [Buf("vT%d" % c) for c in range(8)]
    b_qdT = [Buf("qdT%d" % c) for c in range(NCK)]
    sz = sb("sz", [128, 8, T], BF16)
    b_sz = [Buf("sz%d" % c) for c in range(8)]
    gts = sb("gts", [128, 16, T], BF16)
    b_gts = [Buf("gts%d" % c) for c in range(16)]
    oT = sb("oT", [128, 8, T], F32)
    b_oT = [Buf("oT%d" % c) for c in range(NCK)]
    od = sb("od", [128, 8, T], BF16)
    b_od = [Buf("od%d" % c) for c in range(8)]
    mT = sb("mT", [128, 8, T], BF16)
    b_mT = [Buf("mT%d" % c) for c in range(8)]
    mtmp = Rot("mtmp", sb("mtmp", [128, 2, T], F32), 2)
    actb = sb("actb", [128, NFF, T], BF16)
    b_act = [Buf("act%d" % c) for c in range(NFF)]
    preu = Rot("preu", sb("preu", [128, 4, 2 + T], BF16), 4)
    halu = sb("halu", [128, 44, 2], BF16)
    b_halu = [Buf("halu%d" % c) for c in range(44)]
    sg = Rot("sg", sb("sg", [128, 2, T], F32), 2)
    abt = sb("abt", [64, NCK, 16], F32)
    b_abt = Buf("abt")
    gtok = sb("gtok", [64, NCK, 8], F32)
    beta = sb("beta", [64, NCK, 8], F32)
    b_g = Buf("g")
    b_beta = Buf("beta")
    GU = Rot("GU", sb("GU", [64, 1, 512], F32), 1)
    gcol = Rot("gcol", sb("gcol", [64, 2, 32], F32), 2)
    D1 = Rot("D1", sb("D1", [64, 1, 512], F32), 1)
    D2 = Rot("D2", sb("D2", [64, 1, 512], F32), 1)
    Abf = Rot("Abf", sb("Abf", [64, 3, 512], BF16), 3)
    Mbf = Rot("Mbf", sb("Mbf", [64, 3, 512], BF16), 3)
    Pbf = Rot("Pbf", sb("Pbf", [64, 2, 512], BF16), 2)
    Pf = Rot("Pf", sb("Pf", [64, 1, 512], F32), 1)
    AqkT = Rot("AqkT", sb("AqkT", [64, 1, 512], BF16), 1)
    TTb = Rot("TTb", sb("TTb", [64, 1, 512], BF16), 1)
    kw = Rot("kw", sb("kw", [64, 1, 1024], BF16), 1)
    kd = Rot("kd", sb("kd", [64, 1, 1024], BF16), 1)
    vb = Rot("vb", sb("vb", [64, 1, 1024], BF16), 1)
    vn = Rot("vn", sb("vn", [64, 1, 1024], BF16), 1)
    wTn = Rot("wTn", sb("wTn", [128, 1, 512], BF16), 1)
    Eq = Rot("Eq", sb("Eq", [128, 1, 512], F32), 1)
    S = sb("S", [128, 8, 128], F32)
    Sbf = sb("Sbf", [128, 8, 128], BF16)
    b_S = [Buf("S%d" % h) for h in range(8)]
    b_Sbf = [Buf("Sbf%d" % h) for h in range(8)]
    NRING = 5
    wring = Rot("wring", sb("wring", [128, NRING, 8 * 512], BF16), NRING)
    wab = sb("wab_sb", [128, 8, 16], BF16)
    wabf = sb("wabf", [128, 8 * 16], F32)
    b_wab = Buf("wab")
    ident_f = sb("ident_f", [128, 128], F32)
    ident_b = sb("ident_b", [128, 128], BF16)
    ones_b = sb("ones_b", [128, 128], BF16)
    ones_f = sb("ones_f", [64, 128], F32)
    Umat = sb("Umat", [64, 64], F32)
    par = sb("par_sb", [128, NPAR], F32)
    tokp = sb("tokp", [128, NTOKP], F32)
    cst = sb("cst", [128, 4], F32)
    Esel = sb("Esel", [128, 16, 16], BF16)
    identq = sb("identq", [16, 16], F32)
    rsn = sb("rsn", [16, T], F32)
    b_rsn = Buf("rsn")
    b_const = Buf("const")
    b_wbf = [Buf("wbf%d" % i) for i in range(NBLK)]
    b_dg = [Buf("dg%d" % i) for i in range(NDG)]

    ps = es.enter_context(nc.psum_tensor("ps", [128, 8, 512], F32))
    b_ps = [Buf("ps%d" % i) for i in range(8)]
    ps_state = {"i": -1}

    def ps_next():
        ps_state["i"] = (ps_state["i"] + 1) % 6
        i = ps_state["i"]
        return ps[:, i, :], b_ps[i]

    cx.dma(SP, par[:, :], par_d[:, :], W=[b_const])
    cx.dma(SP, tokp[:, :], tokp_d[:, :], W=[b_const])
    cx.dma(SP, wabf[:, :], wab_d[:, :], W=[b_wab])
    for b in range(NBLK):
        cx.dma(POOL, wbf_d[b], wpack_d[b], W=[b_wbf[b]])

    def pool_c(fn):
        E(POOL, fn, W=[b_const])

    pool_c(lambda h: h.memset(ident_f[:], 0.0))
    pool_c(lambda h: h.affine_select(out=ident_f[:], in_=ident_f[:], pattern=[[-1, 128]],
                                     compare_op=ALU.not_equal, fill=1.0, base=0, channel_multiplier=1))
    pool_c(lambda h: h.tensor_copy(out=ident_b[:], in_=ident_f[:]))
    pool_c(lambda h: h.memset(ones_b[:], 1.0))
    pool_c(lambda h: h.memset(ones_f[:], 1.0))
    pool_c(lambda h: h.memset(Umat[:], 1.0))
    pool_c(lambda h: h.affine_select(out=Umat[:], in_=Umat[:], pattern=[[1, 64]],
                                     compare_op=ALU.is_ge, fill=0.0, base=0, channel_multiplier=-1))
    pool_c(lambda h: h.memset(Esel[:], 1.0))
    pool_c(lambda h: h.affine_select(out=Esel[:], in_=Esel[:], pattern=[[1, 16], [-1, 16]], compare_op=ALU.is_equal,
                                     fill=0.0, base=0, channel_multiplier=0))
    pool_c(lambda h: h.tensor_copy(out=identq[:], in_=ident_f[0:16, 0:16]))
    pool_c(lambda h: h.tensor_scalar(out=identq[:, 8:16], in0=identq[:, 8:16], scalar1=128.0 ** -0.5, scalar2=None,
                                     op0=ALU.mult))
    pool_c(lambda h: h.memset(cst[:, 0:1], EPS))
    pool_c(lambda h: h.memset(cst[:, 1:2], 1.0))
    pool_c(lambda h: h.memset(cbuf[:], 0.0))
    pool_c(lambda h: h.memset(halq[:], 0.0))
    pool_c(lambda h: h.memset(halu[:], 0.0))
    pool_c(lambda h: h.memset(S[:], 0.0))
    pool_c(lambda h: h.memset(Sbf[:], 0.0))
    E(POOL, lambda h: h.tensor_copy(out=wab[:].rearrange("p a b -> p (a b)"), in_=wabf[:]), R=[], W=[b_wab])
    E(ACT, lambda h: h.activation(out=tokp[:, TP_ALOG:TP_ALOG + 8], in_=tokp[:, TP_ALOG:TP_ALOG + 8], func=AF.Exp),
      W=[b_const])
    E(DVE, lambda h: h.tensor_scalar(out=tokp[:, TP_ALOG:TP_ALOG + 8], in0=tokp[:, TP_ALOG:TP_ALOG + 8],
                                     scalar1=-1.0, scalar2=None, op0=ALU.mult), W=[b_const])
    E(DVE, lambda h: h.tensor_scalar(out=par[:, P_CDW:P_CDW + 8 * CK], in0=par[:, P_CDW:P_CDW + 8 * CK], scalar1=0.5,
                                     scalar2=None, op0=ALU.mult), W=[b_const])
    E(DVE, lambda h: h.tensor_scalar(out=par[:, P_BGATE:P_BGATE + 16], in0=par[:, P_BGATE:P_BGATE + 16], scalar1=0.5,
                                     scalar2=None, op0=ALU.mult), W=[b_const])
    E(DVE, lambda h: h.tensor_scalar(out=par[:, P_DNW:P_DNW + 1], in0=par[:, P_DNW:P_DNW + 1], scalar1=0.5,
                                     scalar2=None, op0=ALU.mult), W=[b_const])
    all_state = b_c + b_halq + b_halu + b_S + b_Sbf
    for b in all_state:
        b.w = b_const.w

    neg_reg = nc.gpsimd.to_reg(NEG)
    def build_dg(d, poff, nm):
        i, wb_ = wring.next()
        slot = wring.t[:, i, 0:nm * 128].rearrange("p (m c) -> p m c", c=128)
        E(DVE, lambda h: h.tensor_tensor(out=slot, in0=ident_b[:, :].unsqueeze(1).to_broadcast([128, nm, 128]),
                                         in1=par[:, poff:poff + nm].unsqueeze(2).to_broadcast([128, nm, 128]), op=ALU.mult),
          R=[b_const], W=[wb_])
        cx.dma(SP, dg_d[d][:, 0:nm * 128], wring.t[:, i, 0:nm * 128], R=[wb_], W=[b_dg[d]])

    for j in range(8):
        build_dg(j, P_CDW + j * CK, CK)
    for kk_ in range(3):
        build_dg(8 + kk_, P_DNC + kk_ * 32, 32)
    for d_ in range(6):
        build_dg(11 + d_, P_FDWB + d_ * 24, min(24, 132 - d_ * 24))

    def dload(d):
        nm = CK if d < 8 else (32 if d < 11 else min(24, 132 - (d - 11) * 24))
        i, wb_ = wring.next()
        cx.dma(SP, wring.t[:, i, 0:nm * 128], dg_d[d][:, 0:nm * 128], R=[b_dg[d]], W=[wb_])
        return wring.t[:, i, :], wb_

    eps_c = cst[:, 0:1]
    one_c = cst[:, 1:2]

    def pcol(off, rows=128):
        return par[0:rows, off:off + 1]

    def wload(bidx):
        i, wb = wring.next()
        cx.dma(SP, wring.t[:, i, :], wbf_d[bidx], R=[b_wbf[bidx]], W=[wb])
        return wring.t[:, i, :].rearrange("p (k n) -> p k n", k=8), wb

    def norm_stats(hs, subs):
        out = []
        for (s, rows) in subs:
            si, sbuf_ = stat.next()
            st = stat.t[0:rows, si, :]
            hin = htok[0:rows, hs, s, :]
            E(ACT, lambda h: h.activation(out=junk[0:rows, :], in_=hin, func=AF.Square, accum_out=st[:, 0:1]),
              R=[b_htok[hs][s]], W=[b_junk, sbuf_])
            E(ACT, lambda h: h.activation(out=st[:, 1:2], in_=st[:, 0:1], func=AF.Sqrt, bias=eps_c[0:rows], scale=1.0 / D),
              R=[b_const], W=[sbuf_])
            E(DVE, lambda h: h.reciprocal(out=st[:, 2:3], in_=st[:, 1:2]), W=[sbuf_])
            out.append((st, sbuf_))
        return out

    def norm_apply(hs, subs, nt, woff, stats):
        for (s, rows), (st, sbuf_) in zip(subs, stats):
            hin = htok[0:rows, hs, s, :]
            xi, xb = xn.next()
            xt = xn.t[0:rows, xi, :]
            E(DVE, lambda h: h.tensor_scalar(out=xt, in0=hin, scalar1=st[:, 2:3], scalar2=None, op0=ALU.mult),
              R=[b_htok[hs][s], sbuf_], W=[xb])
            for c in range(8):
                pt, pb = ps_next()
                E(PE, lambda h: h.transpose(pt[:, 0:rows], xt[:, c * 128:(c + 1) * 128], ident_f[0:rows, 0:rows]),
                  R=[xb, b_const], W=[pb])
                tgt = uT[:, c, s * 128:s * 128 + rows]
                if c % 2 == 0:
                    E(ACT, lambda h: h.activation(out=tgt, in_=pt[:, 0:rows], func=AF.Copy, scale=pcol(woff + c)),
                      R=[pb, b_const], W=[b_uT[c]])
                else:
                    E(DVE, lambda h: h.tensor_scalar(out=tgt, in0=pt[:, 0:rows], scalar1=pcol(woff + c), scalar2=None,
                                                     op0=ALU.mult), R=[pb, b_const], W=[b_uT[c]])

    def norm_to_uT(hs, subs, nt, woff):
        norm_apply(hs, subs, nt, woff, norm_stats(hs, subs))

    def proj_chunk(wt, wb, g, nt, src, b_src, nk=8):
        pt, pb = ps_next()
        for kc in range(nk):
            E(PE, lambda h: h.matmul(pt[:, 0:nt], wt[:, kc, g * 128:(g + 1) * 128], src[:, kc, 0:nt],
                                     start=(kc == 0), stop=(kc == nk - 1)),
              R=[wb, b_src[kc]], W=[pb], inc=(kc == nk - 1))
        return pt[:, 0:nt], pb

    def l2_rstd(src_ap, b_src, nt, scale_in):
        qi, qb = sqb.next()
        sq = sqb.t[:, qi, 0:nt]
        E(ACT, lambda h: h.activation(out=sq, in_=src_ap, func=AF.Square), R=[b_src], W=[qb])
        pt, pb = ps_next()
        E(PE, lambda h: h.matmul(pt[:, 0:nt], ones_b[:, :], sq, start=True, stop=True), R=[qb, b_const], W=[pb])
        ri, rb = rr.next()
        r = rr.t[:, ri, 0:nt]
        E(ACT, lambda h: h.activation(out=r, in_=pt[:, 0:nt], func=AF.Sqrt, bias=eps_c, scale=scale_in),
          R=[pb, b_const], W=[rb])
        E(DVE, lambda h: h.reciprocal(out=r, in_=r), W=[rb])
        return r, rb

    def chk(name):
        if stop_after == name:
            raise _Stop()

    reg_subs = [(s_, 128) for s_ in range(NS)]

    def load_x(ti):
        hs_ = ti % 2
        x0_ = (ti - 1) * T
        for (s, rows) in reg_subs:
            cx.dma(SP, htok[:, hs_, s, :], x_d[x0_ + s * 128:x0_ + (s + 1) * 128, :], W=[b_htok[hs_][s]])

    def do_tile(ti, is_meta, has_next):
        hs = ti % 2
        if is_meta:
            nt = NMETA
            subs = [(0, NMETA)]
            chunks = [(0, NMETA)]
            cx.dma(SP, htok[0:NMETA, hs, 0, :], meta_d[:, :], W=[b_htok[hs][0]])
            norm_to_uT(hs, subs, nt, P_MIXW)
        else:
            nt = T
            subs = reg_subs
            chunks = [(c * 64, 64) for c in range(NCK)]

        chk("ph1")
        for jp in range(4):
            wt, wb = wload(jp)
            for jj in range(2):
                j = 2 * jp + jj
                pg, pgb = proj_chunk(wt, wb, 2 * jj, nt, uT, b_uT)
                gi, gb = sig.next()
                sg_ap = sig.t[:, gi, 0:nt]
                E(ACT, lambda h: h.activation(out=sg_ap, in_=pg, func=AF.Tanh, scale=0.5), R=[pgb], W=[gb])
                pv, pvb = proj_chunk(wt, wb, 2 * jj + 1, nt, uT, b_uT)
                E(DVE, lambda h: h.scalar_tensor_tensor(out=cbuf[:, j, 30:30 + nt], in0=sg_ap, scalar=1.0, in1=pv,
                                                        op0=ALU.add, op1=ALU.mult), R=[pvb, gb], W=[b_c[j]])
        bg = []

        def bg_step(n=1):
            for _ in range(n):
                if bg:
                    bg.pop(0)()

        def conv31_item(j):
            dg, dgb = dload(j)
            pt, pb = ps_next()
            for t_ in range(CK):
                E(PE, lambda h: h.matmul(pt[:, 0:nt], dg[:, t_ * 128:(t_ + 1) * 128], cbuf[:, j, t_:t_ + nt],
                                         start=(t_ == 0), stop=(t_ == CK - 1)), R=[dgb, b_c[j]], W=[pb], inc=(t_ == CK - 1))
            E(ACT, lambda h: h.activation(out=ybuf[:, j, 0:nt], in_=pt[:, 0:nt], func=AF.Identity, bias=pcol(P_CDB + j)),
              R=[pb, b_const], W=[b_y[j]])
            E(POOL, lambda h: h.tensor_copy(out=ctmp[:, 0:30], in_=cbuf[:, j, nt:nt + 30]), R=[b_c[j]], W=[b_ctmp])
            E(POOL, lambda h: h.tensor_copy(out=cbuf[:, j, 0:30], in_=ctmp[:, 0:30]), R=[b_ctmp], W=[b_c[j]])

        for j in range(8):
            bg.append(lambda j=j: conv31_item(j))

        qkv_list = [(kind, hh) for kind in ("k", "q", "v") for hh in range(8)]
        kbase = {"q": 0, "k": 8, "v": 16}
        wblk0 = {"k": 4, "q": 6, "v": 8}
        dgidx = {"q": 8, "k": 9, "v": 10}
        qs = {}

        def q_st0(i):
            kind, hh = qkv_list[i]
            if hh == 0:
                qs["dg", kind] = dload(dgidx[kind])
            if hh % 4 == 0:
                qs["w"] = wload(wblk0[kind] + hh // 4)
            wt, wb = qs["w"]
            ch = kbase[kind] + hh
            pp, ppb = proj_chunk(wt, wb, hh % 4, nt, uT, b_uT)
            pi, prb = pre.next()
            phb = b_preh[pi]
            pr = pre.t[:, pi, :]
            E(POOL, lambda h: h.tensor_copy(out=pr[:, 0:3], in_=halq[:, ch, :]), R=[b_halq[ch]], W=[phb])
            if kind == "v":
                E(ACT, lambda h: h.activation(out=pr[:, 3:3 + nt], in_=pp, func=AF.Copy), R=[ppb], W=[prb])
            else:
                E(DVE, lambda h: h.tensor_copy(out=pr[:, 3:3 + nt], in_=pp), R=[ppb], W=[prb])
            E(POOL, lambda h: h.tensor_copy(out=halq[:, ch, :], in_=pr[:, nt:nt + 3]), R=[prb], W=[b_halq[ch]])
            qs[i] = (pr, prb, phb)

        def q_st1(i):
            kind, hh = qkv_list[i]
            pr, prb, phb = qs[i]
            dgq, dgqb = qs["dg", kind]
            pc4, cb = ps_next()
            ca = pc4[:, 0:nt]
            for t_ in range(4):
                mi_ = hh * 4 + t_
                E(PE, lambda h: h.matmul(ca, dgq[:, mi_ * 128:(mi_ + 1) * 128], pr[:, t_:t_ + nt], start=(t_ == 0),
                                         stop=(t_ == 3)), R=[dgqb, prb, phb], W=[cb], inc=(t_ == 3))
            if kind == "v":
                E(ACT, lambda h: h.activation(out=vT[:, hh, 0:nt], in_=ca, func=AF.Silu), R=[cb], W=[b_vT[hh]])
            else:
                dst, dstb = (kT, b_kT) if kind == "k" else (qT, b_qT)
                E(ACT, lambda h: h.activation(out=dst[:, hh, 0:nt], in_=ca, func=AF.Silu), R=[cb], W=[dstb[hh]])
                qi, qb = sqb.next()
                sq = sqb.t[:, qi, 0:nt]
                E(ACT, lambda h: h.activation(out=sq, in_=dst[:, hh, 0:nt], func=AF.Square), R=[dstb[hh]], W=[qb])
                qs[i] = (sq, qb)

        def q_st2(i):
            kind, hh = qkv_list[i]
            if kind == "v":
                return
            sq, qb = qs[i]
            idx = hh if kind == "k" else 8 + hh
            first = (kind == "k" and hh == 0)
            last = (kind == "q" and hh == 7)
            E(PE, lambda h: h.matmul(ps[0:16, 7, 0:nt], Esel[:, idx, :], sq, start=first, stop=last), R=[qb, b_const],
              W=[b_ps[7]])

        for i in range(24 + 3):
            if 0 <= i - 2 < 24:
                q_st1(i - 2)
            if 0 <= i - 3 < 24:
                q_st2(i - 3)
            if i < 24:
                q_st0(i)
        E(ACT, lambda h: h.activation(out=rsn[:, 0:nt], in_=ps[0:16, 7, 0:nt], func=AF.Sqrt, bias=eps_c[0:16], scale=1.0),
          R=[b_ps[7], b_const], W=[b_rsn])
        E(DVE, lambda h: h.reciprocal(out=rsn[:, 0:nt], in_=rsn[:, 0:nt]), W=[b_rsn])
        zg_state = {}

        def z_item(hh):
            if hh % 4 == 0:
                zg_state["w"] = wload(10 + hh // 4)
            wt, wb = zg_state["w"]
            pp, ppb = proj_chunk(wt, wb, hh % 4, nt, uT, b_uT)
            gi_, gb_ = sig.next()
            th = sig.t[:, gi_, 0:nt]
            E(ACT, lambda h: h.activation(out=th, in_=pp, func=AF.Tanh, scale=0.5), R=[ppb], W=[gb_])
            E(DVE, lambda h: h.scalar_tensor_tensor(out=sz[:, hh, 0:nt], in0=th, scalar=1.0, in1=pp, op0=ALU.add,
                                                    op1=ALU.mult), R=[ppb, gb_], W=[b_sz[hh]])

        def gate_item(jg):
            if jg % 4 == 0:
                zg_state["w"] = wload(12 + jg // 4)
            wt, wb = zg_state["w"]
            pp, ppb = proj_chunk(wt, wb, jg % 4, nt, uT, b_uT)
            E(ACT, lambda h: h.activation(out=gts[:, jg, 0:nt], in_=pp, func=AF.Tanh, bias=pcol(P_BGATE + jg),
                                          scale=0.5), R=[ppb, b_const], W=[b_gts[jg]])

        for hh in range(8):
            bg.append(lambda hh=hh: z_item(hh))
        for jg in range(16):
            bg.append(lambda jg=jg: gate_item(jg))
        pab, pabb = ps_next()
        pab3 = pab.rearrange("p (c n) -> p c n", n=16)
        for ck, (o, cl) in enumerate(chunks):
            for kc in range(8):
                E(PE, lambda h: h.matmul(pab3[0:cl, ck, :], uT[:, kc, o:o + cl], wab[:, kc, :], start=(kc == 0),
                                         stop=(kc == 7)), R=[b_uT[kc], b_wab], W=[pabb], inc=(kc == 7))
        nck = len(chunks)
        cl0 = chunks[0][1]
        dtb_b = tokp[0:cl0, TP_DTB:TP_DTB + 8].unsqueeze(1).to_broadcast([cl0, nck, 8])
        negA_b = tokp[0:cl0, TP_ALOG:TP_ALOG + 8].unsqueeze(1).to_broadcast([cl0, nck, 8])
        E(DVE, lambda h: h.tensor_tensor(out=abt[0:cl0, 0:nck, 0:8], in0=pab3[0:cl0, 0:nck, 0:8], in1=dtb_b, op=ALU.add),
          R=[pabb, b_const], W=[b_abt])
        E(ACT, lambda h: h.activation(out=abt[0:cl0, 0:nck, 0:8], in_=abt[0:cl0, 0:nck, 0:8], func=AF.Exp), W=[b_abt])
        E(ACT, lambda h: h.activation(out=abt[0:cl0, 0:nck, 0:8], in_=abt[0:cl0, 0:nck, 0:8], func=AF.Ln,
                                      bias=one_c[0:cl0]), R=[b_const], W=[b_abt])
        E(DVE, lambda h: h.tensor_tensor(out=gtok[0:cl0, 0:nck, :], in0=abt[0:cl0, 0:nck, 0:8], in1=negA_b, op=ALU.mult),
          R=[b_abt, b_const], W=[b_g])
        E(ACT, lambda h: h.activation(out=beta[0:cl0, 0:nck, :], in_=pab3[0:cl0, 0:nck, 8:16], func=AF.Tanh, scale=0.5),
          R=[pabb], W=[b_beta])
        E(DVE, lambda h: h.tensor_scalar(out=beta[0:cl0, 0:nck, :], in0=beta[0:cl0, 0:nck, :], scalar1=0.5, scalar2=0.5,
                                         op0=ALU.mult, op1=ALU.add), W=[b_beta])

        for idx in range(16):
            hh = idx % 8
            dst, dstb = (kT, b_kT) if idx < 8 else (qT, b_qT)
            pt, pb = ps_next()
            E(PE, lambda h: h.matmul(pt[:, 0:nt], identq[:, idx:idx + 1].to_broadcast([16, 128]), rsn[:, 0:nt], start=True,
                                     stop=True), R=[b_rsn, b_const], W=[pb])
            E(DVE, lambda h: h.tensor_tensor(out=dst[:, hh, 0:nt], in0=dst[:, hh, 0:nt], in1=pt[:, 0:nt], op=ALU.mult),
              R=[pb], W=[dstb[hh]])
        chk("ph2")
        chk("ph3")
        for ck, (o, cl) in enumerate(chunks):
            nlev = 5 if cl == 64 else 3
            W8 = 8 * cl
            gi, gub = GU.next()
            gu3 = GU.t[0:cl, gi, 0:W8].rearrange("p (h j) -> p h j", h=8)
            gsl = gtok[0:cl, ck, :]
            E(DVE, lambda h: h.tensor_tensor(out=gu3, in0=gsl.unsqueeze(2).to_broadcast([cl, 8, cl]),
                                             in1=Umat[0:cl, 0:cl].unsqueeze(1).to_broadcast([cl, 8, cl]), op=ALU.mult),
              R=[b_g, b_const], W=[gub])
            bps = ps[:, 6, 0:W8]
            bps3 = bps.rearrange("p (h j) -> p h j", h=8)
            E(PE, lambda h: h.matmul(bps, ones_f[0:cl, :], GU.t[0:cl, gi, 0:W8], start=True, stop=True),
              R=[gub, b_const], W=[b_ps[6]])
            pc, pcb = ps_next()
            E(PE, lambda h: h.matmul(pc[0:cl, 0:8], Umat[0:cl, 0:cl], gsl, start=True, stop=True), R=[b_g, b_const], W=[pcb])
            ci_, gcb = gcol.next()
            gc = gcol.t[0:cl, ci_, :]
            E(ACT, lambda h: h.activation(out=gc[:, 0:8], in_=pc[0:cl, 0:8], func=AF.Copy), R=[pcb], W=[gcb])
            E(ACT, lambda h: h.activation(out=gc[:, 8:16], in_=pc[0:cl, 0:8], func=AF.Exp), R=[pcb], W=[gcb])
            E(DVE, lambda h: h.tensor_tensor(out=gc[:, 16:24], in0=gc[:, 8:16], in1=beta[0:cl, ck, :], op=ALU.mult),
              R=[b_beta], W=[gcb])
            E(DVE, lambda h: h.tensor_tensor(out=gc[:, 24:32], in0=bps3[0:cl, :, cl - 1], in1=gc[:, 0:8], op=ALU.subtract),
              R=[b_ps[6]], W=[gcb])
            E(ACT, lambda h: h.activation(out=gc[:, 24:32], in_=gc[:, 24:32], func=AF.Exp), W=[gcb])
            d1i, d1b = D1.next()
            d2i, d2b = D2.next()
            d1 = D1.t[0:cl, d1i, 0:W8].rearrange("p (h j) -> p h j", h=8)
            d2 = D2.t[0:cl, d2i, 0:W8].rearrange("p (h j) -> p h j", h=8)
            gcb3 = gc[:, 0:8].unsqueeze(2).to_broadcast([cl, 8, cl])
            E(DVE, lambda h: h.tensor_tensor(out=d1, in0=gcb3, in1=bps3[0:cl], op=ALU.subtract), R=[gcb, b_ps[6]], W=[d1b])
            E(DVE, lambda h: h.tensor_tensor(out=d2, in0=bps3[0:cl], in1=gcb3, op=ALU.subtract), R=[gcb, b_ps[6]], W=[d2b])
            E(POOL, lambda h: h.affine_select(out=d1, in_=d1, pattern=[[0, 8], [-1, cl]], compare_op=ALU.is_gt, fill=neg_reg,
                                              base=0, channel_multiplier=1), W=[d1b])
            E(POOL, lambda h: h.affine_select(out=d2, in_=d2, pattern=[[0, 8], [1, cl]], compare_op=ALU.is_ge, fill=neg_reg,
                                              base=0, channel_multiplier=-1), W=[d2b])
            E(ACT, lambda h: h.activation(out=d1, in_=d1, func=AF.Exp), W=[d1b])
            E(ACT, lambda h: h.activation(out=d2, in_=d2, func=AF.Exp), W=[d2b])
            E(DVE, lambda h: h.tensor_tensor(out=d1, in0=d1, in1=beta[0:cl, ck, :].unsqueeze(2).to_broadcast([cl, 8, cl]),
                                              op=ALU.mult), R=[b_beta], W=[d1b])
            chk("d1")
            pkk, pkkb = ps_next()
            pqk, pqkb = ps_next()
            for hh in range(8):
                E(PE, lambda h: h.matmul(pkk[0:cl, hh * cl:(hh + 1) * cl], kT[:, hh, o:o + cl], kT[:, hh, o:o + cl],
                                         start=True, stop=True), R=[b_kT[hh]], W=[pkkb], inc=(hh == 7))
            for hh in range(8):
                E(PE, lambda h: h.matmul(pqk[0:cl, hh * cl:(hh + 1) * cl], kT[:, hh, o:o + cl], qT[:, hh, o:o + cl],
                                         start=True, stop=True), R=[b_kT[hh], b_qT[hh]], W=[pqkb], inc=(hh == 7))
            a_i, a_b = Abf.next()
            A0 = Abf.t[0:cl, a_i, 0:W8]
            E(DVE, lambda h: h.tensor_tensor(out=A0, in0=pkk[0:cl, 0:W8], in1=D1.t[0:cl, d1i, 0:W8], op=ALU.mult),
              R=[pkkb, d1b], W=[a_b])
            q_i, q_b = AqkT.next()
            AQ = AqkT.t[0:cl, q_i, 0:W8]
            E(DVE, lambda h: h.tensor_tensor(out=AQ, in0=pqk[0:cl, 0:W8], in1=D2.t[0:cl, d2i, 0:W8], op=ALU.mult),
              R=[pqkb, d2b], W=[q_b])
            chk("d2")
            bg_step()
            pmt, pmtb = ps_next()
            for hh in range(8):
                E(PE, lambda h: h.matmul(pmt[0:cl, hh * cl:(hh + 1) * cl], A0[:, hh * cl:(hh + 1) * cl], ident_b[0:cl, 0:cl],
                                         start=True, stop=True), R=[a_b, b_const], W=[pmtb], inc=(hh == 7))
            m_i, m_b = Mbf.next()
            M0 = Mbf.t[0:cl, m_i, 0:W8]
            E(ACT, lambda h: h.activation(out=M0, in_=pmt[0:cl, 0:W8], func=AF.Copy), R=[pmtb], W=[m_b])
            chk("d3")
            bg_step()
            pf_i, pf_b = Pf.next()
            PF = Pf.t[0:cl, pf_i, 0:W8]
            E(DVE, lambda h: h.tensor_tensor(out=PF.rearrange("p (h j) -> p h j", h=8),
                                             in0=ident_f[0:cl, 0:cl].unsqueeze(1).to_broadcast([cl, 8, cl]),
                                             in1=M0.rearrange("p (h j) -> p h j", h=8), op=ALU.subtract),
              R=[m_b, b_const], W=[pf_b])
            p_i, p_b = Pbf.next()
            Pk = Pbf.t[0:cl, p_i, 0:W8]
            E(ACT, lambda h: h.activation(out=Pk, in_=PF, func=AF.Copy), R=[pf_b], W=[p_b])
            chk("e0")
            bg_step()
            Ak, Ak_b, Mk, Mk_b = A0, a_b, M0, m_b
            for lev in range(1, nlev + 1):
                last = (lev == nlev)
                if lev == 2:
                    chk("e2")
                pa, pab_ = ps_next()
                for hh in range(8):
                    sl = slice(hh * cl, (hh + 1) * cl)
                    E(PE, lambda h: h.matmul(pa[0:cl, sl], Mk[:, sl], Ak[:, sl], start=True, stop=True),
                      R=[Mk_b, Ak_b], W=[pab_], inc=(hh == 7))
                if not last:
                    pm2, pm2b = ps_next()
                    for hh in range(8):
                        sl = slice(hh * cl, (hh + 1) * cl)
                        E(PE, lambda h: h.matmul(pm2[0:cl, sl], Ak[:, sl], Mk[:, sl], start=True, stop=True),
                          R=[Mk_b, Ak_b], W=[pm2b], inc=(hh == 7))
                chk("e1")
                bg_step()
                na_i, na_b = Abf.next()
                An = Abf.t[0:cl, na_i, 0:W8]
                E(ACT, lambda h: h.activation(out=An, in_=pa[0:cl, 0:W8], func=AF.Copy), R=[pab_], W=[na_b])
                if not last:
                    nm_i, nm_b = Mbf.next()
                    Mn = Mbf.t[0:cl, nm_i, 0:W8]
                    E(DVE, lambda h: h.tensor_copy(out=Mn, in_=pm2[0:cl, 0:W8]), R=[pm2b], W=[nm_b])
                pp_, ppb_ = ps_next()
                for hh in range(8):
                    sl = slice(hh * cl, (hh + 1) * cl)
                    E(PE, lambda h: h.matmul(pp_[0:cl, sl], An[:, sl], Pk[:, sl], start=True, stop=True),
                      R=[na_b, p_b], W=[ppb_], inc=(hh == 7))
                E(DVE, lambda h: h.tensor_tensor(out=PF, in0=PF, in1=pp_[0:cl, 0:W8], op=ALU.add), R=[ppb_], W=[pf_b])
                if not last:
                    np_i, np_b = Pbf.next()
                    Pn = Pbf.t[0:cl, np_i, 0:W8]
                    E(ACT, lambda h: h.activation(out=Pn, in_=PF, func=AF.Copy), R=[pf_b], W=[np_b])
                    Pk, p_b = Pn, np_b
                    Mk, Mk_b = Mn, nm_b
                Ak, Ak_b = An, na_b
            t_i, t_b = TTb.next()
            TT = TTb.t[0:cl, t_i, 0:W8]
            E(ACT, lambda h: h.activation(out=TT, in_=PF, func=AF.Copy), R=[pf_b], W=[t_b])
            chk("d4")
            bg_step()
            kw_i, kw_b = kw.next()
            kd_i, kd_b = kd.next()
            vb_i, vb_b = vb.next()

            def sc(col0, g0):
                return gc[:, col0 + g0:col0 + g0 + 4].unsqueeze(2).to_broadcast([cl, 4, 128])

            for grp in range(2):
                pkt, pktb = ps_next()
                pvt, pvtb = ps_next()
                for g in range(4):
                    hh = grp * 4 + g
                    E(PE, lambda h: h.matmul(pkt[0:cl, g * 128:(g + 1) * 128], kT[:, hh, o:o + cl], ident_b[:, :],
                                             start=True, stop=True), R=[b_kT[hh], b_const], W=[pktb], inc=(g == 3))
                for g in range(4):
                    hh = grp * 4 + g
                    E(PE, lambda h: h.matmul(pvt[0:cl, g * 128:(g + 1) * 128], vT[:, hh, o:o + cl], ident_b[:, :],
                                             start=True, stop=True), R=[b_vT[hh], b_const], W=[pvtb], inc=(g == 3))
                k3 = pkt[0:cl, :].rearrange("p (h d) -> p h d", h=4)
                v3 = pvt[0:cl, :].rearrange("p (h d) -> p h d", h=4)
                csl = slice(grp * 512, (grp + 1) * 512)
                E(DVE, lambda h: h.tensor_tensor(out=kw.t[0:cl, kw_i, csl].rearrange("p (h d) -> p h d", h=4), in0=k3,
                                                 in1=sc(16, grp * 4), op=ALU.mult), R=[pktb, gcb], W=[kw_b])
                E(DVE, lambda h: h.tensor_tensor(out=kd.t[0:cl, kd_i, csl].rearrange("p (h d) -> p h d", h=4), in0=k3,
                                                 in1=sc(24, grp * 4), op=ALU.mult), R=[pktb, gcb], W=[kd_b])
                E(DVE, lambda h: h.tensor_tensor(out=vb.t[0:cl, vb_i, csl].rearrange("p (h d) -> p h d", h=4), in0=v3,
                                                 in1=beta[0:cl, ck, grp * 4:grp * 4 + 4].unsqueeze(2).to_broadcast([cl, 4, 128]),
                                                 op=ALU.mult), R=[pvtb, b_beta], W=[vb_b])
            chk("d5")
            bg_step()
            pw, pwb = ps_next()
            for hh in range(8):
                E(PE, lambda h: h.matmul(pw[:, hh * cl:(hh + 1) * cl], kw.t[0:cl, kw_i, hh * 128:(hh + 1) * 128],
                                         TT[:, hh * cl:(hh + 1) * cl], start=True, stop=True), R=[kw_b, t_b], W=[pwb],
                  inc=(hh == 7))
            w_i, w_b = wTn.next()
            WT = wTn.t[:, w_i, 0:W8]
            E(ACT, lambda h: h.activation(out=WT, in_=pw[:, 0:W8], func=AF.Copy, scale=-1.0), R=[pwb], W=[w_b])
            e_i, e_b = Eq.next()
            EQ = Eq.t[:, e_i, 0:W8]
            E(ACT, lambda h: h.activation(out=EQ, in_=bps, func=AF.Exp), R=[b_ps[6]], W=[e_b])
            EQ3 = EQ.rearrange("p (h j) -> p h j", h=8)
            E(DVE, lambda h: h.tensor_tensor(out=qdT[:, :, o:o + cl], in0=qT[:, :, o:o + cl], in1=EQ3, op=ALU.mult),
              R=b_qT + [e_b], W=[b_qdT[ck]])
            chk("d6")
            bg_step()
            vn_i, vn_b = vn.next()
            VN = vn.t[0:cl, vn_i, :]
            for grp in range(2):
                pv_, pvb_ = ps_next()
                for g in range(4):
                    hh = grp * 4 + g
                    E(PE, lambda h: h.matmul(pv_[0:cl, g * 128:(g + 1) * 128], TT[:, hh * cl:(hh + 1) * cl],
                                             vb.t[0:cl, vb_i, hh * 128:(hh + 1) * 128], start=True, stop=False),
                      R=[t_b, vb_b], W=[pvb_], inc=False)
                    E(PE, lambda h: h.matmul(pv_[0:cl, g * 128:(g + 1) * 128], WT[:, hh * cl:(hh + 1) * cl],
                                             Sbf[:, hh, :], start=False, stop=True), R=[w_b, b_Sbf[hh]], W=[pvb_],
                      inc=(g == 3))
                if grp == 0:
                    E(ACT, lambda h: h.activation(out=VN[:, 0:512], in_=pv_[0:cl, :], func=AF.Copy), R=[pvb_], W=[vn_b])
                else:
                    E(DVE, lambda h: h.tensor_copy(out=VN[:, 512:1024], in_=pv_[0:cl, :]), R=[pvb_], W=[vn_b])
            po, pob = ps_next()
            for hh in range(8):
                E(PE, lambda h: h.matmul(po[:, hh * cl:(hh + 1) * cl], Sbf[:, hh, :], qdT[:, hh, o:o + cl], start=True,
                                         stop=False), R=[b_Sbf[hh], b_qdT[ck]], W=[pob], inc=False)
                E(PE, lambda h: h.matmul(po[:, hh * cl:(hh + 1) * cl], VN[:, hh * 128:(hh + 1) * 128],
                                         AQ[:, hh * cl:(hh + 1) * cl], start=False, stop=True), R=[vn_b, q_b], W=[pob],
                  inc=(hh == 7))
            E(ACT, lambda h: h.activation(out=oT[:, :, o:o + cl], in_=po[:, 0:W8].rearrange("p (h j) -> p h j", h=8),
                                          func=AF.Copy), R=[pob], W=[b_oT[ck]])
            bg_step()
            for grp in range(2):
                pd, pdb = ps_next()
                for g in range(4):
                    hh = grp * 4 + g
                    E(PE, lambda h: h.matmul(pd[:, g * 128:(g + 1) * 128], kd.t[0:cl, kd_i, hh * 128:(hh + 1) * 128],
                                             VN[:, hh * 128:(hh + 1) * 128], start=True, stop=True), R=[kd_b, vn_b],
                      W=[pdb], inc=(g == 3))
                for g in range(4):
                    hh = grp * 4 + g
                    E(DVE, lambda h: h.scalar_tensor_tensor(out=S[:, hh, :], in0=S[:, hh, :],
                                                            scalar=EQ[:, hh * cl + cl - 1:hh * cl + cl],
                                                            in1=pd[:, g * 128:(g + 1) * 128], op0=ALU.mult, op1=ALU.add),
                      R=[e_b, pdb], W=[b_S[hh]])
                g0 = grp * 4
                E(ACT, lambda h: h.activation(out=Sbf[:, g0:g0 + 4, :], in_=S[:, g0:g0 + 4, :], func=AF.Copy),
                  R=b_S[g0:g0 + 4], W=b_Sbf[g0:g0 + 4])
                bg_step()
        bg_step(len(bg))
        for hh in range(8):
            qi, qb = sqb.next()
            sq = sqb.t[:, qi, 0:nt]
            E(ACT, lambda h: h.activation(out=sq, in_=oT[:, hh, 0:nt], func=AF.Square), R=b_oT[0:nck], W=[qb])
            E(PE, lambda h: h.matmul(ps[0:16, 7, 0:nt], Esel[:, hh, :], sq, start=(hh == 0), stop=(hh == 7)),
              R=[qb, b_const], W=[b_ps[7]])
        E(ACT, lambda h: h.activation(out=rsn[:, 0:nt], in_=ps[0:16, 7, 0:nt], func=AF.Sqrt, bias=eps_c[0:16],
                                      scale=1.0 / 128.0), R=[b_ps[7], b_const], W=[b_rsn])
        E(DVE, lambda h: h.reciprocal(out=rsn[:, 0:nt], in_=rsn[:, 0:nt]), W=[b_rsn])
        for hh in range(8):
            pt, pb = ps_next()
            E(PE, lambda h: h.matmul(pt[:, 0:nt], ident_f[0:16, hh:hh + 1].to_broadcast([16, 128]), rsn[:, 0:nt], start=True,
                                     stop=True), R=[b_rsn, b_const], W=[pb])
            ri, rb = rr.next()
            r = rr.t[:, ri, 0:nt]
            E(DVE, lambda h: h.tensor_tensor(out=r, in0=oT[:, hh, 0:nt], in1=pt[:, 0:nt], op=ALU.mult), R=b_oT[0:nck] + [pb],
              W=[rb])
            E(DVE, lambda h: h.scalar_tensor_tensor(out=od[:, hh, 0:nt], in0=r, scalar=pcol(P_DNW), in1=sz[:, hh, 0:nt],
                                                    op0=ALU.mult, op1=ALU.mult), R=[rb, b_sz[hh], b_const], W=[b_od[hh]])

        pm, pmb = ps_next()
        pq, pqb = ps_next()
        for j in range(8):
            yi, yb_ = ybf.next()
            qi, qb_ = ysq.next()
            E(ACT, lambda h: h.activation(out=ybf.t[:, yi, 0:nt], in_=ybuf[:, j, 0:nt], func=AF.Copy), R=[b_y[j]], W=[yb_])
            E(ACT, lambda h: h.activation(out=ysq.t[:, qi, 0:nt], in_=ybuf[:, j, 0:nt], func=AF.Square), R=[b_y[j]], W=[qb_])
            E(PE, lambda h: h.matmul(pm[:, 0:nt], ones_b[:, :], ybf.t[:, yi, 0:nt], start=(j == 0), stop=(j == 7)),
              R=[yb_, b_const], W=[pmb])
            E(PE, lambda h: h.matmul(pq[:, 0:nt], ones_b[:, :], ysq.t[:, qi, 0:nt], start=(j == 0), stop=(j == 7)),
              R=[qb_, b_const], W=[pqb])
        m_ = lnt[:, 0, 0:nt]
        msq = lnt[:, 1, 0:nt]
        var = lnt[:, 2, 0:nt]
        rstd = lnt[:, 3, 0:nt]
        mr = lnt[:, 4, 0:nt]
        E(DVE, lambda h: h.tensor_scalar(out=m_, in0=pm[:, 0:nt], scalar1=1.0 / D, scalar2=None, op0=ALU.mult),
          R=[pmb], W=[b_lnt[0]])
        E(DVE, lambda h: h.tensor_tensor(out=msq, in0=m_, in1=m_, op=ALU.mult), R=[b_lnt[0]], W=[b_lnt[1]])
        E(DVE, lambda h: h.scalar_tensor_tensor(out=var, in0=pq[:, 0:nt], scalar=1.0 / D, in1=msq, op0=ALU.mult,
                                                op1=ALU.subtract), R=[pqb, b_lnt[1]], W=[b_lnt[2]])
        E(ACT, lambda h: h.activation(out=rstd, in_=var, func=AF.Sqrt, bias=eps_c, scale=1.0), R=[b_lnt[2], b_const],
          W=[b_lnt[3]])
        E(DVE, lambda h: h.reciprocal(out=rstd, in_=rstd), W=[b_lnt[3]])
        E(DVE, lambda h: h.tensor_tensor(out=mr, in0=m_, in1=rstd, op=ALU.mult), R=[b_lnt[0], b_lnt[3]], W=[b_lnt[4]])
        for j in range(8):
            ai, ab = accA.next()
            aa = accA.t[:, ai, 0:nt]
            E(DVE, lambda h: h.tensor_tensor(out=aa, in0=ybuf[:, j, 0:nt], in1=rstd, op=ALU.mult), R=[b_y[j], b_lnt[3]],
              W=[ab])
            E(DVE, lambda h: h.tensor_tensor(out=aa, in0=aa, in1=mr, op=ALU.subtract), R=[b_lnt[4]], W=[ab])
            E(ACT, lambda h: h.activation(out=cact[:, j, 0:nt], in_=aa, func=AF.Silu, bias=pcol(P_LNB + j),
                                          scale=pcol(P_LNW + j)), R=[ab, b_const], W=[b_cact[j]])

        chk("ph4")
        for j in range(8):
            if j % 4 == 0:
                wco_ = wload(16 + j // 4)
                wdn_ = wload(18 + j // 4)
            wt, wb = wco_
            pa_, pab2 = proj_chunk(wt, wb, j % 4, nt, cact, b_cact)
            wt, wb = wdn_
            pb_, pbb2 = proj_chunk(wt, wb, j % 4, nt, od, b_od)
            mi, mb = mtmp.next()
            mt_ = mtmp.t[:, mi, 0:nt]
            E(DVE, lambda h: h.scalar_tensor_tensor(out=mt_, in0=gts[:, j, 0:nt], scalar=1.0, in1=pa_, op0=ALU.add,
                                                    op1=ALU.mult), R=[pab2, b_gts[j]], W=[mb])
            mi2, mb2 = mtmp.next()
            mt2 = mtmp.t[:, mi2, 0:nt]
            E(DVE, lambda h: h.scalar_tensor_tensor(out=mt2, in0=gts[:, 8 + j, 0:nt], scalar=1.0, in1=pb_, op0=ALU.add,
                                                    op1=ALU.mult), R=[pbb2, b_gts[8 + j]], W=[mb2])
            E(POOL, lambda h: h.tensor_tensor(out=mT[:, j, 0:nt], in0=mt_, in1=mt2, op=ALU.add), R=[mb, mb2], W=[b_mT[j]])
        for half in range(2):
            wt, wb = wload(20 + half)
            for (s, rows) in subs:
                pt, pb = ps_next()
                for kc in range(8):
                    E(PE, lambda h: h.matmul(pt[0:rows, :], mT[:, kc, s * 128:s * 128 + rows], wt[:, kc, :],
                                             start=(kc == 0), stop=(kc == 7)), R=[b_mT[kc], wb], W=[pb], inc=(kc == 7))
                hsl = htok[0:rows, hs, s, half * 512:(half + 1) * 512]
                E(DVE, lambda h: h.scalar_tensor_tensor(out=hsl, in0=pt[0:rows, :], scalar=0.5, in1=hsl, op0=ALU.mult,
                                                        op1=ALU.add), R=[pb], W=[b_htok[hs][s]])

        chk("ph5")
        norm_to_uT(hs, subs, nt, P_FFW)

        chk("ph6")
        nstats = None
        if has_next:
            load_x(ti + 1)
            nstats = norm_stats((ti + 1) % 2, reg_subs)
        ffn_list = [(jp, g) for jp in range(11) for g in range(4)]
        fs = {}

        def f_st0(i):
            jp, g = ffn_list[i]
            if g == 0:
                if jp % 2 == 0:
                    fs["dg", jp // 2] = dload(11 + jp // 2)
                fs["w"] = wload(22 + jp)
            wt, wb = fs["w"]
            j = 2 * jp + g // 2
            ch = j if g % 2 == 0 else NFF + j
            pp, ppb = proj_chunk(wt, wb, g, nt, uT, b_uT)
            pi, prb = preu.next()
            phb = b_preuh[pi]
            pr = preu.t[:, pi, :]
            E(POOL, lambda h: h.tensor_copy(out=pr[:, 0:2], in_=halu[:, ch, :]), R=[b_halu[ch]], W=[phb])
            if g % 2 == 0:
                E(ACT, lambda h: h.activation(out=pr[:, 2:2 + nt], in_=pp, func=AF.Copy), R=[ppb], W=[prb])
            else:
                E(DVE, lambda h: h.tensor_copy(out=pr[:, 2:2 + nt], in_=pp), R=[ppb], W=[prb])
            E(POOL, lambda h: h.tensor_copy(out=halu[:, ch, :], in_=pr[:, nt:nt + 2]), R=[prb], W=[b_halu[ch]])
            fs[i] = (pr, prb, phb)

        def f_st1(i):
            jp, g = ffn_list[i]
            j = 2 * jp + g // 2
            ch = j if g % 2 == 0 else NFF + j
            pr, prb, phb = fs[i]
            dgu, dgub = fs["dg", jp // 2]
            pc3, pc3b = ps_next()
            for t_ in range(3):
                mi_ = ((jp % 2) * 4 + g) * 3 + t_
                E(PE, lambda h: h.matmul(pc3[:, 0:nt], dgu[:, mi_ * 128:(mi_ + 1) * 128], pr[:, t_:t_ + nt],
                                         start=(t_ == 0), stop=(t_ == 2)), R=[dgub, prb, phb], W=[pc3b], inc=(t_ == 2))
            if g % 2 == 0:
                si_, sgb = sg.next()
                sga = sg.t[:, si_, 0:nt]
                E(ACT, lambda h: h.activation(out=sga, in_=pc3[:, 0:nt], func=AF.Silu, bias=pcol(P_FDB + ch)),
                  R=[pc3b, b_const], W=[sgb])
                fs["sg"] = (sga, sgb)
            else:
                sga, sgb = fs["sg"]
                E(DVE, lambda h: h.scalar_tensor_tensor(out=actb[:, j, 0:nt], in0=pc3[:, 0:nt], scalar=pcol(P_FDB + ch),
                                                        in1=sga, op0=ALU.add, op1=ALU.mult), R=[pc3b, sgb, b_const],
                  W=[b_act[j]])

        for i in range(44 + 2):
            if 0 <= i - 2 < 44:
                f_st1(i - 2)
            if i < 44:
                f_st0(i)

        if has_next:
            norm_apply((ti + 1) % 2, reg_subs, T, P_MIXW, nstats)
        chk("ph7")
        for half in range(2):
            pts = [ps_next() for _ in subs]
            for grp in range(3):
                wt, wb = wload(33 + half * 3 + grp)
                nk = 8 if grp < 2 else NFF - 16
                for si_, (s, rows) in enumerate(subs):
                    pt, pb = pts[si_]
                    for kk in range(nk):
                        kc = grp * 8 + kk
                        last = (kc == NFF - 1)
                        E(PE, lambda h: h.matmul(pt[0:rows, :], actb[:, kc, s * 128:s * 128 + rows], wt[:, kk, :],
                                                 start=(kc == 0), stop=last), R=[b_act[kc], wb], W=[pb],
                          inc=(kk == nk - 1))
            for si_, (s, rows) in enumerate(subs):
                pt, pb = pts[si_]
                hsl = htok[0:rows, hs, s, half * 512:(half + 1) * 512]
                E(DVE, lambda h: h.tensor_tensor(out=hsl, in0=hsl, in1=pt[0:rows, :], op=ALU.add), R=[pb],
                  W=[b_htok[hs][s]])

        chk("ph8")
        if not is_meta:
            x0 = (ti - 1) * T
            for (s, rows) in subs:
                si, sbuf_ = stat.next()
                st = stat.t[0:rows, si, :]
                hin = htok[0:rows, hs, s, :]
                E(ACT, lambda h: h.activation(out=junk[0:rows, :], in_=hin, func=AF.Square, accum_out=st[:, 0:1]),
                  R=[b_htok[hs][s]], W=[b_junk, sbuf_])
                E(ACT, lambda h: h.activation(out=st[:, 1:2], in_=st[:, 0:1], func=AF.Sqrt, bias=eps_c[0:rows],
                                              scale=1.0 / D), R=[b_const], W=[sbuf_])
                E(DVE, lambda h: h.reciprocal(out=st[:, 2:3], in_=st[:, 1:2]), W=[sbuf_])
                E(DVE, lambda h: h.scalar_tensor_tensor(out=hin, in0=hin, scalar=st[:, 2:3],
                                                        in1=tokp[0:rows, TP_NFW:TP_NFW + D], op0=ALU.mult, op1=ALU.mult),
                  R=[sbuf_, b_const], W=[b_htok[hs][s]])
                cx.dma(ACT, out_d[x0 + s * 128:x0 + s * 128 + rows, :], hin, R=[b_htok[hs][s]])

    try:
        chk("setup")
        do_tile(0, True, ntile >= 1)
        chk("meta")
        for ti in range(1, ntile + 1):
            do_tile(ti, False, ti < ntile)
    except _Stop:
        pass
    cx.finish()
    es.close()
    return nc, cx


def _pack_weights(w_in, w_conf_out, w_dn_out, w_out, w_up, w_down):
    blocks = np.zeros((NBLK, 128, 8, 512), np.float32)

    def colblock(W, c0s):
        W3 = W.reshape(8, 128, W.shape[1])
        out = np.empty((128, 8, 512), np.float32)
        for g, c0 in enumerate(c0s):
            out[:, :, g * 128:(g + 1) * 128] = W3[:, :, c0:c0 + 128].transpose(1, 0, 2)
        return out

    b = 0
    for jp in range(4):
        j0, j1 = 2 * jp, 2 * jp + 1
        blocks[b] = colblock(w_in, [1024 + j0 * 128, j0 * 128, 1024 + j1 * 128, j1 * 128]); b += 1
    for base in (3072, 2048, 4096, 5120):
        for hb in range(2):
            blocks[b] = colblock(w_in, [base + (hb * 4 + g) * 128 for g in range(4)]); b += 1
    for base in (6160, 7184):
        for hb in range(2):
            blocks[b] = colblock(w_in, [base + (hb * 4 + g) * 128 for g in range(4)]); b += 1
    for W in (w_conf_out, w_dn_out):
        for hb in range(2):
            blocks[b] = colblock(W, [(hb * 4 + g) * 128 for g in range(4)]); b += 1
    for half in range(2):
        blocks[b] = w_out.reshape(8, 128, 1024)[:, :, half * 512:(half + 1) * 512].transpose(1, 0, 2); b += 1
    for jp in range(11):
        j0, j1 = 2 * jp, 2 * jp + 1
        blocks[b] = colblock(w_up, [j0 * 128, DFF + j0 * 128, j1 * 128, DFF + j1 * 128]); b += 1
    Wd = w_down.reshape(NFF, 128, 1024)
    for half in range(2):
        for grp in range(3):
            nk = 8 if grp < 2 else NFF - 16
            blocks[b][:, 0:nk, :] = Wd[grp * 8:grp * 8 + nk, :, half * 512:(half + 1) * 512].transpose(1, 0, 2); b += 1
    assert b == NBLK
    return blocks.reshape(NBLK, 128, 8 * 512)


def _pack_params(inp):
    par = np.zeros((128, NPAR), np.float32)

    def cols(v):
        return v.reshape(-1, 128).T

    par[:, P_MIXW:P_MIXW + 8] = cols(inp["norm_mix_w"][0])
    par[:, P_BGATE:P_BGATE + 16] = cols(inp["b_gate"][0])
    cdw = inp["conf_dw_w"][0]
    par[:, P_CDW:P_CDW + 8 * CK] = cdw.reshape(CK, 8, 128).transpose(2, 1, 0).reshape(128, 8 * CK)
    par[:, P_CDB:P_CDB + 8] = cols(inp["conf_dw_b"][0])
    par[:, P_LNW:P_LNW + 8] = cols(inp["conf_ln_w"][0])
    par[:, P_LNB:P_LNB + 8] = cols(inp["conf_ln_b"][0])
    dnc = inp["dn_conv_w"][0]
    par[:, P_DNC:P_DNC + 96] = dnc.reshape(4, 24, 128).transpose(2, 1, 0).reshape(128, 96)
    par[:, P_DNW] = inp["dn_norm_w"][0]
    par[:, P_FFW:P_FFW + 8] = cols(inp["norm_ffn_w"][0])
    fdw = inp["ffn_dw_w"][0]
    par[:, P_FDW:P_FDW + 132] = fdw.reshape(3, 44, 128).transpose(2, 1, 0).reshape(128, 132)
    par[:, P_FDB:P_FDB + 44] = cols(inp["ffn_dw_b"][0])
    fcols = par[:, P_FDW:P_FDW + 132].copy()
    for jp in range(11):
        for g in range(4):
            j = 2 * jp + g // 2
            ch = j if g % 2 == 0 else NFF + j
            par[:, P_FDWB + (jp * 4 + g) * 3:P_FDWB + (jp * 4 + g) * 3 + 3] = fcols[:, ch * 3:ch * 3 + 3]
    tokp = np.zeros((128, NTOKP), np.float32)
    tokp[:, TP_DTB:TP_DTB + 8] = inp["dn_dt_bias"][0][None, :]
    tokp[:, TP_ALOG:TP_ALOG + 8] = inp["dn_A_log"][0][None, :]
    tokp[:, TP_NFW:] = inp["norm_final_w"][None, :]
    return par, tokp


_CACHE = {}


def kernel(**inputs):
    inp = {k: np.asarray(v, dtype=np.float32) for k, v in inputs.items()}
    x = inp["x"]
    B, n_x, _ = x.shape
    wpack = _pack_weights(inp["w_in"][0], inp["w_conf_out"][0], inp["w_dn_out"][0], inp["w_out"][0], inp["w_up"][0],
                          inp["w_down"][0])
    wab = np.ascontiguousarray(inp["w_in"][0][:, 6144:6160].reshape(8, 128, 16).transpose(1, 0, 2)).reshape(128, 128)
    par, tokp = _pack_params(inp)
    if n_x not in _CACHE:
        _CACHE[n_x] = build(n_x)[0]
    nc = _CACHE[n_x]
    in_maps = []
    for b in range(B):
        in_maps.append({"x": np.ascontiguousarray(x[b]), "meta": inp["meta_tokens"], "wpack": wpack, "wab": wab,
                        "params": par, "tokpar": tokp})
    res = run_bass_kernel_spmd(nc, in_maps, core_ids=list(range(B)))
    return np.stack([np.asarray(r["out"], dtype=np.float32) for r in res.results], axis=0)
```

```python
from contextlib import ExitStack

import numpy as np
import concourse.bass as bass
import concourse.mybir as mybir
from concourse.bass_utils import run_bass_kernel_spmd

F32 = mybir.dt.float32
BF16 = mybir.dt.bfloat16
ALU = mybir.AluOpType
AF = mybir.ActivationFunctionType

D = 1024
NMETA = 16
CK = 31
DFF = 2816
NFF = DFF // 128
EPS = 1e-6
T = 256
NBLK = 39
NEG = -30000.0

P_MIXW = 0
P_BGATE = P_MIXW + 8
P_CDW = P_BGATE + 16
P_CDB = P_CDW + 8 * CK
P_LNW = P_CDB + 8
P_LNB = P_LNW + 8
P_DNC = P_LNB + 8
P_DNW = P_DNC + 24 * 4
P_FFW = P_DNW + 1
P_FDW = P_FFW + 8
P_FDB = P_FDW + 44 * 3
P_FDWB = P_FDB + 44
NPAR = P_FDWB + 132
NDG = 17
TP_DTB = 0
TP_ALOG = 8
TP_NFW = 16
NTOKP = 16 + D


class Eng:
    def __init__(self, name, h, sem, inc):
        self.name = name
        self.h = h
        self.sem = sem
        self.inc = inc
        self.count = 0
        self.known = {}


class Buf:
    __slots__ = ("name", "w", "r")

    def __init__(self, name):
        self.name = name
        self.w = None
        self.r = {}


class Ctx:
    def __init__(self, nc, es, ndma=24, nsw=40):
        self.nc = nc
        self.es = es
        mk = lambda n: es.enter_context(nc.semaphore(n))
        self.PE = Eng("PE", nc.tensor, mk("s_pe"), 1)
        self.ACT = Eng("ACT", nc.scalar, mk("s_act"), 1)
        self.DVE = Eng("DVE", nc.vector, mk("s_dve"), 1)
        self.POOL = Eng("POOL", nc.gpsimd, mk("s_pool"), 1)
        self.SP = Eng("SP", nc.sync, None, 0)
        self.dsems = [Eng("D%d" % i, None, mk("s_d%d" % i), 16) for i in range(ndma)]
        self.dsems_sw = [Eng("W%d" % i, None, mk("s_w%d" % i), 16) for i in range(nsw)]
        self.dnext = 0
        self.dnext_sw = 0
        self.ninstr = 0

    def _deps(self, R, W):
        d = {}
        for b in R:
            if b.w is not None:
                e, i = b.w
                if d.get(e, 0) < i:
                    d[e] = i
        for b in W:
            if b.w is not None:
                e, i = b.w
                if d.get(e, 0) < i:
                    d[e] = i
            for e, i in b.r.items():
                if d.get(e, 0) < i:
                    d[e] = i
        return d

    def _waits(self, eng, d):
        for src, idx in d.items():
            if src is eng and eng is self.PE:
                continue
            if eng.known.get(src, 0) >= idx:
                continue
            assert idx <= src.count, "dependency on un-signalled instruction of %s" % src.name
            eng.h.wait_ge(src.sem, idx * src.inc)
            eng.known[src] = idx

    def _mark(self, p, R, W):
        e, i = p
        for b in R:
            if b.r.get(e, 0) < i:
                b.r[e] = i
        for b in W:
            b.w = p
            b.r = {}

    def emit(self, eng, fn, R=(), W=(), inc=True):
        self._waits(eng, self._deps(R, W))
        ins = fn(eng.h)
        self.ninstr += 1
        if inc:
            eng.count += 1
            ins.then_inc(eng.sem, 1)
            idx = eng.count
        else:
            idx = eng.count + 1
        self._mark((eng, idx), R, W)
        return ins

    def dma(self, q, out, in_, R=(), W=()):
        self._waits(q, self._deps(R, W))
        if q is self.POOL:
            ds = self.dsems_sw[self.dnext_sw]
            self.dnext_sw = (self.dnext_sw + 1) % len(self.dsems_sw)
        else:
            ds = self.dsems[self.dnext]
            self.dnext = (self.dnext + 1) % len(self.dsems)
        if q.known.get(ds, 0) < ds.count:
            q.h.wait_ge(ds.sem, ds.count * 16)
            q.known[ds] = ds.count
        q.h.dma_start(out=out, in_=in_).then_inc(ds.sem, 16)
        ds.count += 1
        self.ninstr += 1
        self._mark((ds, ds.count), R, W)

    def finish(self):
        for ds in self.dsems + self.dsems_sw:
            if ds.count and self.SP.known.get(ds, 0) < ds.count:
                self.SP.h.wait_ge(ds.sem, ds.count * 16)


class Rot:
    def __init__(self, name, tens, n):
        self.t = tens
        self.n = n
        self.bufs = [Buf("%s%d" % (name, i)) for i in range(n)]
        self.i = -1

    def next(self):
        self.i = (self.i + 1) % self.n
        return self.i, self.bufs[self.i]


class _Stop(Exception):
    pass


def build(n_x=4096, dbg=False, stop_after=None):
    nc = bass.Bass("TRN2", target_bir_lowering=False)
    es = ExitStack()
    cx = Ctx(nc, es)
    PE, ACT, DVE, POOL, SP = cx.PE, cx.ACT, cx.DVE, cx.POOL, cx.SP
    E = cx.emit
    NS = T // 128
    NCK = T // 64
    ntile = n_x // T

    x_d = nc.dram_tensor("x", [n_x, D], F32, kind="ExternalInput").ap()
    meta_d = nc.dram_tensor("meta", [NMETA, D], F32, kind="ExternalInput").ap()
    wpack_d = nc.dram_tensor("wpack", [NBLK, 128, 8 * 512], F32, kind="ExternalInput").ap()
    wab_d = nc.dram_tensor("wab", [128, 8 * 16], F32, kind="ExternalInput").ap()
    par_d = nc.dram_tensor("params", [128, NPAR], F32, kind="ExternalInput").ap()
    tokp_d = nc.dram_tensor("tokpar", [128, NTOKP], F32, kind="ExternalInput").ap()
    out_d = nc.dram_tensor("out", [n_x, D], F32, kind="ExternalOutput").ap()
    wbf_d = nc.dram_tensor("wbf", [NBLK, 128, 8 * 512], BF16, kind="Internal").ap()
    dg_d = nc.dram_tensor("dgd", [NDG, 128, 8 * 512], BF16, kind="Internal").ap()
    dbg_d = {}

    def sb(name, shape, dt):
        return es.enter_context(nc.sbuf_tensor(name, shape, dt))

    htok = sb("htok", [128, 2, NS, D], F32)
    b_htok = [[Buf("htok%d_%d" % (i, s)) for s in range(NS)] for i in range(2)]
    xn = Rot("xn", sb("xn", [128, 1, D], F32), 1)
    junk = sb("junk", [128, D], BF16)
    b_junk = Buf("junk")
    stat = Rot("stat", sb("stat", [128, 8, 4], F32), 8)
    uT = sb("uT", [128, 8, T], BF16)
    b_uT = [Buf("uT%d" % c) for c in range(8)]
    cbuf = sb("cbuf", [128, 8, 30 + T], BF16)
    b_c = [Buf("c%d" % c) for c in range(8)]
    ctmp = sb("ctmp", [128, 32], BF16)
    b_ctmp = Buf("ctmp")
    ybuf = sb("ybuf", [128, 8, T], F32)
    b_y = [Buf("y%d" % c) for c in range(8)]
    accA = Rot("accA", sb("accA", [128, 2, T], F32), 2)
    ybf = Rot("ybf", sb("ybf", [128, 2, T], BF16), 2)
    ysq = Rot("ysq", sb("ysq", [128, 2, T], BF16), 2)
    lnt = sb("lnt", [128, 5, T], F32)
    b_lnt = [Buf("lnt%d" % i) for i in range(5)]
    cact = sb("cact", [128, 8, T], BF16)
    b_cact = [Buf("cact%d" % c) for c in range(8)]
    sig = Rot("sig", sb("sig", [128, 2, T], F32), 2)
    pre = Rot("pre", sb("pre", [128, 3, 3 + T], BF16), 3)
    halq = sb("halq", [128, 24, 3], BF16)
    b_halq = [Buf("halq%d" % c) for c in range(24)]
    b_preh = [Buf("preh%d" % c) for c in range(3)]
    b_preuh = [Buf("preuh%d" % c) for c in range(4)]
    sqb = Rot("sqb", sb("sqb", [128, 2, T], BF16), 2)
    rr = Rot("rr", sb("rr", [128, 2, T], F32), 2)
    qT = sb("qT", [128, 8, T], BF16)
    kT = sb("kT", [128, 8, T], BF16)
    vT = sb("vT", [128, 8, T], BF16)
    qdT = sb("qdT", [128, 8, T], BF16)
    b_qT = [Buf("qT%d" % c) for c in range(8)]
    b_kT = [Buf("kT%d" % c) for c in range(8)]
    b_vT = [Buf("vT%d" % c) for c in range(8)]
    b_qdT = [Buf("qdT%d" % c) for c in range(NCK)]
    sz = sb("sz", [128, 8, T], BF16)
    b_sz = [Buf("sz%d" % c) for c in range(8)]
    gts = sb("gts", [128, 16, T], BF16)
    b_gts = [Buf("gts%d" % c) for c in range(16)]
    oT = sb("oT", [128, 8, T], F32)
    b_oT = [Buf("oT%d" % c) for c in range(NCK)]
    od = sb("od", [128, 8, T], BF16)
    b_od = [Buf("od%d" % c) for c in range(8)]
    mT = sb("mT", [128, 8, T], BF16)
    b_mT = [Buf("mT%d" % c) for c in range(8)]
    mtmp = Rot("mtmp", sb("mtmp", [128, 2, T], F32), 2)
    actb = sb("actb", [128, NFF, T], BF16)
    b_act = [Buf("act%d" % c) for c in range(NFF)]
    preu = Rot("preu", sb("preu", [128, 4, 2 + T], BF16), 4)
    halu = sb("halu", [128, 44, 2], BF16)
    b_halu = [Buf("halu%d" % c) for c in range(44)]
    sg = Rot("sg", sb("sg", [128, 2, T], F32), 2)
    abt = sb("abt", [64, NCK, 16], F32)
    b_abt = Buf("abt")
    gtok = sb("gtok", [64, NCK, 8], F32)
    beta = sb("beta", [64, NCK, 8], F32)
    b_g = Buf("g")
    b_beta = Buf("beta")
    GU = Rot("GU", sb("GU", [64, 1, 512], F32), 1)
    gcol = Rot("gcol", sb("gcol", [64, 2, 32], F32), 2)
    D1 = Rot("D1", sb("D1", [64, 1, 512], F32), 1)
    D2 = Rot("D2", sb("D2", [64, 1, 512], F32), 1)
    Abf = Rot("Abf", sb("Abf", [64, 3, 512], BF16), 3)
    Mbf = Rot("Mbf", sb("Mbf", [64, 3, 512], BF16), 3)
    Pbf = Rot("Pbf", sb("Pbf", [64, 2, 512], BF16), 2)
    Pf = Rot("Pf", sb("Pf", [64, 1, 512], F32), 1)
    AqkT = Rot("AqkT", sb("AqkT", [64, 1, 512], BF16), 1)
    TTb = Rot("TTb", sb("TTb", [64, 1, 512], BF16), 1)
    kw = Rot("kw", sb("kw", [64, 1, 1024], BF16), 1)
    kd = Rot("kd", sb("kd", [64, 1, 1024], BF16), 1)
    vb = Rot("vb", sb("vb", [64, 1, 1024], BF16), 1)
    vn = Rot("vn", sb("vn", [64, 1, 1024], BF16), 1)
    wTn = Rot("wTn", sb("wTn", [128, 1, 512], BF16), 1)
    Eq = Rot("Eq", sb("Eq", [128, 1, 512], F32), 1)
    S = sb("S", [128, 8, 128], F32)
    Sbf = sb("Sbf", [128, 8, 128], BF16)
    b_S = [Buf("S%d" % h) for h in range(8)]
    b_Sbf = [Buf("Sbf%d" % h) for h in range(8)]
    NRING = 5
    wring = Rot("wring", sb("wring", [128, NRING, 8 * 512], BF16), NRING)
    wab = sb("wab_sb", [128, 8, 16], BF16)
    wabf = sb("wabf", [128, 8 * 16], F32)
    b_wab = Buf("wab")
    ident_f = sb("ident_f", [128, 128], F32)
    ident_b = sb("ident_b", [128, 128], BF16)
    ones_b = sb("ones_b", [128, 128], BF16)
    ones_f = sb("ones_f", [64, 128], F32)
    Umat = sb("Umat", [64, 64], F32)
    par = sb("par_sb", [128, NPAR], F32)
    tokp = sb("tokp", [128, NTOKP], F32)
    cst = sb("cst", [128, 4], F32)
    Esel = sb("Esel", [128, 16, 16], BF16)
    identq = sb("identq", [16, 16], F32)
    rsn = sb("rsn", [16, T], F32)
    b_rsn = Buf("rsn")
    b_const = Buf("const")
    b_wbf = [Buf("wbf%d" % i) for i in range(NBLK)]
    b_dg = [Buf("dg%d" % i) for i in range(NDG)]

    ps = es.enter_context(nc.psum_tensor("ps", [128, 8, 512], F32))
    b_ps = [Buf("ps%d" % i) for i in range(8)]
    ps_state = {"i": -1}

    def ps_next():
        ps_state["i"] = (ps_state["i"] + 1) % 6
        i = ps_state["i"]
        return ps[:, i, :], b_ps[i]

    cx.dma(SP, par[:, :], par_d[:, :], W=[b_const])
    cx.dma(SP, tokp[:, :], tokp_d[:, :], W=[b_const])
    cx.dma(SP, wabf[:, :], wab_d[:, :], W=[b_wab])
    for b in range(NBLK):
        cx.dma(POOL, wbf_d[b], wpack_d[b], W=[b_wbf[b]])

    def pool_c(fn):
        E(POOL, fn, W=[b_const])

    pool_c(lambda h: h.memset(ident_f[:], 0.0))
    pool_c(lambda h: h.affine_select(out=ident_f[:], in_=ident_f[:], pattern=[[-1, 128]],
                                     compare_op=ALU.not_equal, fill=1.0, base=0, channel_multiplier=1))
    pool_c(lambda h: h.tensor_copy(out=ident_b[:], in_=ident_f[:]))
    pool_c(lambda h: h.memset(ones_b[:], 1.0))
    pool_c(lambda h: h.memset(ones_f[:], 1.0))
    pool_c(lambda h: h.memset(Umat[:], 1.0))
    pool_c(lambda h: h.affine_select(out=Umat[:], in_=Umat[:], pattern=[[1, 64]],
                                     compare_op=ALU.is_ge, fill=0.0, base=0, channel_multiplier=-1))
    pool_c(lambda h: h.memset(Esel[:], 1.0))
    pool_c(lambda h: h.affine_select(out=Esel[:], in_=Esel[:], pattern=[[1, 16], [-1, 16]], compare_op=ALU.is_equal,
                                     fill=0.0, base=0, channel_multiplier=0))
    pool_c(lambda h: h.tensor_copy(out=identq[:], in_=ident_f[0:16, 0:16]))
    pool_c(lambda h: h.tensor_scalar(out=identq[:, 8:16], in0=identq[:, 8:16], scalar1=128.0 ** -0.5, scalar2=None,
                                     op0=ALU.mult))
    pool_c(lambda h: h.memset(cst[:, 0:1], EPS))
    pool_c(lambda h: h.memset(cst[:, 1:2], 1.0))
    pool_c(lambda h: h.memset(cbuf[:], 0.0))
    pool_c(lambda h: h.memset(halq[:], 0.0))
    pool_c(lambda h: h.memset(halu[:], 0.0))
    pool_c(lambda h: h.memset(S[:], 0.0))
    pool_c(lambda h: h.memset(Sbf[:], 0.0))
    E(POOL, lambda h: h.tensor_copy(out=wab[:].rearrange("p a b -> p (a b)"), in_=wabf[:]), R=[], W=[b_wab])
    E(ACT, lambda h: h.activation(out=tokp[:, TP_ALOG:TP_ALOG + 8], in_=tokp[:, TP_ALOG:TP_ALOG + 8], func=AF.Exp),
      W=[b_const])
    E(DVE, lambda h: h.tensor_scalar(out=tokp[:, TP_ALOG:TP_ALOG + 8], in0=tokp[:, TP_ALOG:TP_ALOG + 8],
                                     scalar1=-1.0, scalar2=None, op0=ALU.mult), W=[b_const])
    E(DVE, lambda h: h.tensor_scalar(out=par[:, P_CDW:P_CDW + 8 * CK], in0=par[:, P_CDW:P_CDW + 8 * CK], scalar1=0.5,
                                     scalar2=None, op0=ALU.mult), W=[b_const])
    E(DVE, lambda h: h.tensor_scalar(out=par[:, P_BGATE:P_BGATE + 16], in0=par[:, P_BGATE:P_BGATE + 16], scalar1=0.5,
                                     scalar2=None, op0=ALU.mult), W=[b_const])
    E(DVE, lambda h: h.tensor_scalar(out=par[:, P_DNW:P_DNW + 1], in0=par[:, P_DNW:P_DNW + 1], scalar1=0.5,
                                     scalar2=None, op0=ALU.mult), W=[b_const])
    all_state = b_c + b_halq + b_halu + b_S + b_Sbf
    for b in all_state:
        b.w = b_const.w

    neg_reg = nc.gpsimd.to_reg(NEG)
    def build_dg(d, poff, nm):
        i, wb_ = wring.next()
        slot = wring.t[:, i, 0:nm * 128].rearrange("p (m c) -> p m c", c=128)
        E(DVE, lambda h: h.tensor_tensor(out=slot, in0=ident_b[:, :].unsqueeze(1).to_broadcast([128, nm, 128]),
                                         in1=par[:, poff:poff + nm].unsqueeze(2).to_broadcast([128, nm, 128]), op=ALU.mult),
          R=[b_const], W=[wb_])
        cx.dma(SP, dg_d[d][:, 0:nm * 128], wring.t[:, i, 0:nm * 128], R=[wb_], W=[b_dg[d]])

    for j in range(8):
        build_dg(j, P_CDW + j * CK, CK)
    for kk_ in range(3):
        build_dg(8 + kk_, P_DNC + kk_ * 32, 32)
    for d_ in range(6):
        build_dg(11 + d_, P_FDWB + d_ * 24, min(24, 132 - d_ * 24))

    def dload(d):
        nm = CK if d < 8 else (32 if d < 11 else min(24, 132 - (d - 11) * 24))
        i, wb_ = wring.next()
        cx.dma(SP, wring.t[:, i, 0:nm * 128], dg_d[d][:, 0:nm * 128], R=[b_dg[d]], W=[wb_])
        return wring.t[:, i, :], wb_

    eps_c = cst[:, 0:1]
    one_c = cst[:, 1:2]

    def pcol(off, rows=128):
        return par[0:rows, off:off + 1]

    def wload(bidx):
        i, wb = wring.next()
        cx.dma(SP, wring.t[:, i, :], wbf_d[bidx], R=[b_wbf[bidx]], W=[wb])
        return wring.t[:, i, :].rearrange("p (k n) -> p k n", k=8), wb

    def norm_stats(hs, subs):
        out = []
        for (s, rows) in subs:
            si, sbuf_ = stat.next()
            st = stat.t[0:rows, si, :]
            hin = htok[0:rows, hs, s, :]
            E(ACT, lambda h: h.activation(out=junk[0:rows, :], in_=hin, func=AF.Square, accum_out=st[:, 0:1]),
              R=[b_htok[hs][s]], W=[b_junk, sbuf_])
            E(ACT, lambda h: h.activation(out=st[:, 1:2], in_=st[:, 0:1], func=AF.Sqrt, bias=eps_c[0:rows], scale=1.0 / D),
              R=[b_const], W=[sbuf_])
            E(DVE, lambda h: h.reciprocal(out=st[:, 2:3], in_=st[:, 1:2]), W=[sbuf_])
            out.append((st, sbuf_))
        return out

    def norm_apply(hs, subs, nt, woff, stats):
        for (s, rows), (st, sbuf_) in zip(subs, stats):
            hin = htok[0:rows, hs, s, :]
            xi, xb = xn.next()
            xt = xn.t[0:rows, xi, :]
            E(DVE, lambda h: h.tensor_scalar(out=xt, in0=hin, scalar1=st[:, 2:3], scalar2=None, op0=ALU.mult),
              R=[b_htok[hs][s], sbuf_], W=[xb])
            for c in range(8):
                pt, pb = ps_next()
                E(PE, lambda h: h.transpose(pt[:, 0:rows], xt[:, c * 128:(c + 1) * 128], ident_f[0:rows, 0:rows]),
                  R=[xb, b_const], W=[pb])
                tgt = uT[:, c, s * 128:s * 128 + rows]
                if c % 2 == 0:
                    E(ACT, lambda h: h.activation(out=tgt, in_=pt[:, 0:rows], func=AF.Copy, scale=pcol(woff + c)),
                      R=[pb, b_const], W=[b_uT[c]])
                else:
                    E(DVE, lambda h: h.tensor_scalar(out=tgt, in0=pt[:, 0:rows], scalar1=pcol(woff + c), scalar2=None,
                                                     op0=ALU.mult), R=[pb, b_const], W=[b_uT[c]])

    def norm_to_uT(hs, subs, nt, woff):
        norm_apply(hs, subs, nt, woff, norm_stats(hs, subs))

    def proj_chunk(wt, wb, g, nt, src, b_src, nk=8):
        pt, pb = ps_next()
        for kc in range(nk):
            E(PE, lambda h: h.matmul(pt[:, 0:nt], wt[:, kc, g * 128:(g + 1) * 128], src[:, kc, 0:nt],
                                     start=(kc == 0), stop=(kc == nk - 1)),
              R=[wb, b_src[kc]], W=[pb], inc=(kc == nk - 1))
        return pt[:, 0:nt], pb

    def l2_rstd(src_ap, b_src, nt, scale_in):
        qi, qb = sqb.next()
        sq = sqb.t[:, qi, 0:nt]
        E(ACT, lambda h: h.activation(out=sq, in_=src_ap, func=AF.Square), R=[b_src], W=[qb])
        pt, pb = ps_next()
        E(PE, lambda h: h.matmul(pt[:, 0:nt], ones_b[:, :], sq, start=True, stop=True), R=[qb, b_const], W=[pb])
        ri, rb = rr.next()
        r = rr.t[:, ri, 0:nt]
        E(ACT, lambda h: h.activation(out=r, in_=pt[:, 0:nt], func=AF.Sqrt, bias=eps_c, scale=scale_in),
          R=[pb, b_const], W=[rb])
        E(DVE, lambda h: h.reciprocal(out=r, in_=r), W=[rb])
        return r, rb

    def chk(name):
        if stop_after == name:
            raise _Stop()

    reg_subs = [(s_, 128) for s_ in range(NS)]

    def load_x(ti):
        hs_ = ti % 2
        x0_ = (ti - 1) * T
        for (s, rows) in reg_subs:
            cx.dma(SP, htok[:, hs_, s, :], x_d[x0_ + s * 128:x0_ + (s + 1) * 128, :], W=[b_htok[hs_][s]])

    def do_tile(ti, is_meta, has_next, prev_back):
        hs = ti % 2
        if is_meta:
            nt = NMETA
            subs = [(0, NMETA)]
            chunks = [(0, NMETA)]
            cx.dma(SP, htok[0:NMETA, hs, 0, :], meta_d[:, :], W=[b_htok[hs][0]])
            norm_to_uT(hs, subs, nt, P_MIXW)
        else:
            nt = T
            subs = reg_subs
            chunks = [(c * 64, 64) for c in range(NCK)]

        chk("ph1")
        for jp in range(4):
            wt, wb = wload(jp)
            for jj in range(2):
                j = 2 * jp + jj
                pg, pgb = proj_chunk(wt, wb, 2 * jj, nt, uT, b_uT)
                gi, gb = sig.next()
                sg_ap = sig.t[:, gi, 0:nt]
                E(ACT, lambda h: h.activation(out=sg_ap, in_=pg, func=AF.Tanh, scale=0.5), R=[pgb], W=[gb])
                pv, pvb = proj_chunk(wt, wb, 2 * jj + 1, nt, uT, b_uT)
                E(DVE, lambda h: h.scalar_tensor_tensor(out=cbuf[:, j, 30:30 + nt], in0=sg_ap, scalar=1.0, in1=pv,
                                                        op0=ALU.add, op1=ALU.mult), R=[pvb, gb], W=[b_c[j]])
        bg = []

        def bg_step(n=1):
            for _ in range(n):
                if bg:
                    bg.pop(0)()

        def conv31_item(j):
            dg, dgb = dload(j)
            pt, pb = ps_next()
            for t_ in range(CK):
                E(PE, lambda h: h.matmul(pt[:, 0:nt], dg[:, t_ * 128:(t_ + 1) * 128], cbuf[:, j, t_:t_ + nt],
                                         start=(t_ == 0), stop=(t_ == CK - 1)), R=[dgb, b_c[j]], W=[pb], inc=(t_ == CK - 1))
            E(ACT, lambda h: h.activation(out=ybuf[:, j, 0:nt], in_=pt[:, 0:nt], func=AF.Identity, bias=pcol(P_CDB + j)),
              R=[pb, b_const], W=[b_y[j]])
            E(POOL, lambda h: h.tensor_copy(out=ctmp[:, 0:30], in_=cbuf[:, j, nt:nt + 30]), R=[b_c[j]], W=[b_ctmp])
            E(POOL, lambda h: h.tensor_copy(out=cbuf[:, j, 0:30], in_=ctmp[:, 0:30]), R=[b_ctmp], W=[b_c[j]])

        if prev_back:
            bg.append(prev_back[0])
        for j in range(8):
            bg.append(lambda j=j: conv31_item(j))
        bg.extend(prev_back[1:])

        qkv_list = [(kind, hh) for kind in ("k", "q", "v") for hh in range(8)]
        kbase = {"q": 0, "k": 8, "v": 16}
        wblk0 = {"k": 4, "q": 6, "v": 8}
        dgidx = {"q": 8, "k": 9, "v": 10}
        qs = {}

        def q_st0(i):
            kind, hh = qkv_list[i]
            if hh == 0:
                qs["dg", kind] = dload(dgidx[kind])
            if hh % 4 == 0:
                qs["w"] = wload(wblk0[kind] + hh // 4)
            wt, wb = qs["w"]
            ch = kbase[kind] + hh
            pp, ppb = proj_chunk(wt, wb, hh % 4, nt, uT, b_uT)
            pi, prb = pre.next()
            phb = b_preh[pi]
            pr = pre.t[:, pi, :]
            E(POOL, lambda h: h.tensor_copy(out=pr[:, 0:3], in_=halq[:, ch, :]), R=[b_halq[ch]], W=[phb])
            if kind == "v":
                E(ACT, lambda h: h.activation(out=pr[:, 3:3 + nt], in_=pp, func=AF.Copy), R=[ppb], W=[prb])
            else:
                E(DVE, lambda h: h.tensor_copy(out=pr[:, 3:3 + nt], in_=pp), R=[ppb], W=[prb])
            E(POOL, lambda h: h.tensor_copy(out=halq[:, ch, :], in_=pr[:, nt:nt + 3]), R=[prb], W=[b_halq[ch]])
            qs[i] = (pr, prb, phb)

        def q_st1(i):
            kind, hh = qkv_list[i]
            pr, prb, phb = qs[i]
            dgq, dgqb = qs["dg", kind]
            pc4, cb = ps_next()
            ca = pc4[:, 0:nt]
            for t_ in range(4):
                mi_ = hh * 4 + t_
                E(PE, lambda h: h.matmul(ca, dgq[:, mi_ * 128:(mi_ + 1) * 128], pr[:, t_:t_ + nt], start=(t_ == 0),
                                         stop=(t_ == 3)), R=[dgqb, prb, phb], W=[cb], inc=(t_ == 3))
            if kind == "v":
                E(ACT, lambda h: h.activation(out=vT[:, hh, 0:nt], in_=ca, func=AF.Silu), R=[cb], W=[b_vT[hh]])
            else:
                dst, dstb = (kT, b_kT) if kind == "k" else (qT, b_qT)
                E(ACT, lambda h: h.activation(out=dst[:, hh, 0:nt], in_=ca, func=AF.Silu), R=[cb], W=[dstb[hh]])
                qi, qb = sqb.next()
                sq = sqb.t[:, qi, 0:nt]
                E(ACT, lambda h: h.activation(out=sq, in_=dst[:, hh, 0:nt], func=AF.Square), R=[dstb[hh]], W=[qb])
                qs[i] = (sq, qb)

        def q_st2(i):
            kind, hh = qkv_list[i]
            if kind == "v":
                return
            sq, qb = qs[i]
            idx = hh if kind == "k" else 8 + hh
            first = (kind == "k" and hh == 0)
            last = (kind == "q" and hh == 7)
            E(PE, lambda h: h.matmul(ps[0:16, 7, 0:nt], Esel[:, idx, :], sq, start=first, stop=last), R=[qb, b_const],
              W=[b_ps[7]])

        for i in range(24 + 3):
            if 0 <= i - 2 < 24:
                q_st1(i - 2)
            if 0 <= i - 3 < 24:
                q_st2(i - 3)
            if i < 24:
                q_st0(i)
        E(ACT, lambda h: h.activation(out=rsn[:, 0:nt], in_=ps[0:16, 7, 0:nt], func=AF.Sqrt, bias=eps_c[0:16], scale=1.0),
          R=[b_ps[7], b_const], W=[b_rsn])
        E(DVE, lambda h: h.reciprocal(out=rsn[:, 0:nt], in_=rsn[:, 0:nt]), W=[b_rsn])
        zg_state = {}

        def z_item(hh):
            if hh % 4 == 0:
                zg_state["w"] = wload(10 + hh // 4)
            wt, wb = zg_state["w"]
            pp, ppb = proj_chunk(wt, wb, hh % 4, nt, uT, b_uT)
            gi_, gb_ = sig.next()
            th = sig.t[:, gi_, 0:nt]
            E(ACT, lambda h: h.activation(out=th, in_=pp, func=AF.Tanh, scale=0.5), R=[ppb], W=[gb_])
            E(DVE, lambda h: h.scalar_tensor_tensor(out=sz[:, hh, 0:nt], in0=th, scalar=1.0, in1=pp, op0=ALU.add,
                                                    op1=ALU.mult), R=[ppb, gb_], W=[b_sz[hh]])

        def gate_item(jg):
            if jg % 4 == 0:
                zg_state["w"] = wload(12 + jg // 4)
            wt, wb = zg_state["w"]
            pp, ppb = proj_chunk(wt, wb, jg % 4, nt, uT, b_uT)
            E(ACT, lambda h: h.activation(out=gts[:, jg, 0:nt], in_=pp, func=AF.Tanh, bias=pcol(P_BGATE + jg),
                                          scale=0.5), R=[ppb, b_const], W=[b_gts[jg]])

        for hh in range(8):
            z_item(hh)
        for jg in range(16):
            gate_item(jg)
        pab, pabb = ps_next()
        pab3 = pab.rearrange("p (c n) -> p c n", n=16)
        for ck, (o, cl) in enumerate(chunks):
            for kc in range(8):
                E(PE, lambda h: h.matmul(pab3[0:cl, ck, :], uT[:, kc, o:o + cl], wab[:, kc, :], start=(kc == 0),
                                         stop=(kc == 7)), R=[b_uT[kc], b_wab], W=[pabb], inc=(kc == 7))
        nck = len(chunks)
        cl0 = chunks[0][1]
        dtb_b = tokp[0:cl0, TP_DTB:TP_DTB + 8].unsqueeze(1).to_broadcast([cl0, nck, 8])
        negA_b = tokp[0:cl0, TP_ALOG:TP_ALOG + 8].unsqueeze(1).to_broadcast([cl0, nck, 8])
        E(DVE, lambda h: h.tensor_tensor(out=abt[0:cl0, 0:nck, 0:8], in0=pab3[0:cl0, 0:nck, 0:8], in1=dtb_b, op=ALU.add),
          R=[pabb, b_const], W=[b_abt])
        E(ACT, lambda h: h.activation(out=abt[0:cl0, 0:nck, 0:8], in_=abt[0:cl0, 0:nck, 0:8], func=AF.Exp), W=[b_abt])
        E(ACT, lambda h: h.activation(out=abt[0:cl0, 0:nck, 0:8], in_=abt[0:cl0, 0:nck, 0:8], func=AF.Ln,
                                      bias=one_c[0:cl0]), R=[b_const], W=[b_abt])
        E(DVE, lambda h: h.tensor_tensor(out=gtok[0:cl0, 0:nck, :], in0=abt[0:cl0, 0:nck, 0:8], in1=negA_b, op=ALU.mult),
          R=[b_abt, b_const], W=[b_g])
        E(ACT, lambda h: h.activation(out=beta[0:cl0, 0:nck, :], in_=pab3[0:cl0, 0:nck, 8:16], func=AF.Tanh, scale=0.5),
          R=[pabb], W=[b_beta])
        E(DVE, lambda h: h.tensor_scalar(out=beta[0:cl0, 0:nck, :], in0=beta[0:cl0, 0:nck, :], scalar1=0.5, scalar2=0.5,
                                         op0=ALU.mult, op1=ALU.add), W=[b_beta])

        for idx in range(16):
            hh = idx % 8
            dst, dstb = (kT, b_kT) if idx < 8 else (qT, b_qT)
            pt, pb = ps_next()
            E(PE, lambda h: h.matmul(pt[:, 0:nt], identq[:, idx:idx + 1].to_broadcast([16, 128]), rsn[:, 0:nt], start=True,
                                     stop=True), R=[b_rsn, b_const], W=[pb])
            E(DVE, lambda h: h.tensor_tensor(out=dst[:, hh, 0:nt], in0=dst[:, hh, 0:nt], in1=pt[:, 0:nt], op=ALU.mult),
              R=[pb], W=[dstb[hh]])
        chk("ph2")
        chk("ph3")
        for ck, (o, cl) in enumerate(chunks):
            nlev = 5 if cl == 64 else 3
            W8 = 8 * cl
            gi, gub = GU.next()
            gu3 = GU.t[0:cl, gi, 0:W8].rearrange("p (h j) -> p h j", h=8)
            gsl = gtok[0:cl, ck, :]
            E(DVE, lambda h: h.tensor_tensor(out=gu3, in0=gsl.unsqueeze(2).to_broadcast([cl, 8, cl]),
                                             in1=Umat[0:cl, 0:cl].unsqueeze(1).to_broadcast([cl, 8, cl]), op=ALU.mult),
              R=[b_g, b_const], W=[gub])
            bps = ps[:, 6, 0:W8]
            bps3 = bps.rearrange("p (h j) -> p h j", h=8)
            E(PE, lambda h: h.matmul(bps, ones_f[0:cl, :], GU.t[0:cl, gi, 0:W8], start=True, stop=True),
              R=[gub, b_const], W=[b_ps[6]])
            pc, pcb = ps_next()
            E(PE, lambda h: h.matmul(pc[0:cl, 0:8], Umat[0:cl, 0:cl], gsl, start=True, stop=True), R=[b_g, b_const], W=[pcb])
            ci_, gcb = gcol.next()
            gc = gcol.t[0:cl, ci_, :]
            E(ACT, lambda h: h.activation(out=gc[:, 0:8], in_=pc[0:cl, 0:8], func=AF.Copy), R=[pcb], W=[gcb])
            E(ACT, lambda h: h.activation(out=gc[:, 8:16], in_=pc[0:cl, 0:8], func=AF.Exp), R=[pcb], W=[gcb])
            E(DVE, lambda h: h.tensor_tensor(out=gc[:, 16:24], in0=gc[:, 8:16], in1=beta[0:cl, ck, :], op=ALU.mult),
              R=[b_beta], W=[gcb])
            E(DVE, lambda h: h.tensor_tensor(out=gc[:, 24:32], in0=bps3[0:cl, :, cl - 1], in1=gc[:, 0:8], op=ALU.subtract),
              R=[b_ps[6]], W=[gcb])
            E(ACT, lambda h: h.activation(out=gc[:, 24:32], in_=gc[:, 24:32], func=AF.Exp), W=[gcb])
            d1i, d1b = D1.next()
            d2i, d2b = D2.next()
            d1 = D1.t[0:cl, d1i, 0:W8].rearrange("p (h j) -> p h j", h=8)
            d2 = D2.t[0:cl, d2i, 0:W8].rearrange("p (h j) -> p h j", h=8)
            gcb3 = gc[:, 0:8].unsqueeze(2).to_broadcast([cl, 8, cl])
            E(DVE, lambda h: h.tensor_tensor(out=d1, in0=gcb3, in1=bps3[0:cl], op=ALU.subtract), R=[gcb, b_ps[6]], W=[d1b])
            E(DVE, lambda h: h.tensor_tensor(out=d2, in0=bps3[0:cl], in1=gcb3, op=ALU.subtract), R=[gcb, b_ps[6]], W=[d2b])
            E(POOL, lambda h: h.affine_select(out=d1, in_=d1, pattern=[[0, 8], [-1, cl]], compare_op=ALU.is_gt, fill=neg_reg,
                                              base=0, channel_multiplier=1), W=[d1b])
            E(POOL, lambda h: h.affine_select(out=d2, in_=d2, pattern=[[0, 8], [1, cl]], compare_op=ALU.is_ge, fill=neg_reg,
                                              base=0, channel_multiplier=-1), W=[d2b])
            E(ACT, lambda h: h.activation(out=d1, in_=d1, func=AF.Exp), W=[d1b])
            E(ACT, lambda h: h.activation(out=d2, in_=d2, func=AF.Exp), W=[d2b])
            E(DVE, lambda h: h.tensor_tensor(out=d1, in0=d1, in1=beta[0:cl, ck, :].unsqueeze(2).to_broadcast([cl, 8, cl]),
                                              op=ALU.mult), R=[b_beta], W=[d1b])
            chk("d1")
            pkk, pkkb = ps_next()
            pqk, pqkb = ps_next()
            for hh in range(8):
                E(PE, lambda h: h.matmul(pkk[0:cl, hh * cl:(hh + 1) * cl], kT[:, hh, o:o + cl], kT[:, hh, o:o + cl],
                                         start=True, stop=True), R=[b_kT[hh]], W=[pkkb], inc=(hh == 7))
            for hh in range(8):
                E(PE, lambda h: h.matmul(pqk[0:cl, hh * cl:(hh + 1) * cl], kT[:, hh, o:o + cl], qT[:, hh, o:o + cl],
                                         start=True, stop=True), R=[b_kT[hh], b_qT[hh]], W=[pqkb], inc=(hh == 7))
            a_i, a_b = Abf.next()
            A0 = Abf.t[0:cl, a_i, 0:W8]
            E(DVE, lambda h: h.tensor_tensor(out=A0, in0=pkk[0:cl, 0:W8], in1=D1.t[0:cl, d1i, 0:W8], op=ALU.mult),
              R=[pkkb, d1b], W=[a_b])
            q_i, q_b = AqkT.next()
            AQ = AqkT.t[0:cl, q_i, 0:W8]
            E(DVE, lambda h: h.tensor_tensor(out=AQ, in0=pqk[0:cl, 0:W8], in1=D2.t[0:cl, d2i, 0:W8], op=ALU.mult),
              R=[pqkb, d2b], W=[q_b])
            chk("d2")
            bg_step()
            pmt, pmtb = ps_next()
            for hh in range(8):
                E(PE, lambda h: h.matmul(pmt[0:cl, hh * cl:(hh + 1) * cl], A0[:, hh * cl:(hh + 1) * cl], ident_b[0:cl, 0:cl],
                                         start=True, stop=True), R=[a_b, b_const], W=[pmtb], inc=(hh == 7))
            m_i, m_b = Mbf.next()
            M0 = Mbf.t[0:cl, m_i, 0:W8]
            E(ACT, lambda h: h.activation(out=M0, in_=pmt[0:cl, 0:W8], func=AF.Copy), R=[pmtb], W=[m_b])
            chk("d3")
            bg_step()
            pf_i, pf_b = Pf.next()
            PF = Pf.t[0:cl, pf_i, 0:W8]
            E(DVE, lambda h: h.tensor_tensor(out=PF.rearrange("p (h j) -> p h j", h=8),
                                             in0=ident_f[0:cl, 0:cl].unsqueeze(1).to_broadcast([cl, 8, cl]),
                                             in1=M0.rearrange("p (h j) -> p h j", h=8), op=ALU.subtract),
              R=[m_b, b_const], W=[pf_b])
            p_i, p_b = Pbf.next()
            Pk = Pbf.t[0:cl, p_i, 0:W8]
            E(ACT, lambda h: h.activation(out=Pk, in_=PF, func=AF.Copy), R=[pf_b], W=[p_b])
            chk("e0")
            bg_step()
            Ak, Ak_b, Mk, Mk_b = A0, a_b, M0, m_b
            for lev in range(1, nlev + 1):
                last = (lev == nlev)
                if lev == 2:
                    chk("e2")
                pa, pab_ = ps_next()
                for hh in range(8):
                    sl = slice(hh * cl, (hh + 1) * cl)
                    E(PE, lambda h: h.matmul(pa[0:cl, sl], Mk[:, sl], Ak[:, sl], start=True, stop=True),
                      R=[Mk_b, Ak_b], W=[pab_], inc=(hh == 7))
                if not last:
                    pm2, pm2b = ps_next()
                    for hh in range(8):
                        sl = slice(hh * cl, (hh + 1) * cl)
                        E(PE, lambda h: h.matmul(pm2[0:cl, sl], Ak[:, sl], Mk[:, sl], start=True, stop=True),
                          R=[Mk_b, Ak_b], W=[pm2b], inc=(hh == 7))
                chk("e1")
                bg_step()
                na_i, na_b = Abf.next()
                An = Abf.t[0:cl, na_i, 0:W8]
                E(ACT, lambda h: h.activation(out=An, in_=pa[0:cl, 0:W8], func=AF.Copy), R=[pab_], W=[na_b])
                if not last:
                    nm_i, nm_b = Mbf.next()
                    Mn = Mbf.t[0:cl, nm_i, 0:W8]
                    E(DVE, lambda h: h.tensor_copy(out=Mn, in_=pm2[0:cl, 0:W8]), R=[pm2b], W=[nm_b])
                pp_, ppb_ = ps_next()
                for hh in range(8):
                    sl = slice(hh * cl, (hh + 1) * cl)
                    E(PE, lambda h: h.matmul(pp_[0:cl, sl], An[:, sl], Pk[:, sl], start=True, stop=True),
                      R=[na_b, p_b], W=[ppb_], inc=(hh == 7))
                E(DVE, lambda h: h.tensor_tensor(out=PF, in0=PF, in1=pp_[0:cl, 0:W8], op=ALU.add), R=[ppb_], W=[pf_b])
                if not last:
                    np_i, np_b = Pbf.next()
                    Pn = Pbf.t[0:cl, np_i, 0:W8]
                    E(ACT, lambda h: h.activation(out=Pn, in_=PF, func=AF.Copy), R=[pf_b], W=[np_b])
                    Pk, p_b = Pn, np_b
                    Mk, Mk_b = Mn, nm_b
                Ak, Ak_b = An, na_b
            t_i, t_b = TTb.next()
            TT = TTb.t[0:cl, t_i, 0:W8]
            E(ACT, lambda h: h.activation(out=TT, in_=PF, func=AF.Copy), R=[pf_b], W=[t_b])
            chk("d4")
            bg_step()
            kw_i, kw_b = kw.next()
            kd_i, kd_b = kd.next()
            vb_i, vb_b = vb.next()

            def sc(col0, g0):
                return gc[:, col0 + g0:col0 + g0 + 4].unsqueeze(2).to_broadcast([cl, 4, 128])

            for grp in range(2):
                pkt, pktb = ps_next()
                pvt, pvtb = ps_next()
                for g in range(4):
                    hh = grp * 4 + g
                    E(PE, lambda h: h.matmul(pkt[0:cl, g * 128:(g + 1) * 128], kT[:, hh, o:o + cl], ident_b[:, :],
                                             start=True, stop=True), R=[b_kT[hh], b_const], W=[pktb], inc=(g == 3))
                for g in range(4):
                    hh = grp * 4 + g
                    E(PE, lambda h: h.matmul(pvt[0:cl, g * 128:(g + 1) * 128], vT[:, hh, o:o + cl], ident_b[:, :],
                                             start=True, stop=True), R=[b_vT[hh], b_const], W=[pvtb], inc=(g == 3))
                k3 = pkt[0:cl, :].rearrange("p (h d) -> p h d", h=4)
                v3 = pvt[0:cl, :].rearrange("p (h d) -> p h d", h=4)
                csl = slice(grp * 512, (grp + 1) * 512)
                E(DVE, lambda h: h.tensor_tensor(out=kw.t[0:cl, kw_i, csl].rearrange("p (h d) -> p h d", h=4), in0=k3,
                                                 in1=sc(16, grp * 4), op=ALU.mult), R=[pktb, gcb], W=[kw_b])
                E(DVE, lambda h: h.tensor_tensor(out=kd.t[0:cl, kd_i, csl].rearrange("p (h d) -> p h d", h=4), in0=k3,
                                                 in1=sc(24, grp * 4), op=ALU.mult), R=[pktb, gcb], W=[kd_b])
                E(DVE, lambda h: h.tensor_tensor(out=vb.t[0:cl, vb_i, csl].rearrange("p (h d) -> p h d", h=4), in0=v3,
                                                 in1=beta[0:cl, ck, grp * 4:grp * 4 + 4].unsqueeze(2).to_broadcast([cl, 4, 128]),
                                                 op=ALU.mult), R=[pvtb, b_beta], W=[vb_b])
            chk("d5")
            bg_step()
            pw, pwb = ps_next()
            for hh in range(8):
                E(PE, lambda h: h.matmul(pw[:, hh * cl:(hh + 1) * cl], kw.t[0:cl, kw_i, hh * 128:(hh + 1) * 128],
                                         TT[:, hh * cl:(hh + 1) * cl], start=True, stop=True), R=[kw_b, t_b], W=[pwb],
                  inc=(hh == 7))
            w_i, w_b = wTn.next()
            WT = wTn.t[:, w_i, 0:W8]
            E(ACT, lambda h: h.activation(out=WT, in_=pw[:, 0:W8], func=AF.Copy, scale=-1.0), R=[pwb], W=[w_b])
            e_i, e_b = Eq.next()
            EQ = Eq.t[:, e_i, 0:W8]
            E(ACT, lambda h: h.activation(out=EQ, in_=bps, func=AF.Exp), R=[b_ps[6]], W=[e_b])
            EQ3 = EQ.rearrange("p (h j) -> p h j", h=8)
            E(DVE, lambda h: h.tensor_tensor(out=qdT[:, :, o:o + cl], in0=qT[:, :, o:o + cl], in1=EQ3, op=ALU.mult),
              R=b_qT + [e_b], W=[b_qdT[ck]])
            chk("d6")
            bg_step()
            vn_i, vn_b = vn.next()
            VN = vn.t[0:cl, vn_i, :]
            for grp in range(2):
                pv_, pvb_ = ps_next()
                for g in range(4):
                    hh = grp * 4 + g
                    E(PE, lambda h: h.matmul(pv_[0:cl, g * 128:(g + 1) * 128], TT[:, hh * cl:(hh + 1) * cl],
                                             vb.t[0:cl, vb_i, hh * 128:(hh + 1) * 128], start=True, stop=False),
                      R=[t_b, vb_b], W=[pvb_], inc=False)
                    E(PE, lambda h: h.matmul(pv_[0:cl, g * 128:(g + 1) * 128], WT[:, hh * cl:(hh + 1) * cl],
                                             Sbf[:, hh, :], start=False, stop=True), R=[w_b, b_Sbf[hh]], W=[pvb_],
                      inc=(g == 3))
                if grp == 0:
                    E(ACT, lambda h: h.activation(out=VN[:, 0:512], in_=pv_[0:cl, :], func=AF.Copy), R=[pvb_], W=[vn_b])
                else:
                    E(DVE, lambda h: h.tensor_copy(out=VN[:, 512:1024], in_=pv_[0:cl, :]), R=[pvb_], W=[vn_b])
            po, pob = ps_next()
            for hh in range(8):
                E(PE, lambda h: h.matmul(po[:, hh * cl:(hh + 1) * cl], Sbf[:, hh, :], qdT[:, hh, o:o + cl], start=True,
                                         stop=False), R=[b_Sbf[hh], b_qdT[ck]], W=[pob], inc=False)
                E(PE, lambda h: h.matmul(po[:, hh * cl:(hh + 1) * cl], VN[:, hh * 128:(hh + 1) * 128],
                                         AQ[:, hh * cl:(hh + 1) * cl], start=False, stop=True), R=[vn_b, q_b], W=[pob],
                  inc=(hh == 7))
            E(ACT, lambda h: h.activation(out=oT[:, :, o:o + cl], in_=po[:, 0:W8].rearrange("p (h j) -> p h j", h=8),
                                          func=AF.Copy), R=[pob], W=[b_oT[ck]])
            bg_step()
            for grp in range(2):
                pd, pdb = ps_next()
                for g in range(4):
                    hh = grp * 4 + g
                    E(PE, lambda h: h.matmul(pd[:, g * 128:(g + 1) * 128], kd.t[0:cl, kd_i, hh * 128:(hh + 1) * 128],
                                             VN[:, hh * 128:(hh + 1) * 128], start=True, stop=True), R=[kd_b, vn_b],
                      W=[pdb], inc=(g == 3))
                for g in range(4):
                    hh = grp * 4 + g
                    E(DVE, lambda h: h.scalar_tensor_tensor(out=S[:, hh, :], in0=S[:, hh, :],
                                                            scalar=EQ[:, hh * cl + cl - 1:hh * cl + cl],
                                                            in1=pd[:, g * 128:(g + 1) * 128], op0=ALU.mult, op1=ALU.add),
                      R=[e_b, pdb], W=[b_S[hh]])
                g0 = grp * 4
                E(ACT, lambda h: h.activation(out=Sbf[:, g0:g0 + 4, :], in_=S[:, g0:g0 + 4, :], func=AF.Copy),
                  R=b_S[g0:g0 + 4], W=b_Sbf[g0:g0 + 4])
                bg_step()
        bg_step(len(bg))
        nstats = None
        if has_next:
            load_x(ti + 1)
            nstats = norm_stats((ti + 1) % 2, reg_subs)
        for hh in range(8):
            qi, qb = sqb.next()
            sq = sqb.t[:, qi, 0:nt]
            E(ACT, lambda h: h.activation(out=sq, in_=oT[:, hh, 0:nt], func=AF.Square), R=b_oT[0:nck], W=[qb])
            E(PE, lambda h: h.matmul(ps[0:16, 7, 0:nt], Esel[:, hh, :], sq, start=(hh == 0), stop=(hh == 7)),
              R=[qb, b_const], W=[b_ps[7]])
        E(ACT, lambda h: h.activation(out=rsn[:, 0:nt], in_=ps[0:16, 7, 0:nt], func=AF.Sqrt, bias=eps_c[0:16],
                                      scale=1.0 / 128.0), R=[b_ps[7], b_const], W=[b_rsn])
        E(DVE, lambda h: h.reciprocal(out=rsn[:, 0:nt], in_=rsn[:, 0:nt]), W=[b_rsn])
        for hh in range(8):
            pt, pb = ps_next()
            E(PE, lambda h: h.matmul(pt[:, 0:nt], ident_f[0:16, hh:hh + 1].to_broadcast([16, 128]), rsn[:, 0:nt], start=True,
                                     stop=True), R=[b_rsn, b_const], W=[pb])
            ri, rb = rr.next()
            r = rr.t[:, ri, 0:nt]
            E(DVE, lambda h: h.tensor_tensor(out=r, in0=oT[:, hh, 0:nt], in1=pt[:, 0:nt], op=ALU.mult), R=b_oT[0:nck] + [pb],
              W=[rb])
            E(DVE, lambda h: h.scalar_tensor_tensor(out=od[:, hh, 0:nt], in0=r, scalar=pcol(P_DNW), in1=sz[:, hh, 0:nt],
                                                    op0=ALU.mult, op1=ALU.mult), R=[rb, b_sz[hh], b_const], W=[b_od[hh]])

        pm, pmb = ps_next()
        pq, pqb = ps_next()
        for j in range(8):
            yi, yb_ = ybf.next()
            qi, qb_ = ysq.next()
            E(ACT, lambda h: h.activation(out=ybf.t[:, yi, 0:nt], in_=ybuf[:, j, 0:nt], func=AF.Copy), R=[b_y[j]], W=[yb_])
            E(ACT, lambda h: h.activation(out=ysq.t[:, qi, 0:nt], in_=ybuf[:, j, 0:nt], func=AF.Square), R=[b_y[j]], W=[qb_])
            E(PE, lambda h: h.matmul(pm[:, 0:nt], ones_b[:, :], ybf.t[:, yi, 0:nt], start=(j == 0), stop=(j == 7)),
              R=[yb_, b_const], W=[pmb])
            E(PE, lambda h: h.matmul(pq[:, 0:nt], ones_b[:, :], ysq.t[:, qi, 0:nt], start=(j == 0), stop=(j == 7)),
              R=[qb_, b_const], W=[pqb])
        m_ = lnt[:, 0, 0:nt]
        msq = lnt[:, 1, 0:nt]
        var = lnt[:, 2, 0:nt]
        rstd = lnt[:, 3, 0:nt]
        mr = lnt[:, 4, 0:nt]
        E(DVE, lambda h: h.tensor_scalar(out=m_, in0=pm[:, 0:nt], scalar1=1.0 / D, scalar2=None, op0=ALU.mult),
          R=[pmb], W=[b_lnt[0]])
        E(DVE, lambda h: h.tensor_tensor(out=msq, in0=m_, in1=m_, op=ALU.mult), R=[b_lnt[0]], W=[b_lnt[1]])
        E(DVE, lambda h: h.scalar_tensor_tensor(out=var, in0=pq[:, 0:nt], scalar=1.0 / D, in1=msq, op0=ALU.mult,
                                                op1=ALU.subtract), R=[pqb, b_lnt[1]], W=[b_lnt[2]])
        E(ACT, lambda h: h.activation(out=rstd, in_=var, func=AF.Sqrt, bias=eps_c, scale=1.0), R=[b_lnt[2], b_const],
          W=[b_lnt[3]])
        E(DVE, lambda h: h.reciprocal(out=rstd, in_=rstd), W=[b_lnt[3]])
        E(DVE, lambda h: h.tensor_tensor(out=mr, in0=m_, in1=rstd, op=ALU.mult), R=[b_lnt[0], b_lnt[3]], W=[b_lnt[4]])
        for j in range(8):
            ai, ab = accA.next()
            aa = accA.t[:, ai, 0:nt]
            E(DVE, lambda h: h.tensor_tensor(out=aa, in0=ybuf[:, j, 0:nt], in1=rstd, op=ALU.mult), R=[b_y[j], b_lnt[3]],
              W=[ab])
            E(DVE, lambda h: h.tensor_tensor(out=aa, in0=aa, in1=mr, op=ALU.subtract), R=[b_lnt[4]], W=[ab])
            E(ACT, lambda h: h.activation(out=cact[:, j, 0:nt], in_=aa, func=AF.Silu, bias=pcol(P_LNB + j),
                                          scale=pcol(P_LNW + j)), R=[ab, b_const], W=[b_cact[j]])

        chk("ph4")
        for j in range(8):
            if j % 4 == 0:
                wco_ = wload(16 + j // 4)
                wdn_ = wload(18 + j // 4)
            wt, wb = wco_
            pa_, pab2 = proj_chunk(wt, wb, j % 4, nt, cact, b_cact)
            wt, wb = wdn_
            pb_, pbb2 = proj_chunk(wt, wb, j % 4, nt, od, b_od)
            mi, mb = mtmp.next()
            mt_ = mtmp.t[:, mi, 0:nt]
            E(DVE, lambda h: h.scalar_tensor_tensor(out=mt_, in0=gts[:, j, 0:nt], scalar=1.0, in1=pa_, op0=ALU.add,
                                                    op1=ALU.mult), R=[pab2, b_gts[j]], W=[mb])
            mi2, mb2 = mtmp.next()
            mt2 = mtmp.t[:, mi2, 0:nt]
            E(DVE, lambda h: h.scalar_tensor_tensor(out=mt2, in0=gts[:, 8 + j, 0:nt], scalar=1.0, in1=pb_, op0=ALU.add,
                                                    op1=ALU.mult), R=[pbb2, b_gts[8 + j]], W=[mb2])
            E(POOL, lambda h: h.tensor_tensor(out=mT[:, j, 0:nt], in0=mt_, in1=mt2, op=ALU.add), R=[mb, mb2], W=[b_mT[j]])
        for half in range(2):
            wt, wb = wload(20 + half)
            for (s, rows) in subs:
                pt, pb = ps_next()
                for kc in range(8):
                    E(PE, lambda h: h.matmul(pt[0:rows, :], mT[:, kc, s * 128:s * 128 + rows], wt[:, kc, :],
                                             start=(kc == 0), stop=(kc == 7)), R=[b_mT[kc], wb], W=[pb], inc=(kc == 7))
                hsl = htok[0:rows, hs, s, half * 512:(half + 1) * 512]
                E(DVE, lambda h: h.scalar_tensor_tensor(out=hsl, in0=pt[0:rows, :], scalar=0.5, in1=hsl, op0=ALU.mult,
                                                        op1=ALU.add), R=[pb], W=[b_htok[hs][s]])

        chk("ph5")
        if has_next:
            norm_apply((ti + 1) % 2, reg_subs, T, P_MIXW, nstats)

    def make_back(ti, is_meta):
        hs = ti % 2
        if is_meta:
            nt = NMETA
            subs = [(0, NMETA)]
        else:
            nt = T
            subs = reg_subs
        items = []
        st_ = {}

        def it_stats():
            st_["ns"] = norm_stats(hs, subs)

        def it_apply():
            norm_apply(hs, subs, nt, P_FFW, st_["ns"])

        ffn_list = [(jp, g) for jp in range(11) for g in range(4)]
        fs = {}

        def f_st0(i):
            jp, g = ffn_list[i]
            if g == 0:
                if jp % 2 == 0:
                    fs["dg", jp // 2] = dload(11 + jp // 2)
                fs["w"] = wload(22 + jp)
            wt, wb = fs["w"]
            j = 2 * jp + g // 2
            ch = j if g % 2 == 0 else NFF + j
            pp, ppb = proj_chunk(wt, wb, g, nt, uT, b_uT)
            pi, prb = preu.next()
            phb = b_preuh[pi]
            pr = preu.t[:, pi, :]
            E(POOL, lambda h: h.tensor_copy(out=pr[:, 0:2], in_=halu[:, ch, :]), R=[b_halu[ch]], W=[phb])
            if g % 2 == 0:
                E(ACT, lambda h: h.activation(out=pr[:, 2:2 + nt], in_=pp, func=AF.Copy), R=[ppb], W=[prb])
            else:
                E(DVE, lambda h: h.tensor_copy(out=pr[:, 2:2 + nt], in_=pp), R=[ppb], W=[prb])
            E(POOL, lambda h: h.tensor_copy(out=halu[:, ch, :], in_=pr[:, nt:nt + 2]), R=[prb], W=[b_halu[ch]])
            fs[i] = (pr, prb, phb)

        def f_st1(i):
            jp, g = ffn_list[i]
            j = 2 * jp + g // 2
            ch = j if g % 2 == 0 else NFF + j
            pr, prb, phb = fs[i]
            dgu, dgub = fs["dg", jp // 2]
            pc3, pc3b = ps_next()
            for t_ in range(3):
                mi_ = ((jp % 2) * 4 + g) * 3 + t_
                E(PE, lambda h: h.matmul(pc3[:, 0:nt], dgu[:, mi_ * 128:(mi_ + 1) * 128], pr[:, t_:t_ + nt],
                                         start=(t_ == 0), stop=(t_ == 2)), R=[dgub, prb, phb], W=[pc3b], inc=(t_ == 2))
            if g % 2 == 0:
                si_, sgb = sg.next()
                sga = sg.t[:, si_, 0:nt]
                E(ACT, lambda h: h.activation(out=sga, in_=pc3[:, 0:nt], func=AF.Silu, bias=pcol(P_FDB + ch)),
                  R=[pc3b, b_const], W=[sgb])
                fs["sg"] = (sga, sgb)
            else:
                sga, sgb = fs["sg"]
                E(DVE, lambda h: h.scalar_tensor_tensor(out=actb[:, j, 0:nt], in0=pc3[:, 0:nt], scalar=pcol(P_FDB + ch),
                                                        in1=sga, op0=ALU.add, op1=ALU.mult), R=[pc3b, sgb, b_const],
                  W=[b_act[j]])

        def it_up(i):
            if 0 <= i - 2 < 44:
                f_st1(i - 2)
            if i < 44:
                f_st0(i)

        def it_down(half):
            pts = [ps_next() for _ in subs]
            for grp in range(3):
                wt, wb = wload(33 + half * 3 + grp)
                nk = 8 if grp < 2 else NFF - 16
                for si_, (s, rows) in enumerate(subs):
                    pt, pb = pts[si_]
                    for kk in range(nk):
                        kc = grp * 8 + kk
                        last = (kc == NFF - 1)
                        E(PE, lambda h: h.matmul(pt[0:rows, :], actb[:, kc, s * 128:s * 128 + rows], wt[:, kk, :],
                                                 start=(kc == 0), stop=last), R=[b_act[kc], wb], W=[pb],
                          inc=(kk == nk - 1))
            for si_, (s, rows) in enumerate(subs):
                pt, pb = pts[si_]
                hsl = htok[0:rows, hs, s, half * 512:(half + 1) * 512]
                E(DVE, lambda h: h.tensor_tensor(out=hsl, in0=hsl, in1=pt[0:rows, :], op=ALU.add), R=[pb],
                  W=[b_htok[hs][s]])

        def it_final(si_):
            (s, rows) = subs[si_]
            x0 = (ti - 1) * T
            si, sbuf_ = stat.next()
            st = stat.t[0:rows, si, :]
            hin = htok[0:rows, hs, s, :]
            E(ACT, lambda h: h.activation(out=junk[0:rows, :], in_=hin, func=AF.Square, accum_out=st[:, 0:1]),
              R=[b_htok[hs][s]], W=[b_junk, sbuf_])
            E(ACT, lambda h: h.activation(out=st[:, 1:2], in_=st[:, 0:1], func=AF.Sqrt, bias=eps_c[0:rows],
                                          scale=1.0 / D), R=[b_const], W=[sbuf_])
            E(DVE, lambda h: h.reciprocal(out=st[:, 2:3], in_=st[:, 1:2]), W=[sbuf_])
            E(DVE, lambda h: h.scalar_tensor_tensor(out=hin, in0=hin, scalar=st[:, 2:3],
                                                    in1=tokp[0:rows, TP_NFW:TP_NFW + D], op0=ALU.mult, op1=ALU.mult),
              R=[sbuf_, b_const], W=[b_htok[hs][s]])
            cx.dma(ACT, out_d[x0 + s * 128:x0 + s * 128 + rows, :], hin, R=[b_htok[hs][s]])

        items.append(it_stats)
        items.append(it_apply)
        for i in range(44 + 2):
            items.append(lambda i=i: it_up(i))
        for half in range(2):
            items.append(lambda half=half: it_down(half))
        if not is_meta:
            for si_ in range(len(subs)):
                items.append(lambda si_=si_: it_final(si_))
        return items

    try:
        chk("setup")
        back_items = []
        do_tile(0, True, ntile >= 1, back_items)
        back_items = make_back(0, True)
        chk("meta")
        for ti in range(1, ntile + 1):
            do_tile(ti, False, ti < ntile, back_items)
            back_items = make_back(ti, False)
        for it in back_items:
            it()
    except _Stop:
        pass
    cx.finish()
    es.close()
    return nc, cx


def _pack_weights(w_in, w_conf_out, w_dn_out, w_out, w_up, w_down):
    blocks = np.zeros((NBLK, 128, 8, 512), np.float32)

    def colblock(W, c0s):
        W3 = W.reshape(8, 128, W.shape[1])
        out = np.empty((128, 8, 512), np.float32)
        for g, c0 in enumerate(c0s):
            out[:, :, g * 128:(g + 1) * 128] = W3[:, :, c0:c0 + 128].transpose(1, 0, 2)
        return out

    b = 0
    for jp in range(4):
        j0, j1 = 2 * jp, 2 * jp + 1
        blocks[b] = colblock(w_in, [1024 + j0 * 128, j0 * 128, 1024 + j1 * 128, j1 * 128]); b += 1
    for base in (3072, 2048, 4096, 5120):
        for hb in range(2):
            blocks[b] = colblock(w_in, [base + (hb * 4 + g) * 128 for g in range(4)]); b += 1
    for base in (6160, 7184):
        for hb in range(2):
            blocks[b] = colblock(w_in, [base + (hb * 4 + g) * 128 for g in range(4)]); b += 1
    for W in (w_conf_out, w_dn_out):
        for hb in range(2):
            blocks[b] = colblock(W, [(hb * 4 + g) * 128 for g in range(4)]); b += 1
    for half in range(2):
        blocks[b] = w_out.reshape(8, 128, 1024)[:, :, half * 512:(half + 1) * 512].transpose(1, 0, 2); b += 1
    for jp in range(11):
        j0, j1 = 2 * jp, 2 * jp + 1
        blocks[b] = colblock(w_up, [j0 * 128, DFF + j0 * 128, j1 * 128, DFF + j1 * 128]); b += 1
    Wd = w_down.reshape(NFF, 128, 1024)
    for half in range(2):
        for grp in range(3):
            nk = 8 if grp < 2 else NFF - 16
            blocks[b][:, 0:nk, :] = Wd[grp * 8:grp * 8 + nk, :, half * 512:(half + 1) * 512].transpose(1, 0, 2); b += 1
    assert b == NBLK
    return blocks.reshape(NBLK, 128, 8 * 512)


def _pack_params(inp):
    par = np.zeros((128, NPAR), np.float32)

    def cols(v):
        return v.reshape(-1, 128).T

    par[:, P_MIXW:P_MIXW + 8] = cols(inp["norm_mix_w"][0])
    par[:, P_BGATE:P_BGATE + 16] = cols(inp["b_gate"][0])
    cdw = inp["conf_dw_w"][0]
    par[:, P_CDW:P_CDW + 8 * CK] = cdw.reshape(CK, 8, 128).transpose(2, 1, 0).reshape(128, 8 * CK)
    par[:, P_CDB:P_CDB + 8] = cols(inp["conf_dw_b"][0])
    par[:, P_LNW:P_LNW + 8] = cols(inp["conf_ln_w"][0])
    par[:, P_LNB:P_LNB + 8] = cols(inp["conf_ln_b"][0])
    dnc = inp["dn_conv_w"][0]
    par[:, P_DNC:P_DNC + 96] = dnc.reshape(4, 24, 128).transpose(2, 1, 0).reshape(128, 96)
    par[:, P_DNW] = inp["dn_norm_w"][0]
    par[:, P_FFW:P_FFW + 8] = cols(inp["norm_ffn_w"][0])
    fdw = inp["ffn_dw_w"][0]
    par[:, P_FDW:P_FDW + 132] = fdw.reshape(3, 44, 128).transpose(2, 1, 0).reshape(128, 132)
    par[:, P_FDB:P_FDB + 44] = cols(inp["ffn_dw_b"][0])
    fcols = par[:, P_FDW:P_FDW + 132].copy()
    for jp in range(11):
        for g in range(4):
            j = 2 * jp + g // 2
            ch = j if g % 2 == 0 else NFF + j
            par[:, P_FDWB + (jp * 4 + g) * 3:P_FDWB + (jp * 4 + g) * 3 + 3] = fcols[:, ch * 3:ch * 3 + 3]
    tokp = np.zeros((128, NTOKP), np.float32)
    tokp[:, TP_DTB:TP_DTB + 8] = inp["dn_dt_bias"][0][None, :]
    tokp[:, TP_ALOG:TP_ALOG + 8] = inp["dn_A_log"][0][None, :]
    tokp[:, TP_NFW:] = inp["norm_final_w"][None, :]
    return par, tokp


_CACHE = {}


def kernel(**inputs):
    inp = {k: np.asarray(v, dtype=np.float32) for k, v in inputs.items()}
    x = inp["x"]
    B, n_x, _ = x.shape
    wpack = _pack_weights(inp["w_in"][0], inp["w_conf_out"][0], inp["w_dn_out"][0], inp["w_out"][0], inp["w_up"][0],
                          inp["w_down"][0])
    wab = np.ascontiguousarray(inp["w_in"][0][:, 6144:6160].reshape(8, 128, 16).transpose(1, 0, 2)).reshape(128, 128)
    par, tokp = _pack_params(inp)
    if n_x not in _CACHE:
        _CACHE[n_x] = build(n_x)[0]
    nc = _CACHE[n_x]
    in_maps = []
    for b in range(B):
        in_maps.append({"x": np.ascontiguousarray(x[b]), "meta": inp["meta_tokens"], "wpack": wpack, "wab": wab,
                        "params": par, "tokpar": tokp})
    res = run_bass_kernel_spmd(nc, in_maps, core_ids=list(range(B)))
    return np.stack([np.asarray(r["out"], dtype=np.float32) for r in res.results], axis=0)
```

```python
from contextlib import ExitStack

import numpy as np
import concourse.bass as bass
import concourse.mybir as mybir
from concourse.bass_utils import run_bass_kernel_spmd

F32 = mybir.dt.float32
BF16 = mybir.dt.bfloat16
ALU = mybir.AluOpType
AF = mybir.ActivationFunctionType

D = 1024
NMETA = 16
CK = 31
DFF = 2816
NFF = DFF // 128
EPS = 1e-6
T = 256
NBLK = 39
NEG = -30000.0

P_MIXW = 0
P_BGATE = P_MIXW + 8
P_CDW = P_BGATE + 16
P_CDB = P_CDW + 8 * CK
P_LNW = P_CDB + 8
P_LNB = P_LNW + 8
P_DNC = P_LNB + 8
P_DNW = P_DNC + 24 * 4
P_FFW = P_DNW + 1
P_FDW = P_FFW + 8
P_FDB = P_FDW + 44 * 3
P_FDWB = P_FDB + 44
NPAR = P_FDWB + 132
NDG = 17
TP_DTB = 0
TP_ALOG = 8
TP_NFW = 16
NTOKP = 16 + D


class Eng:
    def __init__(self, name, h, sem, inc):
        self.name = name
        self.h = h
        self.sem = sem
        self.inc = inc
        self.count = 0
        self.known = {}


class Buf:
    __slots__ = ("name", "w", "r")

    def __init__(self, name):
        self.name = name
        self.w = None
        self.r = {}


class Ctx:
    def __init__(self, nc, es, ndma=24, nsw=40):
        self.nc = nc
        self.es = es
        mk = lambda n: es.enter_context(nc.semaphore(n))
        self.PE = Eng("PE", nc.tensor, mk("s_pe"), 1)
        self.ACT = Eng("ACT", nc.scalar, mk("s_act"), 1)
        self.DVE = Eng("DVE", nc.vector, mk("s_dve"), 1)
        self.POOL = Eng("POOL", nc.gpsimd, mk("s_pool"), 1)
        self.SP = Eng("SP", nc.sync, None, 0)
        self.dsems = [Eng("D%d" % i, None, mk("s_d%d" % i), 16) for i in range(ndma)]
        self.dsems_sw = [Eng("W%d" % i, None, mk("s_w%d" % i), 16) for i in range(nsw)]
        self.dnext = 0
        self.dnext_sw = 0
        self.ninstr = 0

    def _deps(self, R, W):
        d = {}
        for b in R:
            if b.w is not None:
                e, i = b.w
                if d.get(e, 0) < i:
                    d[e] = i
        for b in W:
            if b.w is not None:
                e, i = b.w
                if d.get(e, 0) < i:
                    d[e] = i
            for e, i in b.r.items():
                if d.get(e, 0) < i:
                    d[e] = i
        return d

    def _waits(self, eng, d):
        for src, idx in d.items():
            if src is eng and eng is self.PE:
                continue
            if eng.known.get(src, 0) >= idx:
                continue
            assert idx <= src.count, "dependency on un-signalled instruction of %s" % src.name
            eng.h.wait_ge(src.sem, idx * src.inc)
            eng.known[src] = idx

    def _mark(self, p, R, W):
        e, i = p
        for b in R:
            if b.r.get(e, 0) < i:
                b.r[e] = i
        for b in W:
            b.w = p
            b.r = {}

    def emit(self, eng, fn, R=(), W=(), inc=True):
        self._waits(eng, self._deps(R, W))
        ins = fn(eng.h)
        self.ninstr += 1
        if inc:
            eng.count += 1
            ins.then_inc(eng.sem, 1)
            idx = eng.count
        else:
            idx = eng.count + 1
        self._mark((eng, idx), R, W)
        return ins

    def dma(self, q, out, in_, R=(), W=()):
        self._waits(q, self._deps(R, W))
        if q is self.POOL:
            ds = self.dsems_sw[self.dnext_sw]
            self.dnext_sw = (self.dnext_sw + 1) % len(self.dsems_sw)
        else:
            ds = self.dsems[self.dnext]
            self.dnext = (self.dnext + 1) % len(self.dsems)
        if q.known.get(ds, 0) < ds.count:
            q.h.wait_ge(ds.sem, ds.count * 16)
            q.known[ds] = ds.count
        q.h.dma_start(out=out, in_=in_).then_inc(ds.sem, 16)
        ds.count += 1
        self.ninstr += 1
        self._mark((ds, ds.count), R, W)

    def finish(self):
        for ds in self.dsems + self.dsems_sw:
            if ds.count and self.SP.known.get(ds, 0) < ds.count:
                self.SP.h.wait_ge(ds.sem, ds.count * 16)


class Rot:
    def __init__(self, name, tens, n):
        self.t = tens
        self.n = n
        self.bufs = [Buf("%s%d" % (name, i)) for i in range(n)]
        self.i = -1

    def next(self):
        self.i = (self.i + 1) % self.n
        return self.i, self.bufs[self.i]


class _Stop(Exception):
    pass


def build(n_x=4096, dbg=False, stop_after=None):
    nc = bass.Bass("TRN2", target_bir_lowering=False)
    es = ExitStack()
    cx = Ctx(nc, es)
    PE, ACT, DVE, POOL, SP = cx.PE, cx.ACT, cx.DVE, cx.POOL, cx.SP
    E = cx.emit
    NS = T // 128
    NCK = T // 64
    ntile = n_x // T

    x_d = nc.dram_tensor("x", [n_x, D], F32, kind="ExternalInput").ap()
    meta_d = nc.dram_tensor("meta", [NMETA, D], F32, kind="ExternalInput").ap()
    wpack_d = nc.dram_tensor("wpack", [NBLK, 128, 8 * 512], F32, kind="ExternalInput").ap()
    wab_d = nc.dram_tensor("wab", [128, 8 * 16], F32, kind="ExternalInput").ap()
    par_d = nc.dram_tensor("params", [128, NPAR], F32, kind="ExternalInput").ap()
    tokp_d = nc.dram_tensor("tokpar", [128, NTOKP], F32, kind="ExternalInput").ap()
    out_d = nc.dram_tensor("out", [n_x, D], F32, kind="ExternalOutput").ap()
    wbf_d = nc.dram_tensor("wbf", [NBLK, 128, 8 * 512], BF16, kind="Internal").ap()
    dg_d = nc.dram_tensor("dgd", [NDG, 128, 8 * 512], BF16, kind="Internal").ap()
    dbg_d = {}

    def sb(name, shape, dt):
        return es.enter_context(nc.sbuf_tensor(name, shape, dt))

    htok = sb("htok", [128, 2, NS, D], F32)
    b_htok = [[Buf("htok%d_%d" % (i, s)) for s in range(NS)] for i in range(2)]
    xn = Rot("xn", sb("xn", [128, 1, D], F32), 1)
    junk = sb("junk", [128, D], BF16)
    b_junk = Buf("junk")
    stat = Rot("stat", sb("stat", [128, 8, 4], F32), 8)
    uT = sb("uT", [128, 8, T], BF16)
    b_uT = [Buf("uT%d" % c) for c in range(8)]
    cbuf = sb("cbuf", [128, 8, 30 + T], BF16)
    b_c = [Buf("c%d" % c) for c in range(8)]
    ctmp = sb("ctmp", [128, 32], BF16)
    b_ctmp = Buf("ctmp")
    ybuf = sb("ybuf", [128, 8, T], F32)
    b_y = [Buf("y%d" % c) for c in range(8)]
    accA = Rot("accA", sb("accA", [128, 2, T], F32), 2)
    ybf = Rot("ybf", sb("ybf", [128, 2, T], BF16), 2)
    ysq = Rot("ysq", sb("ysq", [128, 2, T], BF16), 2)
    lnt = sb("lnt", [128, 5, T], F32)
    b_lnt = [Buf("lnt%d" % i) for i in range(5)]
    cact = sb("cact", [128, 8, T], BF16)
    b_cact = [Buf("cact%d" % c) for c in range(8)]
    sig = Rot("sig", sb("sig", [128, 2, T], F32), 2)
    pre = Rot("pre", sb("pre", [128, 3, 3 + T], BF16), 3)
    halq = sb("halq", [128, 24, 3], BF16)
    b_halq = [Buf("halq%d" % c) for c in range(24)]
    b_preh = [Buf("preh%d" % c) for c in range(3)]
    b_preuh = [Buf("preuh%d" % c) for c in range(4)]
    sqb = Rot("sqb", sb("sqb", [128, 2, T], BF16), 2)
    rr = Rot("rr", sb("rr", [128, 2, T], F32), 2)
    qT = sb("qT", [128, 8, T], BF16)
    kT = sb("kT", [128, 8, T], BF16)
    vT = sb("vT", [128, 8, T], BF16)
    qdT = sb("qdT", [128, 8, T], BF16)
    b_qT = [Buf("qT%d" % c) for c in range(8)]
    b_kT = [Buf("kT%d" % c) for c in range(8)]
    b_vT = [Buf("vT%d" % c) for c in range(8)]
    b_qdT = [Buf("qdT%d" % c) for c in range(NCK)]
    sz = sb("sz", [128, 8, T], BF16)
    b_sz = [Buf("sz%d" % c) for c in range(8)]
    gts = sb("gts", [128, 16, T], BF16)
    b_gts = [Buf("gts%d" % c) for c in range(16)]
    oT = sb("oT", [128, 8, T], F32)
    b_oT = [Buf("oT%d" % c) for c in range(NCK)]
    od = sb("od", [128, 8, T], BF16)
    b_od = [Buf("od%d" % c) for c in range(8)]
    mT = sb("mT", [128, 8, T], BF16)
    b_mT = [Buf("mT%d" % c) for c in range(8)]
    mtmp = Rot("mtmp", sb("mtmp", [128, 2, T], F32), 2)
    actb = sb("actb", [128, NFF, T], BF16)
    b_act = [Buf("act%d" % c) for c in range(NFF)]
    preu = Rot("preu", sb("preu", [128, 4, 2 + T], BF16), 4)
    halu = sb("halu", [128, 44, 2], BF16)
    b_halu = [Buf("halu%d" % c) for c in range(44)]
    sg = Rot("sg", sb("sg", [128, 2, T], F32), 2)
    abt = sb("abt", [64, NCK, 16], F32)
    b_abt = Buf("abt")
    gtok = sb("gtok", [64, NCK, 8], F32)
    beta = sb("beta", [64, NCK, 8], F32)
    b_g = Buf("g")
    b_beta = Buf("beta")
    GU = Rot("GU", sb("GU", [64, 1, 512], F32), 1)
    gcol = Rot("gcol", sb("gcol", [64, 2, 32], F32), 2)
    D1 = Rot("D1", sb("D1", [64, 1, 512], F32), 1)
    D2 = Rot("D2", sb("D2", [64, 1, 512], F32), 1)
    Abf = Rot("Abf", sb("Abf", [64, 3, 512], BF16), 3)
    Mbf = Rot("Mbf", sb("Mbf", [64, 3, 512], BF16), 3)
    Pbf = Rot("Pbf", sb("Pbf", [64, 2, 512], BF16), 2)
    Pf = Rot("Pf", sb("Pf", [64, 1, 512], F32), 1)
    AqkT = Rot("AqkT", sb("AqkT", [64, 1, 512], BF16), 1)
    TTb = Rot("TTb", sb("TTb", [64, 1, 512], BF16), 1)
    kw = Rot("kw", sb("kw", [64, 1, 1024], BF16), 1)
    kd = Rot("kd", sb("kd", [64, 1, 1024], BF16), 1)
    vb = Rot("vb", sb("vb", [64, 1, 1024], BF16), 1)
    vn = Rot("vn", sb("vn", [64, 1, 1024], BF16), 1)
    wTn = Rot("wTn", sb("wTn", [128, 1, 512], BF16), 1)
    Eq = Rot("Eq", sb("Eq", [128, 1, 512], F32), 1)
    S = sb("S", [128, 8, 128], F32)
    Sbf = sb("Sbf", [128, 8, 128], BF16)
    b_S = [Buf("S%d" % h) for h in range(8)]
    b_Sbf = [Buf("Sbf%d" % h) for h in range(8)]
    NRING = 5
    wring = Rot("wring", sb("wring", [128, NRING, 8 * 512], BF16), NRING)
    wab = sb("wab_sb", [128, 8, 16], BF16)
    wabf = sb("wabf", [128, 8 * 16], F32)
    b_wab = Buf("wab")
    ident_f = sb("ident_f", [128, 128], F32)
    ident_b = sb("ident_b", [128, 128], BF16)
    ones_b = sb("ones_b", [128, 128], BF16)
    ones_f = sb("ones_f", [64, 128], F32)
    Umat = sb("Umat", [64, 64], F32)
    par = sb("par_sb", [128, NPAR], F32)
    tokp = sb("tokp", [128, NTOKP], F32)
    cst = sb("cst", [128, 4], F32)
    Esel = sb("Esel", [128, 16, 16], BF16)
    identq = sb("identq", [16, 16], F32)
    rsn = sb("rsn", [16, T], F32)
    b_rsn = Buf("rsn")
    b_const = Buf("const")
    b_wbf = [Buf("wbf%d" % i) for i in range(NBLK)]
    b_dg = [Buf("dg%d" % i) for i in range(NDG)]

    ps = es.enter_context(nc.psum_tensor("ps", [128, 8, 512], F32))
    b_ps = [Buf("ps%d" % i) for i in range(8)]
    ps_state = {"i": -1}

    def ps_next():
        ps_state["i"] = (ps_state["i"] + 1) % 6
        i = ps_state["i"]
        return ps[:, i, :], b_ps[i]

    cx.dma(SP, par[:, :], par_d[:, :], W=[b_const])
    cx.dma(SP, tokp[:, :], tokp_d[:, :], W=[b_const])
    cx.dma(SP, wabf[:, :], wab_d[:, :], W=[b_wab])
    for b in range(NBLK):
        cx.dma(POOL, wbf_d[b], wpack_d[b], W=[b_wbf[b]])

    def pool_c(fn):
        E(POOL, fn, W=[b_const])

    pool_c(lambda h: h.memset(ident_f[:], 0.0))
    pool_c(lambda h: h.affine_select(out=ident_f[:], in_=ident_f[:], pattern=[[-1, 128]],
                                     compare_op=ALU.not_equal, fill=1.0, base=0, channel_multiplier=1))
    pool_c(lambda h: h.tensor_copy(out=ident_b[:], in_=ident_f[:]))
    pool_c(lambda h: h.memset(ones_b[:], 1.0))
    pool_c(lambda h: h.memset(ones_f[:], 1.0))
    pool_c(lambda h: h.memset(Umat[:], 1.0))
    pool_c(lambda h: h.affine_select(out=Umat[:], in_=Umat[:], pattern=[[1, 64]],
                                     compare_op=ALU.is_ge, fill=0.0, base=0, channel_multiplier=-1))
    pool_c(lambda h: h.memset(Esel[:], 1.0))
    pool_c(lambda h: h.affine_select(out=Esel[:], in_=Esel[:], pattern=[[1, 16], [-1, 16]], compare_op=ALU.is_equal,
                                     fill=0.0, base=0, channel_multiplier=0))
    pool_c(lambda h: h.tensor_copy(out=identq[:], in_=ident_f[0:16, 0:16]))
    pool_c(lambda h: h.tensor_scalar(out=identq[:, 8:16], in0=identq[:, 8:16], scalar1=128.0 ** -0.5, scalar2=None,
                                     op0=ALU.mult))
    pool_c(lambda h: h.memset(cst[:, 0:1], EPS))
    pool_c(lambda h: h.memset(cst[:, 1:2], 1.0))
    pool_c(lambda h: h.memset(cbuf[:], 0.0))
    pool_c(lambda h: h.memset(halq[:], 0.0))
    pool_c(lambda h: h.memset(halu[:], 0.0))
    pool_c(lambda h: h.memset(S[:], 0.0))
    pool_c(lambda h: h.memset(Sbf[:], 0.0))
    E(POOL, lambda h: h.tensor_copy(out=wab[:].rearrange("p a b -> p (a b)"), in_=wabf[:]), R=[], W=[b_wab])
    E(ACT, lambda h: h.activation(out=tokp[:, TP_ALOG:TP_ALOG + 8], in_=tokp[:, TP_ALOG:TP_ALOG + 8], func=AF.Exp),
      W=[b_const])
    E(DVE, lambda h: h.tensor_scalar(out=tokp[:, TP_ALOG:TP_ALOG + 8], in0=tokp[:, TP_ALOG:TP_ALOG + 8],
                                     scalar1=-1.0, scalar2=None, op0=ALU.mult), W=[b_const])
    E(DVE, lambda h: h.tensor_scalar(out=par[:, P_CDW:P_CDW + 8 * CK], in0=par[:, P_CDW:P_CDW + 8 * CK], scalar1=0.5,
                                     scalar2=None, op0=ALU.mult), W=[b_const])
    E(DVE, lambda h: h.tensor_scalar(out=par[:, P_BGATE:P_BGATE + 16], in0=par[:, P_BGATE:P_BGATE + 16], scalar1=0.5,
                                     scalar2=None, op0=ALU.mult), W=[b_const])
    E(DVE, lambda h: h.tensor_scalar(out=par[:, P_DNW:P_DNW + 1], in0=par[:, P_DNW:P_DNW + 1], scalar1=0.5,
                                     scalar2=None, op0=ALU.mult), W=[b_const])
    all_state = b_c + b_halq + b_halu + b_S + b_Sbf
    for b in all_state:
        b.w = b_const.w

    neg_reg = nc.gpsimd.to_reg(NEG)
    def build_dg(d, poff, nm):
        i, wb_ = wring.next()
        slot = wring.t[:, i, 0:nm * 128].rearrange("p (m c) -> p m c", c=128)
        E(DVE, lambda h: h.tensor_tensor(out=slot, in0=ident_b[:, :].unsqueeze(1).to_broadcast([128, nm, 128]),
                                         in1=par[:, poff:poff + nm].unsqueeze(2).to_broadcast([128, nm, 128]), op=ALU.mult),
          R=[b_const], W=[wb_])
        cx.dma(SP, dg_d[d][:, 0:nm * 128], wring.t[:, i, 0:nm * 128], R=[wb_], W=[b_dg[d]])

    for j in range(8):
        build_dg(j, P_CDW + j * CK, CK)
    for kk_ in range(3):
        build_dg(8 + kk_, P_DNC + kk_ * 32, 32)
    for d_ in range(6):
        build_dg(11 + d_, P_FDWB + d_ * 24, min(24, 132 - d_ * 24))

    def dload(d):
        nm = CK if d < 8 else (32 if d < 11 else min(24, 132 - (d - 11) * 24))
        i, wb_ = wring.next()
        cx.dma(SP, wring.t[:, i, 0:nm * 128], dg_d[d][:, 0:nm * 128], R=[b_dg[d]], W=[wb_])
        return wring.t[:, i, :], wb_

    eps_c = cst[:, 0:1]
    one_c = cst[:, 1:2]

    def pcol(off, rows=128):
        return par[0:rows, off:off + 1]

    def wload(bidx):
        i, wb = wring.next()
        cx.dma(SP, wring.t[:, i, :], wbf_d[bidx], R=[b_wbf[bidx]], W=[wb])
        return wring.t[:, i, :].rearrange("p (k n) -> p k n", k=8), wb

    def norm_stats(hs, subs):
        out = []
        for (s, rows) in subs:
            si, sbuf_ = stat.next()
            st = stat.t[0:rows, si, :]
            hin = htok[0:rows, hs, s, :]
            E(ACT, lambda h: h.activation(out=junk[0:rows, :], in_=hin, func=AF.Square, accum_out=st[:, 0:1]),
              R=[b_htok[hs][s]], W=[b_junk, sbuf_])
            E(ACT, lambda h: h.activation(out=st[:, 1:2], in_=st[:, 0:1], func=AF.Sqrt, bias=eps_c[0:rows], scale=1.0 / D),
              R=[b_const], W=[sbuf_])
            E(DVE, lambda h: h.reciprocal(out=st[:, 2:3], in_=st[:, 1:2]), W=[sbuf_])
            out.append((st, sbuf_))
        return out

    def norm_apply(hs, subs, nt, woff, stats):
        for (s, rows), (st, sbuf_) in zip(subs, stats):
            hin = htok[0:rows, hs, s, :]
            xi, xb = xn.next()
            xt = xn.t[0:rows, xi, :]
            E(DVE, lambda h: h.tensor_scalar(out=xt, in0=hin, scalar1=st[:, 2:3], scalar2=None, op0=ALU.mult),
              R=[b_htok[hs][s], sbuf_], W=[xb])
            for c in range(8):
                pt, pb = ps_next()
                E(PE, lambda h: h.transpose(pt[:, 0:rows], xt[:, c * 128:(c + 1) * 128], ident_f[0:rows, 0:rows]),
                  R=[xb, b_const], W=[pb])
                tgt = uT[:, c, s * 128:s * 128 + rows]
                if c % 2 == 0:
                    E(ACT, lambda h: h.activation(out=tgt, in_=pt[:, 0:rows], func=AF.Copy, scale=pcol(woff + c)),
                      R=[pb, b_const], W=[b_uT[c]])
                else:
                    E(DVE, lambda h: h.tensor_scalar(out=tgt, in0=pt[:, 0:rows], scalar1=pcol(woff + c), scalar2=None,
                                                     op0=ALU.mult), R=[pb, b_const], W=[b_uT[c]])

    def norm_to_uT(hs, subs, nt, woff):
        norm_apply(hs, subs, nt, woff, norm_stats(hs, subs))

    def proj_chunk(wt, wb, g, nt, src, b_src, nk=8):
        pt, pb = ps_next()
        for kc in range(nk):
            E(PE, lambda h: h.matmul(pt[:, 0:nt], wt[:, kc, g * 128:(g + 1) * 128], src[:, kc, 0:nt],
                                     start=(kc == 0), stop=(kc == nk - 1)),
              R=[wb, b_src[kc]], W=[pb], inc=(kc == nk - 1))
        return pt[:, 0:nt], pb

    def l2_rstd(src_ap, b_src, nt, scale_in):
        qi, qb = sqb.next()
        sq = sqb.t[:, qi, 0:nt]
        E(ACT, lambda h: h.activation(out=sq, in_=src_ap, func=AF.Square), R=[b_src], W=[qb])
        pt, pb = ps_next()
        E(PE, lambda h: h.matmul(pt[:, 0:nt], ones_b[:, :], sq, start=True, stop=True), R=[qb, b_const], W=[pb])
        ri, rb = rr.next()
        r = rr.t[:, ri, 0:nt]
        E(ACT, lambda h: h.activation(out=r, in_=pt[:, 0:nt], func=AF.Sqrt, bias=eps_c, scale=scale_in),
          R=[pb, b_const], W=[rb])
        E(DVE, lambda h: h.reciprocal(out=r, in_=r), W=[rb])
        return r, rb

    def chk(name):
        if stop_after == name:
            raise _Stop()

    reg_subs = [(s_, 128) for s_ in range(NS)]

    def load_x(ti):
        hs_ = ti % 2
        x0_ = (ti - 1) * T
        for (s, rows) in reg_subs:
            cx.dma(SP, htok[:, hs_, s, :], x_d[x0_ + s * 128:x0_ + (s + 1) * 128, :], W=[b_htok[hs_][s]])

    def do_tile(ti, is_meta, has_next, prev_back):
        hs = ti % 2
        if is_meta:
            nt = NMETA
            subs = [(0, NMETA)]
            chunks = [(0, NMETA)]
            cx.dma(SP, htok[0:NMETA, hs, 0, :], meta_d[:, :], W=[b_htok[hs][0]])
            norm_to_uT(hs, subs, nt, P_MIXW)
        else:
            nt = T
            subs = reg_subs
            chunks = [(c * 64, 64) for c in range(NCK)]

        chk("ph1")
        for jp in range(4):
            wt, wb = wload(jp)
            for jj in range(2):
                j = 2 * jp + jj
                pg, pgb = proj_chunk(wt, wb, 2 * jj, nt, uT, b_uT)
                gi, gb = sig.next()
                sg_ap = sig.t[:, gi, 0:nt]
                E(ACT, lambda h: h.activation(out=sg_ap, in_=pg, func=AF.Tanh, scale=0.5), R=[pgb], W=[gb])
                pv, pvb = proj_chunk(wt, wb, 2 * jj + 1, nt, uT, b_uT)
                E(DVE, lambda h: h.scalar_tensor_tensor(out=cbuf[:, j, 30:30 + nt], in0=sg_ap, scalar=1.0, in1=pv,
                                                        op0=ALU.add, op1=ALU.mult), R=[pvb, gb], W=[b_c[j]])
        bg = []

        def bg_step(n=1):
            for _ in range(n):
                if bg:
                    bg.pop(0)()

        def conv31_item(j):
            dg, dgb = dload(j)
            pt, pb = ps_next()
            for t_ in range(CK):
                E(PE, lambda h: h.matmul(pt[:, 0:nt], dg[:, t_ * 128:(t_ + 1) * 128], cbuf[:, j, t_:t_ + nt],
                                         start=(t_ == 0), stop=(t_ == CK - 1)), R=[dgb, b_c[j]], W=[pb], inc=(t_ == CK - 1))
            E(ACT, lambda h: h.activation(out=ybuf[:, j, 0:nt], in_=pt[:, 0:nt], func=AF.Identity, bias=pcol(P_CDB + j)),
              R=[pb, b_const], W=[b_y[j]])
            E(POOL, lambda h: h.tensor_copy(out=ctmp[:, 0:30], in_=cbuf[:, j, nt:nt + 30]), R=[b_c[j]], W=[b_ctmp])
            E(POOL, lambda h: h.tensor_copy(out=cbuf[:, j, 0:30], in_=ctmp[:, 0:30]), R=[b_ctmp], W=[b_c[j]])

        if prev_back:
            bg.append(prev_back[0])
        for j in range(8):
            bg.append(lambda j=j: conv31_item(j))
        bg.extend(prev_back[1:])

        qkv_list = [(kind, hh) for kind in ("k", "q", "v") for hh in range(8)]
        kbase = {"q": 0, "k": 8, "v": 16}
        wblk0 = {"k": 4, "q": 6, "v": 8}
        dgidx = {"q": 8, "k": 9, "v": 10}
        qs = {}

        def q_st0(i):
            kind, hh = qkv_list[i]
            if hh == 0:
                qs["dg", kind] = dload(dgidx[kind])
            if hh % 4 == 0:
                qs["w"] = wload(wblk0[kind] + hh // 4)
            wt, wb = qs["w"]
            ch = kbase[kind] + hh
            pp, ppb = proj_chunk(wt, wb, hh % 4, nt, uT, b_uT)
            pi, prb = pre.next()
            phb = b_preh[pi]
            pr = pre.t[:, pi, :]
            E(POOL, lambda h: h.tensor_copy(out=pr[:, 0:3], in_=halq[:, ch, :]), R=[b_halq[ch]], W=[phb])
            if kind == "v":
                E(ACT, lambda h: h.activation(out=pr[:, 3:3 + nt], in_=pp, func=AF.Copy), R=[ppb], W=[prb])
            else:
                E(DVE, lambda h: h.tensor_copy(out=pr[:, 3:3 + nt], in_=pp), R=[ppb], W=[prb])
            E(POOL, lambda h: h.tensor_copy(out=halq[:, ch, :], in_=pr[:, nt:nt + 3]), R=[prb], W=[b_halq[ch]])
            qs[i] = (pr, prb, phb)

        def q_st1(i):
            kind, hh = qkv_list[i]
            pr, prb, phb = qs[i]
            dgq, dgqb = qs["dg", kind]
            pc4, cb = ps_next()
            ca = pc4[:, 0:nt]
            for t_ in range(4):
                mi_ = hh * 4 + t_
                E(PE, lambda h: h.matmul(ca, dgq[:, mi_ * 128:(mi_ + 1) * 128], pr[:, t_:t_ + nt], start=(t_ == 0),
                                         stop=(t_ == 3)), R=[dgqb, prb, phb], W=[cb], inc=(t_ == 3))
            if kind == "v":
                E(ACT, lambda h: h.activation(out=vT[:, hh, 0:nt], in_=ca, func=AF.Silu), R=[cb], W=[b_vT[hh]])
            else:
                dst, dstb = (kT, b_kT) if kind == "k" else (qT, b_qT)
                E(ACT, lambda h: h.activation(out=dst[:, hh, 0:nt], in_=ca, func=AF.Silu), R=[cb], W=[dstb[hh]])
                qi, qb = sqb.next()
                sq = sqb.t[:, qi, 0:nt]
                E(ACT, lambda h: h.activation(out=sq, in_=dst[:, hh, 0:nt], func=AF.Square), R=[dstb[hh]], W=[qb])
                qs[i] = (sq, qb)

        def q_st2(i):
            kind, hh = qkv_list[i]
            if kind == "v":
                return
            sq, qb = qs[i]
            idx = hh if kind == "k" else 8 + hh
            first = (kind == "k" and hh == 0)
            last = (kind == "q" and hh == 7)
            E(PE, lambda h: h.matmul(ps[0:16, 7, 0:nt], Esel[:, idx, :], sq, start=first, stop=last), R=[qb, b_const],
              W=[b_ps[7]])

        for i in range(24 + 3):
            if 0 <= i - 2 < 24:
                q_st1(i - 2)
            if 0 <= i - 3 < 24:
                q_st2(i - 3)
            if i < 24:
                q_st0(i)
        E(ACT, lambda h: h.activation(out=rsn[:, 0:nt], in_=ps[0:16, 7, 0:nt], func=AF.Sqrt, bias=eps_c[0:16], scale=1.0),
          R=[b_ps[7], b_const], W=[b_rsn])
        E(DVE, lambda h: h.reciprocal(out=rsn[:, 0:nt], in_=rsn[:, 0:nt]), W=[b_rsn])
        zg_state = {}

        def z_item(hh):
            if hh % 4 == 0:
                zg_state["w"] = wload(10 + hh // 4)
            wt, wb = zg_state["w"]
            pp, ppb = proj_chunk(wt, wb, hh % 4, nt, uT, b_uT)
            gi_, gb_ = sig.next()
            th = sig.t[:, gi_, 0:nt]
            E(ACT, lambda h: h.activation(out=th, in_=pp, func=AF.Tanh, scale=0.5), R=[ppb], W=[gb_])
            E(DVE, lambda h: h.scalar_tensor_tensor(out=sz[:, hh, 0:nt], in0=th, scalar=1.0, in1=pp, op0=ALU.add,
                                                    op1=ALU.mult), R=[ppb, gb_], W=[b_sz[hh]])

        def gate_item(jg):
            if jg % 4 == 0:
                zg_state["w"] = wload(12 + jg // 4)
            wt, wb = zg_state["w"]
            pp, ppb = proj_chunk(wt, wb, jg % 4, nt, uT, b_uT)
            E(ACT, lambda h: h.activation(out=gts[:, jg, 0:nt], in_=pp, func=AF.Tanh, bias=pcol(P_BGATE + jg),
                                          scale=0.5), R=[ppb, b_const], W=[b_gts[jg]])

        for hh in range(8):
            z_item(hh)
        for jg in range(16):
            gate_item(jg)
        pab, pabb = ps_next()
        pab3 = pab.rearrange("p (c n) -> p c n", n=16)
        for ck, (o, cl) in enumerate(chunks):
            for kc in range(8):
                E(PE, lambda h: h.matmul(pab3[0:cl, ck, :], uT[:, kc, o:o + cl], wab[:, kc, :], start=(kc == 0),
                                         stop=(kc == 7)), R=[b_uT[kc], b_wab], W=[pabb], inc=(kc == 7))
        nck = len(chunks)
        cl0 = chunks[0][1]
        dtb_b = tokp[0:cl0, TP_DTB:TP_DTB + 8].unsqueeze(1).to_broadcast([cl0, nck, 8])
        negA_b = tokp[0:cl0, TP_ALOG:TP_ALOG + 8].unsqueeze(1).to_broadcast([cl0, nck, 8])
        E(DVE, lambda h: h.tensor_tensor(out=abt[0:cl0, 0:nck, 0:8], in0=pab3[0:cl0, 0:nck, 0:8], in1=dtb_b, op=ALU.add),
          R=[pabb, b_const], W=[b_abt])
        E(ACT, lambda h: h.activation(out=abt[0:cl0, 0:nck, 0:8], in_=abt[0:cl0, 0:nck, 0:8], func=AF.Exp), W=[b_abt])
        E(ACT, lambda h: h.activation(out=abt[0:cl0, 0:nck, 0:8], in_=abt[0:cl0, 0:nck, 0:8], func=AF.Ln,
                                      bias=one_c[0:cl0]), R=[b_const], W=[b_abt])
        E(DVE, lambda h: h.tensor_tensor(out=gtok[0:cl0, 0:nck, :], in0=abt[0:cl0, 0:nck, 0:8], in1=negA_b, op=ALU.mult),
          R=[b_abt, b_const], W=[b_g])
        E(ACT, lambda h: h.activation(out=beta[0:cl0, 0:nck, :], in_=pab3[0:cl0, 0:nck, 8:16], func=AF.Tanh, scale=0.5),
          R=[pabb], W=[b_beta])
        E(DVE, lambda h: h.tensor_scalar(out=beta[0:cl0, 0:nck, :], in0=beta[0:cl0, 0:nck, :], scalar1=0.5, scalar2=0.5,
                                         op0=ALU.mult, op1=ALU.add), W=[b_beta])

        for idx in range(16):
            hh = idx % 8
            dst, dstb = (kT, b_kT) if idx < 8 else (qT, b_qT)
            pt, pb = ps_next()
            E(PE, lambda h: h.matmul(pt[:, 0:nt], identq[:, idx:idx + 1].to_broadcast([16, 128]), rsn[:, 0:nt], start=True,
                                     stop=True), R=[b_rsn, b_const], W=[pb])
            E(DVE, lambda h: h.tensor_tensor(out=dst[:, hh, 0:nt], in0=dst[:, hh, 0:nt], in1=pt[:, 0:nt], op=ALU.mult),
              R=[pb], W=[dstb[hh]])
        chk("ph2")
        chk("ph3")
        for ck, (o, cl) in enumerate(chunks):
            nlev = 5 if cl == 64 else 3
            W8 = 8 * cl
            gi, gub = GU.next()
            gu3 = GU.t[0:cl, gi, 0:W8].rearrange("p (h j) -> p h j", h=8)
            gsl = gtok[0:cl, ck, :]
            E(DVE, lambda h: h.tensor_tensor(out=gu3, in0=gsl.unsqueeze(2).to_broadcast([cl, 8, cl]),
                                             in1=Umat[0:cl, 0:cl].unsqueeze(1).to_broadcast([cl, 8, cl]), op=ALU.mult),
              R=[b_g, b_const], W=[gub])
            bps = ps[:, 6, 0:W8]
            bps3 = bps.rearrange("p (h j) -> p h j", h=8)
            E(PE, lambda h: h.matmul(bps, ones_f[0:cl, :], GU.t[0:cl, gi, 0:W8], start=True, stop=True),
              R=[gub, b_const], W=[b_ps[6]])
            pc, pcb = ps_next()
            E(PE, lambda h: h.matmul(pc[0:cl, 0:8], Umat[0:cl, 0:cl], gsl, start=True, stop=True), R=[b_g, b_const], W=[pcb])
            ci_, gcb = gcol.next()
            gc = gcol.t[0:cl, ci_, :]
            E(ACT, lambda h: h.activation(out=gc[:, 0:8], in_=pc[0:cl, 0:8], func=AF.Copy), R=[pcb], W=[gcb])
            E(ACT, lambda h: h.activation(out=gc[:, 8:16], in_=pc[0:cl, 0:8], func=AF.Exp), R=[pcb], W=[gcb])
            E(DVE, lambda h: h.tensor_tensor(out=gc[:, 16:24], in0=gc[:, 8:16], in1=beta[0:cl, ck, :], op=ALU.mult),
              R=[b_beta], W=[gcb])
            E(DVE, lambda h: h.tensor_tensor(out=gc[:, 24:32], in0=bps3[0:cl, :, cl - 1], in1=gc[:, 0:8], op=ALU.subtract),
              R=[b_ps[6]], W=[gcb])
            E(ACT, lambda h: h.activation(out=gc[:, 24:32], in_=gc[:, 24:32], func=AF.Exp), W=[gcb])
            d1i, d1b = D1.next()
            d2i, d2b = D2.next()
            d1 = D1.t[0:cl, d1i, 0:W8].rearrange("p (h j) -> p h j", h=8)
            d2 = D2.t[0:cl, d2i, 0:W8].rearrange("p (h j) -> p h j", h=8)
            gcb3 = gc[:, 0:8].unsqueeze(2).to_broadcast([cl, 8, cl])
            E(DVE, lambda h: h.tensor_tensor(out=d1, in0=gcb3, in1=bps3[0:cl], op=ALU.subtract), R=[gcb, b_ps[6]], W=[d1b])
            E(DVE, lambda h: h.tensor_tensor(out=d2, in0=bps3[0:cl], in1=gcb3, op=ALU.subtract), R=[gcb, b_ps[6]], W=[d2b])
            E(POOL, lambda h: h.affine_select(out=d1, in_=d1, pattern=[[0, 8], [-1, cl]], compare_op=ALU.is_gt, fill=neg_reg,
                                              base=0, channel_multiplier=1), W=[d1b])
            E(POOL, lambda h: h.affine_select(out=d2, in_=d2, pattern=[[0, 8], [1, cl]], compare_op=ALU.is_ge, fill=neg_reg,
                                              base=0, channel_multiplier=-1), W=[d2b])
            E(ACT, lambda h: h.activation(out=d1, in_=d1, func=AF.Exp), W=[d1b])
            E(ACT, lambda h: h.activation(out=d2, in_=d2, func=AF.Exp), W=[d2b])
            E(DVE, lambda h: h.tensor_tensor(out=d1, in0=d1, in1=beta[0:cl, ck, :].unsqueeze(2).to_broadcast([cl, 8, cl]),
                                              op=ALU.mult), R=[b_beta], W=[d1b])
            chk("d1")
            pkk, pkkb = ps_next()
            pqk, pqkb = ps_next()
            for hh in range(8):
                E(PE, lambda h: h.matmul(pkk[0:cl, hh * cl:(hh + 1) * cl], kT[:, hh, o:o + cl], kT[:, hh, o:o + cl],
                                         start=True, stop=True), R=[b_kT[hh]], W=[pkkb], inc=(hh == 7))
            for hh in range(8):
                E(PE, lambda h: h.matmul(pqk[0:cl, hh * cl:(hh + 1) * cl], kT[:, hh, o:o + cl], qT[:, hh, o:o + cl],
                                         start=True, stop=True), R=[b_kT[hh], b_qT[hh]], W=[pqkb], inc=(hh == 7))
            a_i, a_b = Abf.next()
            A0 = Abf.t[0:cl, a_i, 0:W8]
            E(DVE, lambda h: h.tensor_tensor(out=A0, in0=pkk[0:cl, 0:W8], in1=D1.t[0:cl, d1i, 0:W8], op=ALU.mult),
              R=[pkkb, d1b], W=[a_b])
            q_i, q_b = AqkT.next()
            AQ = AqkT.t[0:cl, q_i, 0:W8]
            E(DVE, lambda h: h.tensor_tensor(out=AQ, in0=pqk[0:cl, 0:W8], in1=D2.t[0:cl, d2i, 0:W8], op=ALU.mult),
              R=[pqkb, d2b], W=[q_b])
            chk("d2")
            bg_step()
            pmt, pmtb = ps_next()
            for hh in range(8):
                E(PE, lambda h: h.matmul(pmt[0:cl, hh * cl:(hh + 1) * cl], A0[:, hh * cl:(hh + 1) * cl], ident_b[0:cl, 0:cl],
                                         start=True, stop=True), R=[a_b, b_const], W=[pmtb], inc=(hh == 7))
            m_i, m_b = Mbf.next()
            M0 = Mbf.t[0:cl, m_i, 0:W8]
            E(ACT, lambda h: h.activation(out=M0, in_=pmt[0:cl, 0:W8], func=AF.Copy), R=[pmtb], W=[m_b])
            chk("d3")
            bg_step()
            pf_i, pf_b = Pf.next()
            PF = Pf.t[0:cl, pf_i, 0:W8]
            E(DVE, lambda h: h.tensor_tensor(out=PF.rearrange("p (h j) -> p h j", h=8),
                                             in0=ident_f[0:cl, 0:cl].unsqueeze(1).to_broadcast([cl, 8, cl]),
                                             in1=M0.rearrange("p (h j) -> p h j", h=8), op=ALU.subtract),
              R=[m_b, b_const], W=[pf_b])
            p_i, p_b = Pbf.next()
            Pk = Pbf.t[0:cl, p_i, 0:W8]
            E(ACT, lambda h: h.activation(out=Pk, in_=PF, func=AF.Copy), R=[pf_b], W=[p_b])
            chk("e0")
            bg_step()
            Ak, Ak_b, Mk, Mk_b = A0, a_b, M0, m_b
            for lev in range(1, nlev + 1):
                last = (lev == nlev)
                if lev == 2:
                    chk("e2")
                pa, pab_ = ps_next()
                for hh in range(8):
                    sl = slice(hh * cl, (hh + 1) * cl)
                    E(PE, lambda h: h.matmul(pa[0:cl, sl], Mk[:, sl], Ak[:, sl], start=True, stop=True),
                      R=[Mk_b, Ak_b], W=[pab_], inc=(hh == 7))
                if not last:
                    pm2, pm2b = ps_next()
                    for hh in range(8):
                        sl = slice(hh * cl, (hh + 1) * cl)
                        E(PE, lambda h: h.matmul(pm2[0:cl, sl], Ak[:, sl], Mk[:, sl], start=True, stop=True),
                          R=[Mk_b, Ak_b], W=[pm2b], inc=(hh == 7))
                chk("e1")
                na_i, na_b = Abf.next()
                An = Abf.t[0:cl, na_i, 0:W8]
                E(ACT, lambda h: h.activation(out=An, in_=pa[0:cl, 0:W8], func=AF.Copy), R=[pab_], W=[na_b])
                if not last:
                    nm_i, nm_b = Mbf.next()
                    Mn = Mbf.t[0:cl, nm_i, 0:W8]
                    E(DVE, lambda h: h.tensor_copy(out=Mn, in_=pm2[0:cl, 0:W8]), R=[pm2b], W=[nm_b])
                pp_, ppb_ = ps_next()
                for hh in range(8):
                    sl = slice(hh * cl, (hh + 1) * cl)
                    E(PE, lambda h: h.matmul(pp_[0:cl, sl], An[:, sl], Pk[:, sl], start=True, stop=True),
                      R=[na_b, p_b], W=[ppb_], inc=(hh == 7))
                if not last:
                    np_i, np_b = Pbf.next()
                    Pn = Pbf.t[0:cl, np_i, 0:W8]
                    E(DVE, lambda h: h.tensor_tensor(out=Pn, in0=PF, in1=pp_[0:cl, 0:W8], op=ALU.add), R=[ppb_, pf_b],
                      W=[np_b])
                E(DVE, lambda h: h.tensor_tensor(out=PF, in0=PF, in1=pp_[0:cl, 0:W8], op=ALU.add), R=[ppb_], W=[pf_b])
                if not last:
                    Pk, p_b = Pn, np_b
                    Mk, Mk_b = Mn, nm_b
                Ak, Ak_b = An, na_b
                bg_step(2)
            t_i, t_b = TTb.next()
            TT = TTb.t[0:cl, t_i, 0:W8]
            E(ACT, lambda h: h.activation(out=TT, in_=PF, func=AF.Copy), R=[pf_b], W=[t_b])
            chk("d4")
            bg_step()
            kw_i, kw_b = kw.next()
            kd_i, kd_b = kd.next()
            vb_i, vb_b = vb.next()

            def sc(col0, g0):
                return gc[:, col0 + g0:col0 + g0 + 4].unsqueeze(2).to_broadcast([cl, 4, 128])

            for grp in range(2):
                pkt, pktb = ps_next()
                pvt, pvtb = ps_next()
                for g in range(4):
                    hh = grp * 4 + g
                    E(PE, lambda h: h.matmul(pkt[0:cl, g * 128:(g + 1) * 128], kT[:, hh, o:o + cl], ident_b[:, :],
                                             start=True, stop=True), R=[b_kT[hh], b_const], W=[pktb], inc=(g == 3))
                for g in range(4):
                    hh = grp * 4 + g
                    E(PE, lambda h: h.matmul(pvt[0:cl, g * 128:(g + 1) * 128], vT[:, hh, o:o + cl], ident_b[:, :],
                                             start=True, stop=True), R=[b_vT[hh], b_const], W=[pvtb], inc=(g == 3))
                k3 = pkt[0:cl, :].rearrange("p (h d) -> p h d", h=4)
                v3 = pvt[0:cl, :].rearrange("p (h d) -> p h d", h=4)
                csl = slice(grp * 512, (grp + 1) * 512)
                E(DVE, lambda h: h.tensor_tensor(out=kw.t[0:cl, kw_i, csl].rearrange("p (h d) -> p h d", h=4), in0=k3,
                                                 in1=sc(16, grp * 4), op=ALU.mult), R=[pktb, gcb], W=[kw_b])
                E(DVE, lambda h: h.tensor_tensor(out=kd.t[0:cl, kd_i, csl].rearrange("p (h d) -> p h d", h=4), in0=k3,
                                                 in1=sc(24, grp * 4), op=ALU.mult), R=[pktb, gcb], W=[kd_b])
                E(DVE, lambda h: h.tensor_tensor(out=vb.t[0:cl, vb_i, csl].rearrange("p (h d) -> p h d", h=4), in0=v3,
                                                 in1=beta[0:cl, ck, grp * 4:grp * 4 + 4].unsqueeze(2).to_broadcast([cl, 4, 128]),
                                                 op=ALU.mult), R=[pvtb, b_beta], W=[vb_b])
            chk("d5")
            bg_step()
            pw, pwb = ps_next()
            for hh in range(8):
                E(PE, lambda h: h.matmul(pw[:, hh * cl:(hh + 1) * cl], kw.t[0:cl, kw_i, hh * 128:(hh + 1) * 128],
                                         TT[:, hh * cl:(hh + 1) * cl], start=True, stop=True), R=[kw_b, t_b], W=[pwb],
                  inc=(hh == 7))
            w_i, w_b = wTn.next()
            WT = wTn.t[:, w_i, 0:W8]
            E(ACT, lambda h: h.activation(out=WT, in_=pw[:, 0:W8], func=AF.Copy, scale=-1.0), R=[pwb], W=[w_b])
            e_i, e_b = Eq.next()
            EQ = Eq.t[:, e_i, 0:W8]
            E(ACT, lambda h: h.activation(out=EQ, in_=bps, func=AF.Exp), R=[b_ps[6]], W=[e_b])
            EQ3 = EQ.rearrange("p (h j) -> p h j", h=8)
            E(DVE, lambda h: h.tensor_tensor(out=qdT[:, :, o:o + cl], in0=qT[:, :, o:o + cl], in1=EQ3, op=ALU.mult),
              R=b_qT + [e_b], W=[b_qdT[ck]])
            chk("d6")
            bg_step()
            vn_i, vn_b = vn.next()
            VN = vn.t[0:cl, vn_i, :]
            for grp in range(2):
                pv_, pvb_ = ps_next()
                for g in range(4):
                    hh = grp * 4 + g
                    E(PE, lambda h: h.matmul(pv_[0:cl, g * 128:(g + 1) * 128], TT[:, hh * cl:(hh + 1) * cl],
                                             vb.t[0:cl, vb_i, hh * 128:(hh + 1) * 128], start=True, stop=False),
                      R=[t_b, vb_b], W=[pvb_], inc=False)
                    E(PE, lambda h: h.matmul(pv_[0:cl, g * 128:(g + 1) * 128], WT[:, hh * cl:(hh + 1) * cl],
                                             Sbf[:, hh, :], start=False, stop=True), R=[w_b, b_Sbf[hh]], W=[pvb_],
                      inc=(g == 3))
                if grp == 0:
                    E(ACT, lambda h: h.activation(out=VN[:, 0:512], in_=pv_[0:cl, :], func=AF.Copy), R=[pvb_], W=[vn_b])
                else:
                    E(DVE, lambda h: h.tensor_copy(out=VN[:, 512:1024], in_=pv_[0:cl, :]), R=[pvb_], W=[vn_b])
            po, pob = ps_next()
            for hh in range(8):
                E(PE, lambda h: h.matmul(po[:, hh * cl:(hh + 1) * cl], Sbf[:, hh, :], qdT[:, hh, o:o + cl], start=True,
                                         stop=False), R=[b_Sbf[hh], b_qdT[ck]], W=[pob], inc=False)
                E(PE, lambda h: h.matmul(po[:, hh * cl:(hh + 1) * cl], VN[:, hh * 128:(hh + 1) * 128],
                                         AQ[:, hh * cl:(hh + 1) * cl], start=False, stop=True), R=[vn_b, q_b], W=[pob],
                  inc=(hh == 7))
            E(ACT, lambda h: h.activation(out=oT[:, :, o:o + cl], in_=po[:, 0:W8].rearrange("p (h j) -> p h j", h=8),
                                          func=AF.Copy), R=[pob], W=[b_oT[ck]])
            bg_step()
            for grp in range(2):
                pd, pdb = ps_next()
                for g in range(4):
                    hh = grp * 4 + g
                    E(PE, lambda h: h.matmul(pd[:, g * 128:(g + 1) * 128], kd.t[0:cl, kd_i, hh * 128:(hh + 1) * 128],
                                             VN[:, hh * 128:(hh + 1) * 128], start=True, stop=True), R=[kd_b, vn_b],
                      W=[pdb], inc=(g == 3))
                for g in range(4):
                    hh = grp * 4 + g
                    E(DVE, lambda h: h.scalar_tensor_tensor(out=S[:, hh, :], in0=S[:, hh, :],
                                                            scalar=EQ[:, hh * cl + cl - 1:hh * cl + cl],
                                                            in1=pd[:, g * 128:(g + 1) * 128], op0=ALU.mult, op1=ALU.add),
                      R=[e_b, pdb], W=[b_S[hh]])
                g0 = grp * 4
                E(ACT, lambda h: h.activation(out=Sbf[:, g0:g0 + 4, :], in_=S[:, g0:g0 + 4, :], func=AF.Copy),
                  R=b_S[g0:g0 + 4], W=b_Sbf[g0:g0 + 4])
                bg_step()
        bg_step(len(bg))
        nstats = None
        if has_next:
            load_x(ti + 1)
            nstats = norm_stats((ti + 1) % 2, reg_subs)
        for hh in range(8):
            qi, qb = sqb.next()
            sq = sqb.t[:, qi, 0:nt]
            E(ACT, lambda h: h.activation(out=sq, in_=oT[:, hh, 0:nt], func=AF.Square), R=b_oT[0:nck], W=[qb])
            E(PE, lambda h: h.matmul(ps[0:16, 7, 0:nt], Esel[:, hh, :], sq, start=(hh == 0), stop=(hh == 7)),
              R=[qb, b_const], W=[b_ps[7]])
        E(ACT, lambda h: h.activation(out=rsn[:, 0:nt], in_=ps[0:16, 7, 0:nt], func=AF.Sqrt, bias=eps_c[0:16],
                                      scale=1.0 / 128.0), R=[b_ps[7], b_const], W=[b_rsn])
        E(DVE, lambda h: h.reciprocal(out=rsn[:, 0:nt], in_=rsn[:, 0:nt]), W=[b_rsn])
        for hh in range(8):
            pt, pb = ps_next()
            E(PE, lambda h: h.matmul(pt[:, 0:nt], ident_f[0:16, hh:hh + 1].to_broadcast([16, 128]), rsn[:, 0:nt], start=True,
                                     stop=True), R=[b_rsn, b_const], W=[pb])
            ri, rb = rr.next()
            r = rr.t[:, ri, 0:nt]
            E(DVE, lambda h: h.tensor_tensor(out=r, in0=oT[:, hh, 0:nt], in1=pt[:, 0:nt], op=ALU.mult), R=b_oT[0:nck] + [pb],
              W=[rb])
            E(DVE, lambda h: h.scalar_tensor_tensor(out=od[:, hh, 0:nt], in0=r, scalar=pcol(P_DNW), in1=sz[:, hh, 0:nt],
                                                    op0=ALU.mult, op1=ALU.mult), R=[rb, b_sz[hh], b_const], W=[b_od[hh]])

        pm, pmb = ps_next()
        pq, pqb = ps_next()
        for j in range(8):
            yi, yb_ = ybf.next()
            qi, qb_ = ysq.next()
            E(ACT, lambda h: h.activation(out=ybf.t[:, yi, 0:nt], in_=ybuf[:, j, 0:nt], func=AF.Copy), R=[b_y[j]], W=[yb_])
            E(ACT, lambda h: h.activation(out=ysq.t[:, qi, 0:nt], in_=ybuf[:, j, 0:nt], func=AF.Square), R=[b_y[j]], W=[qb_])
            E(PE, lambda h: h.matmul(pm[:, 0:nt], ones_b[:, :], ybf.t[:, yi, 0:nt], start=(j == 0), stop=(j == 7)),
              R=[yb_, b_const], W=[pmb])
            E(PE, lambda h: h.matmul(pq[:, 0:nt], ones_b[:, :], ysq.t[:, qi, 0:nt], start=(j == 0), stop=(j == 7)),
              R=[qb_, b_const], W=[pqb])
        m_ = lnt[:, 0, 0:nt]
        msq = lnt[:, 1, 0:nt]
        var = lnt[:, 2, 0:nt]
        rstd = lnt[:, 3, 0:nt]
        mr = lnt[:, 4, 0:nt]
        E(DVE, lambda h: h.tensor_scalar(out=m_, in0=pm[:, 0:nt], scalar1=1.0 / D, scalar2=None, op0=ALU.mult),
          R=[pmb], W=[b_lnt[0]])
        E(DVE, lambda h: h.tensor_tensor(out=msq, in0=m_, in1=m_, op=ALU.mult), R=[b_lnt[0]], W=[b_lnt[1]])
        E(DVE, lambda h: h.scalar_tensor_tensor(out=var, in0=pq[:, 0:nt], scalar=1.0 / D, in1=msq, op0=ALU.mult,
                                                op1=ALU.subtract), R=[pqb, b_lnt[1]], W=[b_lnt[2]])
        E(ACT, lambda h: h.activation(out=rstd, in_=var, func=AF.Sqrt, bias=eps_c, scale=1.0), R=[b_lnt[2], b_const],
          W=[b_lnt[3]])
        E(DVE, lambda h: h.reciprocal(out=rstd, in_=rstd), W=[b_lnt[3]])
        E(DVE, lambda h: h.tensor_tensor(out=mr, in0=m_, in1=rstd, op=ALU.mult), R=[b_lnt[0], b_lnt[3]], W=[b_lnt[4]])
        for j in range(8):
            ai, ab = accA.next()
            aa = accA.t[:, ai, 0:nt]
            E(DVE, lambda h: h.tensor_tensor(out=aa, in0=ybuf[:, j, 0:nt], in1=rstd, op=ALU.mult), R=[b_y[j], b_lnt[3]],
              W=[ab])
            E(DVE, lambda h: h.tensor_tensor(out=aa, in0=aa, in1=mr, op=ALU.subtract), R=[b_lnt[4]], W=[ab])
            E(ACT, lambda h: h.activation(out=cact[:, j, 0:nt], in_=aa, func=AF.Silu, bias=pcol(P_LNB + j),
                                          scale=pcol(P_LNW + j)), R=[ab, b_const], W=[b_cact[j]])

        chk("ph4")
        for j in range(8):
            if j % 4 == 0:
                wco_ = wload(16 + j // 4)
                wdn_ = wload(18 + j // 4)
            wt, wb = wco_
            pa_, pab2 = proj_chunk(wt, wb, j % 4, nt, cact, b_cact)
            wt, wb = wdn_
            pb_, pbb2 = proj_chunk(wt, wb, j % 4, nt, od, b_od)
            mi, mb = mtmp.next()
            mt_ = mtmp.t[:, mi, 0:nt]
            E(DVE, lambda h: h.scalar_tensor_tensor(out=mt_, in0=gts[:, j, 0:nt], scalar=1.0, in1=pa_, op0=ALU.add,
                                                    op1=ALU.mult), R=[pab2, b_gts[j]], W=[mb])
            mi2, mb2 = mtmp.next()
            mt2 = mtmp.t[:, mi2, 0:nt]
            E(DVE, lambda h: h.scalar_tensor_tensor(out=mt2, in0=gts[:, 8 + j, 0:nt], scalar=1.0, in1=pb_, op0=ALU.add,
                                                    op1=ALU.mult), R=[pbb2, b_gts[8 + j]], W=[mb2])
            E(POOL, lambda h: h.tensor_tensor(out=mT[:, j, 0:nt], in0=mt_, in1=mt2, op=ALU.add), R=[mb, mb2], W=[b_mT[j]])
        for half in range(2):
            wt, wb = wload(20 + half)
            for (s, rows) in subs:
                pt, pb = ps_next()
                for kc in range(8):
                    E(PE, lambda h: h.matmul(pt[0:rows, :], mT[:, kc, s * 128:s * 128 + rows], wt[:, kc, :],
                                             start=(kc == 0), stop=(kc == 7)), R=[b_mT[kc], wb], W=[pb], inc=(kc == 7))
                hsl = htok[0:rows, hs, s, half * 512:(half + 1) * 512]
                E(DVE, lambda h: h.scalar_tensor_tensor(out=hsl, in0=pt[0:rows, :], scalar=0.5, in1=hsl, op0=ALU.mult,
                                                        op1=ALU.add), R=[pb], W=[b_htok[hs][s]])

        chk("ph5")
        if has_next:
            norm_apply((ti + 1) % 2, reg_subs, T, P_MIXW, nstats)

    def make_back(ti, is_meta):
        hs = ti % 2
        if is_meta:
            nt = NMETA
            subs = [(0, NMETA)]
        else:
            nt = T
            subs = reg_subs
        items = []
        st_ = {}

        def it_stats():
            st_["ns"] = norm_stats(hs, subs)

        def it_apply():
            norm_apply(hs, subs, nt, P_FFW, st_["ns"])

        ffn_list = [(jp, g) for jp in range(11) for g in range(4)]
        fs = {}

        def f_st0(i):
            jp, g = ffn_list[i]
            if g == 0:
                if jp % 2 == 0:
                    fs["dg", jp // 2] = dload(11 + jp // 2)
                fs["w"] = wload(22 + jp)
            wt, wb = fs["w"]
            j = 2 * jp + g // 2
            ch = j if g % 2 == 0 else NFF + j
            pp, ppb = proj_chunk(wt, wb, g, nt, uT, b_uT)
            pi, prb = preu.next()
            phb = b_preuh[pi]
            pr = preu.t[:, pi, :]
            E(POOL, lambda h: h.tensor_copy(out=pr[:, 0:2], in_=halu[:, ch, :]), R=[b_halu[ch]], W=[phb])
            if g % 2 == 0:
                E(ACT, lambda h: h.activation(out=pr[:, 2:2 + nt], in_=pp, func=AF.Copy), R=[ppb], W=[prb])
            else:
                E(DVE, lambda h: h.tensor_copy(out=pr[:, 2:2 + nt], in_=pp), R=[ppb], W=[prb])
            E(POOL, lambda h: h.tensor_copy(out=halu[:, ch, :], in_=pr[:, nt:nt + 2]), R=[prb], W=[b_halu[ch]])
            fs[i] = (pr, prb, phb)

        def f_st1(i):
            jp, g = ffn_list[i]
            j = 2 * jp + g // 2
            ch = j if g % 2 == 0 else NFF + j
            pr, prb, phb = fs[i]
            dgu, dgub = fs["dg", jp // 2]
            pc3, pc3b = ps_next()
            for t_ in range(3):
                mi_ = ((jp % 2) * 4 + g) * 3 + t_
                E(PE, lambda h: h.matmul(pc3[:, 0:nt], dgu[:, mi_ * 128:(mi_ + 1) * 128], pr[:, t_:t_ + nt],
                                         start=(t_ == 0), stop=(t_ == 2)), R=[dgub, prb, phb], W=[pc3b], inc=(t_ == 2))
            if g % 2 == 0:
                si_, sgb = sg.next()
                sga = sg.t[:, si_, 0:nt]
                E(ACT, lambda h: h.activation(out=sga, in_=pc3[:, 0:nt], func=AF.Silu, bias=pcol(P_FDB + ch)),
                  R=[pc3b, b_const], W=[sgb])
                fs["sg"] = (sga, sgb)
            else:
                sga, sgb = fs["sg"]
                E(DVE, lambda h: h.scalar_tensor_tensor(out=actb[:, j, 0:nt], in0=pc3[:, 0:nt], scalar=pcol(P_FDB + ch),
                                                        in1=sga, op0=ALU.add, op1=ALU.mult), R=[pc3b, sgb, b_const],
                  W=[b_act[j]])

        def it_up(i):
            if 0 <= i - 2 < 44:
                f_st1(i - 2)
            if i < 44:
                f_st0(i)

        def it_down(half):
            pts = [ps_next() for _ in subs]
            for grp in range(3):
                wt, wb = wload(33 + half * 3 + grp)
                nk = 8 if grp < 2 else NFF - 16
                for si_, (s, rows) in enumerate(subs):
                    pt, pb = pts[si_]
                    for kk in range(nk):
                        kc = grp * 8 + kk
                        last = (kc == NFF - 1)
                        E(PE, lambda h: h.matmul(pt[0:rows, :], actb[:, kc, s * 128:s * 128 + rows], wt[:, kk, :],
                                                 start=(kc == 0), stop=last), R=[b_act[kc], wb], W=[pb],
                          inc=(kk == nk - 1))
            for si_, (s, rows) in enumerate(subs):
                pt, pb = pts[si_]
                hsl = htok[0:rows, hs, s, half * 512:(half + 1) * 512]
                E(DVE, lambda h: h.tensor_tensor(out=hsl, in0=hsl, in1=pt[0:rows, :], op=ALU.add), R=[pb],
                  W=[b_htok[hs][s]])

        def it_final(si_):
            (s, rows) = subs[si_]
            x0 = (ti - 1) * T
            si, sbuf_ = stat.next()
            st = stat.t[0:rows, si, :]
            hin = htok[0:rows, hs, s, :]
            E(ACT, lambda h: h.activation(out=junk[0:rows, :], in_=hin, func=AF.Square, accum_out=st[:, 0:1]),
              R=[b_htok[hs][s]], W=[b_junk, sbuf_])
            E(ACT, lambda h: h.activation(out=st[:, 1:2], in_=st[:, 0:1], func=AF.Sqrt, bias=eps_c[0:rows],
                                          scale=1.0 / D), R=[b_const], W=[sbuf_])
            E(DVE, lambda h: h.reciprocal(out=st[:, 2:3], in_=st[:, 1:2]), W=[sbuf_])
            E(DVE, lambda h: h.scalar_tensor_tensor(out=hin, in0=hin, scalar=st[:, 2:3],
                                                    in1=tokp[0:rows, TP_NFW:TP_NFW + D], op0=ALU.mult, op1=ALU.mult),
              R=[sbuf_, b_const], W=[b_htok[hs][s]])
            cx.dma(ACT, out_d[x0 + s * 128:x0 + s * 128 + rows, :], hin, R=[b_htok[hs][s]])

        items.append(it_stats)
        items.append(it_apply)
        for i in range(44 + 2):
            items.append(lambda i=i: it_up(i))
        for half in range(2):
            items.append(lambda half=half: it_down(half))
        if not is_meta:
            for si_ in range(len(subs)):
                items.append(lambda si_=si_: it_final(si_))
        return items

    try:
        chk("setup")
        back_items = []
        do_tile(0, True, ntile >= 1, back_items)
        back_items = make_back(0, True)
        chk("meta")
        for ti in range(1, ntile + 1):
            do_tile(ti, False, ti < ntile, back_items)
            back_items = make_back(ti, False)
        for it in back_items:
            it()
    except _Stop:
        pass
    cx.finish()
    es.close()
    return nc, cx


def _pack_weights(w_in, w_conf_out, w_dn_out, w_out, w_up, w_down):
    blocks = np.zeros((NBLK, 128, 8, 512), np.float32)

    def colblock(W, c0s):
        W3 = W.reshape(8, 128, W.shape[1])
        out = np.empty((128, 8, 512), np.float32)
        for g, c0 in enumerate(c0s):
            out[:, :, g * 128:(g + 1) * 128] = W3[:, :, c0:c0 + 128].transpose(1, 0, 2)
        return out

    b = 0
    for jp in range(4):
        j0, j1 = 2 * jp, 2 * jp + 1
        blocks[b] = colblock(w_in, [1024 + j0 * 128, j0 * 128, 1024 + j1 * 128, j1 * 128]); b += 1
    for base in (3072, 2048, 4096, 5120):
        for hb in range(2):
            blocks[b] = colblock(w_in, [base + (hb * 4 + g) * 128 for g in range(4)]); b += 1
    for base in (6160, 7184):
        for hb in range(2):
            blocks[b] = colblock(w_in, [base + (hb * 4 + g) * 128 for g in range(4)]); b += 1
    for W in (w_conf_out, w_dn_out):
        for hb in range(2):
            blocks[b] = colblock(W, [(hb * 4 + g) * 128 for g in range(4)]); b += 1
    for half in range(2):
        blocks[b] = w_out.reshape(8, 128, 1024)[:, :, half * 512:(half + 1) * 512].transpose(1, 0, 2); b += 1
    for jp in range(11):
        j0, j1 = 2 * jp, 2 * jp + 1
        blocks[b] = colblock(w_up, [j0 * 128, DFF + j0 * 128, j1 * 128, DFF + j1 * 128]); b += 1
    Wd = w_down.reshape(NFF, 128, 1024)
    for half in range(2):
        for grp in range(3):
            nk = 8 if grp < 2 else NFF - 16
            blocks[b][:, 0:nk, :] = Wd[grp * 8:grp * 8 + nk, :, half * 512:(half + 1) * 512].transpose(1, 0, 2); b += 1
    assert b == NBLK
    return blocks.reshape(NBLK, 128, 8 * 512)


def _pack_params(inp):
    par = np.zeros((128, NPAR), np.float32)

    def cols(v):
        return v.reshape(-1, 128).T

    par[:, P_MIXW:P_MIXW + 8] = cols(inp["norm_mix_w"][0])
    par[:, P_BGATE:P_BGATE + 16] = cols(inp["b_gate"][0])
    cdw = inp["conf_dw_w"][0]
    par[:, P_CDW:P_CDW + 8 * CK] = cdw.reshape(CK, 8, 128).transpose(2, 1, 0).reshape(128, 8 * CK)
    par[:, P_CDB:P_CDB + 8] = cols(inp["conf_dw_b"][0])
    par[:, P_LNW:P_LNW + 8] = cols(inp["conf_ln_w"][0])
    par[:, P_LNB:P_LNB + 8] = cols(inp["conf_ln_b"][0])
    dnc = inp["dn_conv_w"][0]
    par[:, P_DNC:P_DNC + 96] = dnc.reshape(4, 24, 128).transpose(2, 1, 0).reshape(128, 96)
    par[:, P_DNW] = inp["dn_norm_w"][0]
    par[:, P_FFW:P_FFW + 8] = cols(inp["norm_ffn_w"][0])
    fdw = inp["ffn_dw_w"][0]
    par[:, P_FDW:P_FDW + 132] = fdw.reshape(3, 44, 128).transpose(2, 1, 0).reshape(128, 132)
    par[:, P_FDB:P_FDB + 44] = cols(inp["ffn_dw_b"][0])
    fcols = par[:, P_FDW:P_FDW + 132].copy()
    for jp in range(11):
        for g in range(4):
            j = 2 * jp + g // 2
            ch = j if g % 2 == 0 else NFF + j
            par[:, P_FDWB + (jp * 4 + g) * 3:P_FDWB + (jp * 4 + g) * 3 + 3] = fcols[:, ch * 3:ch * 3 + 3]
    tokp = np.zeros((128, NTOKP), np.float32)
    tokp[:, TP_DTB:TP_DTB + 8] = inp["dn_dt_bias"][0][None, :]
    tokp[:, TP_ALOG:TP_ALOG + 8] = inp["dn_A_log"][0][None, :]
    tokp[:, TP_NFW:] = inp["norm_final_w"][None, :]
    return par, tokp


_CACHE = {}


def kernel(**inputs):
    inp = {k: np.asarray(v, dtype=np.float32) for k, v in inputs.items()}
    x = inp["x"]
    B, n_x, _ = x.shape
    wpack = _pack_weights(inp["w_in"][0], inp["w_conf_out"][0], inp["w_dn_out"][0], inp["w_out"][0], inp["w_up"][0],
                          inp["w_down"][0])
    wab = np.ascontiguousarray(inp["w_in"][0][:, 6144:6160].reshape(8, 128, 16).transpose(1, 0, 2)).reshape(128, 128)
    par, tokp = _pack_params(inp)
    if n_x not in _CACHE:
        _CACHE[n_x] = build(n_x)[0]
    nc = _CACHE[n_x]
    in_maps = []
    for b in range(B):
        in_maps.append({"x": np.ascontiguousarray(x[b]), "meta": inp["meta_tokens"], "wpack": wpack, "wab": wab,
                        "params": par, "tokpar": tokp})
    res = run_bass_kernel_spmd(nc, in_maps, core_ids=list(range(B)))
    return np.stack([np.asarray(r["out"], dtype=np.float32) for r in res.results], axis=0)
```
